# Optimizing a Trainium2 kernel written in Bass

```python
import math
import jax
import jax.numpy as jnp
from jax import lax
import numpy as np


D_MODEL = 2048
BATCH = 8
SEQ = 2048
DEPTH = 2
DEC_BATCH = 2
DEC_SEQ = 8192
PAST_LEN = 128

BR_W = D_MODEL // 2
N_BRANCH = 5
HEAD_DIM = 128
ATT_HEADS = BR_W // HEAD_DIM
ATT_KV = 2
Q_BLOCK = 128
ROPE_THETA = 10000.0
GRID_W = 64
SSM_P = 16
SSM_G = BR_W // SSM_P
SSM_N = 64
SSM_DT_MIN = 1e-3
SSM_DT_MAX = 1e-1
SGU_CHUNK = 128
SGU_GROUPS = 8
SGU_CH = BR_W // SGU_GROUPS
DN_HEADS = BR_W // HEAD_DIM
DN_CHUNK = 64
CONV_W = 3
XA_HEADS = 4
XA_HD = BR_W // XA_HEADS
MEM_TOKENS = 256
EPS = 1e-6

IN_SPLITS = (
    BR_W, ATT_KV * HEAD_DIM, ATT_KV * HEAD_DIM, BR_W,
    BR_W, BR_W,
    BR_W, BR_W, BR_W,
    BR_W, BR_W, BR_W, BR_W, 2 * DN_HEADS, 2 * DN_HEADS,
    BR_W, BR_W,
    N_BRANCH * D_MODEL,
)
IN_COLS = sum(IN_SPLITS)

kernel_name = 'hybrid_gated_branch_encoder'


def _rmsnorm(x, g):
    xf = x.astype(jnp.float32)
    y = xf * lax.rsqrt(jnp.mean(xf * xf, axis=-1, keepdims=True) + EPS)
    return (y * g.astype(jnp.float32)).astype(x.dtype)


def _layernorm(x, g, b):
    xf = x.astype(jnp.float32)
    mu = jnp.mean(xf, axis=-1, keepdims=True)
    xc = xf - mu
    y = xc * lax.rsqrt(jnp.mean(xc * xc, axis=-1, keepdims=True) + EPS)
    return (y * g.astype(jnp.float32) + b.astype(jnp.float32)).astype(x.dtype)


def _l2norm(x):
    return x * lax.rsqrt(jnp.sum(x * x, axis=-1, keepdims=True) + EPS)


def _axial_rope_tables(L):
    rows = L // GRID_W
    row = jnp.repeat(jnp.arange(rows, dtype=jnp.float32), GRID_W)
    col = jnp.tile(jnp.arange(GRID_W, dtype=jnp.float32), rows)
    n_freq = HEAD_DIM // 4
    freq = ROPE_THETA ** (-jnp.arange(n_freq, dtype=jnp.float32) / n_freq)
    ang = jnp.stack([row[:, None] * freq, col[:, None] * freq], axis=1)
    return jnp.cos(ang), jnp.sin(ang)


def _apply_rope(x, cos, sin):
    B, L, H, _ = x.shape
    xr = x.astype(jnp.float32).reshape(B, L, H, 2, 2, HEAD_DIM // 4)
    x1, x2 = xr[..., 0, :], xr[..., 1, :]
    c = cos[None, :, None]
    s = sin[None, :, None]
    out = jnp.stack([x1 * c - x2 * s, x2 * c + x1 * s], axis=-2)
    return out.reshape(B, L, H, HEAD_DIM).astype(x.dtype)


def _attn_branch(q, k, v, z, qn_g, kn_g):
    B, L, _ = q.shape
    q = _rmsnorm(q.reshape(B, L, ATT_HEADS, HEAD_DIM), qn_g)
    k = _rmsnorm(k.reshape(B, L, ATT_KV, HEAD_DIM), kn_g)
    v = v.reshape(B, L, ATT_KV, HEAD_DIM)
    cos, sin = _axial_rope_tables(L)
    q = _apply_rope(q, cos, sin)
    k = _apply_rope(k, cos, sin)
    grp = ATT_HEADS // ATT_KV
    qb = q.reshape(B, L // Q_BLOCK, Q_BLOCK, ATT_KV, grp, HEAD_DIM).transpose(1, 0, 2, 3, 4, 5)
    scale = HEAD_DIM ** -0.5

    def block(qblk):
        s = jnp.einsum('bqkgd,bskd->bkgqs', qblk, k, preferred_element_type=jnp.float32) * scale
        p = jax.nn.softmax(s, axis=-1).astype(v.dtype)
        return jnp.einsum('bkgqs,bskd->bqkgd', p, v)

    o = lax.map(block, qb)
    o = o.transpose(1, 0, 2, 3, 4, 5).reshape(B, L, BR_W)
    return o * jax.nn.silu(z)


def _cplx_linear_combine(e1, e2):
    a1r, a1i, b1r, b1i = e1
    a2r, a2i, b2r, b2i = e2
    return (a2r * a1r - a2i * a1i,
            a2r * a1i + a2i * a1r,
            a2r * b1r - a2i * b1i + b2r,
            a2r * b1i + a2i * b1r + b2i)


def _ssm_branch(u, z, lam_re, lam_im, log_dt, b_re, b_im, c_re, c_im, d_skip, glu_w, glu_b):
    B, L, _ = u.shape
    uf = u.astype(jnp.float32).reshape(B, L, SSM_G, SSM_P)
    y = uf * d_skip.astype(jnp.float32).reshape(SSM_G, SSM_P)
    for direction in range(2):
        lr = lam_re[direction].astype(jnp.float32)
        li = lam_im[direction].astype(jnp.float32)
        dt = jnp.exp(log_dt[direction].astype(jnp.float32))[:, None]
        mag = jnp.exp(lr * dt)
        ar = mag * jnp.cos(li * dt)
        ai = mag * jnp.sin(li * dt)
        den = lr * lr + li * li
        fr = ((ar - 1.0) * lr + ai * li) / den
        fi = (ai * lr - (ar - 1.0) * li) / den
        bu_r = jnp.einsum('blgp,gnp->blgn', uf, b_re[direction].astype(jnp.float32))
        bu_i = jnp.einsum('blgp,gnp->blgn', uf, b_im[direction].astype(jnp.float32))
        br = fr * bu_r - fi * bu_i
        bi = fr * bu_i + fi * bu_r
        if direction == 1:
            br = jnp.flip(br, axis=1)
            bi = jnp.flip(bi, axis=1)
        a_r = jnp.broadcast_to(ar, (1, L, SSM_G, SSM_N))
        a_i = jnp.broadcast_to(ai, (1, L, SSM_G, SSM_N))
        _, _, hr, hi = lax.associative_scan(_cplx_linear_combine, (a_r, a_i, br, bi), axis=1)
        if direction == 1:
            hr = jnp.flip(hr, axis=1)
            hi = jnp.flip(hi, axis=1)
        y = y + jnp.einsum('blgn,gpn->blgp', hr, c_re[direction].astype(jnp.float32)) \
              - jnp.einsum('blgn,gpn->blgp', hi, c_im[direction].astype(jnp.float32))
    y = jax.nn.gelu(y.reshape(B, L, BR_W)).astype(u.dtype)
    y = y * jax.nn.sigmoid(y @ glu_w + glu_b)
    return y * jax.nn.silu(z)


def _sgu_branch(u, v, z, ln_g, ln_b, w_s, b_s):
    B, L, _ = u.shape
    v = _layernorm(v, ln_g, ln_b)
    vc = v.reshape(B, L // SGU_CHUNK, SGU_CHUNK, SGU_GROUPS, SGU_CH)
    mixed = jnp.einsum('gpq,bcqgd->bcpgd', w_s, vc) + b_s.T[None, None, :, :, None]
    return u * mixed.reshape(B, L, BR_W) * jax.nn.silu(z)


def _chunk_gated_delta(q, k, v, g, beta):
    B, L, H, Dk = q.shape
    Dv = v.shape[-1]
    nc = L // DN_CHUNK

    def chunks(t):
        t = t.reshape(B, nc, DN_CHUNK, H, *t.shape[3:])
        return jnp.moveaxis(t, 3, 1)

    q = chunks(q * (Dk ** -0.5))
    k = chunks(k)
    v = chunks(v)
    g = jnp.cumsum(chunks(g), axis=-1)
    beta = chunks(beta)
    incl = jnp.tril(jnp.ones((DN_CHUNK, DN_CHUNK), dtype=bool))
    strict = jnp.tril(jnp.ones((DN_CHUNK, DN_CHUNK), dtype=bool), -1)
    diff = g[..., :, None] - g[..., None, :]
    decay = jnp.where(incl, jnp.exp(jnp.where(incl, diff, 0.0)), 0.0)
    kb = k * beta[..., None]
    m = jnp.where(strict, jnp.einsum('bhnck,bhnsk->bhncs', kb, k) * decay, 0.0)
    eye = jnp.eye(DN_CHUNK, dtype=m.dtype)
    t_inv = lax.linalg.triangular_solve(m + eye, jnp.broadcast_to(eye, m.shape),
                                        left_side=True, lower=True, unit_diagonal=True)
    u_c = t_inv @ (v * beta[..., None])
    w_c = t_inv @ (kb * jnp.exp(g)[..., None])
    attn = jnp.einsum('bhnck,bhnsk->bhncs', q, k) * decay
    qg = q * jnp.exp(g)[..., None]
    kd = k * jnp.exp(g[..., -1:] - g)[..., None]
    g_last = jnp.exp(g[..., -1])
    xs = (jnp.moveaxis(u_c, 2, 0), jnp.moveaxis(w_c, 2, 0), jnp.moveaxis(attn, 2, 0),
          jnp.moveaxis(qg, 2, 0), jnp.moveaxis(kd, 2, 0), jnp.moveaxis(g_last, 2, 0))

    def step(S, xc):
        u_i, w_i, a_i, qg_i, kd_i, gl_i = xc
        v_new = u_i - jnp.einsum('bhck,bhkv->bhcv', w_i, S)
        o_i = jnp.einsum('bhck,bhkv->bhcv', qg_i, S) + jnp.einsum('bhcs,bhsv->bhcv', a_i, v_new)
        S = S * gl_i[..., None, None] + jnp.einsum('bhck,bhcv->bhkv', kd_i, v_new)
        return S, o_i

    S0 = jnp.zeros((B, H, Dk, Dv), dtype=q.dtype)
    _, o = lax.scan(step, S0, xs)
    return o.transpose(1, 0, 3, 2, 4).reshape(B, L, H, Dv)


def _deltanet_branch(q, k, v, z, beta_raw, a_raw, conv_w, a_log, dt_bias, norm_g):
    B, L, _ = q.shape
    qkv = jnp.concatenate([q, k, v], axis=-1)
    qkv = lax.conv_general_dilated(qkv, conv_w[:, None, :], window_strides=(1,),
                                   padding=[((CONV_W - 1) // 2, CONV_W // 2)],
                                   dimension_numbers=('NWC', 'WIO', 'NWC'),
                                   feature_group_count=3 * BR_W)
    qkv = jax.nn.silu(qkv).astype(jnp.float32)
    q, k, v = jnp.split(qkv, 3, axis=-1)
    q = _l2norm(q.reshape(B, L, DN_HEADS, HEAD_DIM))
    k = _l2norm(k.reshape(B, L, DN_HEADS, HEAD_DIM))
    v = v.reshape(B, L, DN_HEADS, HEAD_DIM)
    beta = jax.nn.sigmoid(beta_raw.astype(jnp.float32)).reshape(B, L, 2, DN_HEADS)
    g = -jnp.exp(a_log.astype(jnp.float32)) * jax.nn.softplus(
        a_raw.astype(jnp.float32).reshape(B, L, 2, DN_HEADS) + dt_bias.astype(jnp.float32))
    o_f = _chunk_gated_delta(q, k, v, g[:, :, 0], beta[:, :, 0])
    fl = lambda t: jnp.flip(t, axis=1)
    o_b = fl(_chunk_gated_delta(fl(q), fl(k), fl(v), fl(g[:, :, 1]), fl(beta[:, :, 1])))
    o = _rmsnorm(o_f + o_b, norm_g).reshape(B, L, BR_W)
    return (o * jax.nn.silu(z.astype(jnp.float32))).astype(z.dtype)


def _xattn_branch(q, z, mem_n, w_kv):
    B, L, _ = q.shape
    M = mem_n.shape[1]
    k, v = jnp.split(mem_n @ w_kv, 2, axis=-1)
    k = k.reshape(B, M, XA_HEADS, XA_HD)
    v = v.reshape(B, M, XA_HEADS, XA_HD)
    q = q.reshape(B, L, XA_HEADS, XA_HD)
    s = jnp.einsum('blhd,bmhd->bhlm', q, k, preferred_element_type=jnp.float32) * (XA_HD ** -0.5)
    p = jax.nn.softmax(s, axis=-1).astype(v.dtype)
    o = jnp.einsum('bhlm,bmhd->blhd', p, v).reshape(B, L, BR_W)
    return o * jax.nn.silu(z)


def _trunk(x, mem, p):
    B, L, _ = x.shape
    offsets = [int(o) for o in np.cumsum(IN_SPLITS)[:-1]]
    for i in range(DEPTH):
        h = _rmsnorm(x, p['ln_g'][i])
        (aq, ak, av, az, su, sz, gu, gv, gz, dq, dk, dv, dz, dbeta, da, xq, xz, gates) = \
            jnp.split(h @ p['w_in'][i], offsets, axis=-1)
        mem_n = _rmsnorm(mem, p['mem_ln_g'][i])
        branches = (
            _attn_branch(aq, ak, av, az, p['attn_qn_g'][i], p['attn_kn_g'][i]),
            _ssm_branch(su, sz, p['ssm_lam_re'][i], p['ssm_lam_im'][i], p['ssm_log_dt'][i],
                        p['ssm_b_re'][i], p['ssm_b_im'][i], p['ssm_c_re'][i], p['ssm_c_im'][i],
                        p['ssm_d'][i], p['ssm_glu_w'][i], p['ssm_glu_b'][i]),
            _sgu_branch(gu, gv, gz, p['sgu_ln_g'][i], p['sgu_ln_b'][i], p['sgu_w'][i], p['sgu_b'][i]),
            _deltanet_branch(dq, dk, dv, dz, dbeta, da, p['dn_conv_w'][i], p['dn_a_log'][i],
                             p['dn_dt_bias'][i], p['dn_norm_g'][i]),
            _xattn_branch(xq, xz, mem_n, p['xa_w_kv'][i]),
        )
        gates = jax.nn.sigmoid(gates.reshape(B, L, N_BRANCH, D_MODEL))
        merged = gates[:, :, 0] * (branches[0] @ p['w_branch'][i, 0])
        for c in range(1, N_BRANCH):
            merged = merged + gates[:, :, c] * (branches[c] @ p['w_branch'][i, c])
        x = x + (merged @ p['w_out'][i]).astype(x.dtype)
    return _rmsnorm(x, p['final_g'])


def setup_inputs(seed: int = 0) -> dict:
    key = jax.random.key(seed)
    ks = iter(jax.random.split(key, 40))
    f32 = jnp.float32

    def nrm(shape, scale):
        return jax.random.normal(next(ks), shape, f32) * scale

    x_prompt = nrm((BATCH, SEQ, D_MODEL), 1.0)
    x_sample = nrm((DEC_BATCH, DEC_SEQ, D_MODEL), 1.0)
    mem_prompt = nrm((BATCH, MEM_TOKENS, D_MODEL), 1.0)
    mem_sample = nrm((DEC_BATCH, MEM_TOKENS, D_MODEL), 1.0)
    ln_g = 1.0 + nrm((DEPTH, D_MODEL), 0.02)
    mem_ln_g = 1.0 + nrm((DEPTH, D_MODEL), 0.02)
    w_in = nrm((DEPTH, D_MODEL, IN_COLS), D_MODEL ** -0.5)
    attn_qn_g = 1.0 + nrm((DEPTH, HEAD_DIM), 0.02)
    attn_kn_g = 1.0 + nrm((DEPTH, HEAD_DIM), 0.02)
    n_idx = jnp.arange(SSM_N, dtype=f32)
    ssm_lam_re = -0.5 + nrm((DEPTH, 2, SSM_G, SSM_N), 0.01)
    ssm_lam_im = math.pi * n_idx + nrm((DEPTH, 2, SSM_G, SSM_N), 0.01)
    ssm_log_dt = jax.random.uniform(next(ks), (DEPTH, 2, SSM_G), f32,
                                    math.log(SSM_DT_MIN), math.log(SSM_DT_MAX))
    ssm_b_re = nrm((DEPTH, 2, SSM_G, SSM_N, SSM_P), (2 * SSM_P) ** -0.5)
    ssm_b_im = nrm((DEPTH, 2, SSM_G, SSM_N, SSM_P), (2 * SSM_P) ** -0.5)
    ssm_c_re = nrm((DEPTH, 2, SSM_G, SSM_P, SSM_N), (2 * SSM_N) ** -0.5)
    ssm_c_im = nrm((DEPTH, 2, SSM_G, SSM_P, SSM_N), (2 * SSM_N) ** -0.5)
    ssm_d = nrm((DEPTH, BR_W), 1.0)
    ssm_glu_w = nrm((DEPTH, BR_W, BR_W), BR_W ** -0.5)
    ssm_glu_b = nrm((DEPTH, BR_W), 0.02)
    sgu_ln_g = 1.0 + nrm((DEPTH, BR_W), 0.02)
    sgu_ln_b = nrm((DEPTH, BR_W), 0.02)
    sgu_w = nrm((DEPTH, SGU_GROUPS, SGU_CHUNK, SGU_CHUNK), SGU_CHUNK ** -0.5)
    sgu_b = 1.0 + nrm((DEPTH, SGU_GROUPS, SGU_CHUNK), 0.02)
    dn_conv_w = nrm((DEPTH, CONV_W, 3 * BR_W), CONV_W ** -0.5)
    dn_a_log = jnp.log(jax.random.uniform(next(ks), (DEPTH, 2, DN_HEADS), f32, 1.0, 16.0))
    dn_dt_bias = nrm((DEPTH, 2, DN_HEADS), 0.1)
    dn_norm_g = 1.0 + nrm((DEPTH, HEAD_DIM), 0.02)
    xa_w_kv = nrm((DEPTH, D_MODEL, 2 * BR_W), D_MODEL ** -0.5)
    w_branch = nrm((DEPTH, N_BRANCH, BR_W, D_MODEL), BR_W ** -0.5)
    w_out = nrm((DEPTH, D_MODEL, D_MODEL), D_MODEL ** -0.5)
    final_g = 1.0 + nrm((D_MODEL,), 0.02)
    return {'x_prompt': x_prompt, 'x_sample': x_sample, 'mem_prompt': mem_prompt,
            'mem_sample': mem_sample, 'ln_g': ln_g, 'mem_ln_g': mem_ln_g, 'w_in': w_in,
            'attn_qn_g': attn_qn_g, 'attn_kn_g': attn_kn_g, 'ssm_lam_re': ssm_lam_re,
            'ssm_lam_im': ssm_lam_im, 'ssm_log_dt': ssm_log_dt, 'ssm_b_re': ssm_b_re,
            'ssm_b_im': ssm_b_im, 'ssm_c_re': ssm_c_re, 'ssm_c_im': ssm_c_im, 'ssm_d': ssm_d,
            'ssm_glu_w': ssm_glu_w, 'ssm_glu_b': ssm_glu_b, 'sgu_ln_g': sgu_ln_g,
            'sgu_ln_b': sgu_ln_b, 'sgu_w': sgu_w, 'sgu_b': sgu_b, 'dn_conv_w': dn_conv_w,
            'dn_a_log': dn_a_log, 'dn_dt_bias': dn_dt_bias, 'dn_norm_g': dn_norm_g,
            'xa_w_kv': xa_w_kv, 'w_branch': w_branch, 'w_out': w_out, 'final_g': final_g}


def reference(x_prompt, x_sample, mem_prompt, mem_sample, ln_g, mem_ln_g, w_in, attn_qn_g,
              attn_kn_g, ssm_lam_re, ssm_lam_im, ssm_log_dt, ssm_b_re, ssm_b_im, ssm_c_re,
              ssm_c_im, ssm_d, ssm_glu_w, ssm_glu_b, sgu_ln_g, sgu_ln_b, sgu_w, sgu_b,
              dn_conv_w, dn_a_log, dn_dt_bias, dn_norm_g, xa_w_kv, w_branch, w_out, final_g):
    params = dict(ln_g=ln_g, mem_ln_g=mem_ln_g, w_in=w_in, attn_qn_g=attn_qn_g,
                  attn_kn_g=attn_kn_g, ssm_lam_re=ssm_lam_re, ssm_lam_im=ssm_lam_im,
                  ssm_log_dt=ssm_log_dt, ssm_b_re=ssm_b_re, ssm_b_im=ssm_b_im,
                  ssm_c_re=ssm_c_re, ssm_c_im=ssm_c_im, ssm_d=ssm_d, ssm_glu_w=ssm_glu_w,
                  ssm_glu_b=ssm_glu_b, sgu_ln_g=sgu_ln_g, sgu_ln_b=sgu_ln_b, sgu_w=sgu_w,
                  sgu_b=sgu_b, dn_conv_w=dn_conv_w, dn_a_log=dn_a_log, dn_dt_bias=dn_dt_bias,
                  dn_norm_g=dn_norm_g, xa_w_kv=xa_w_kv, w_branch=w_branch, w_out=w_out,
                  final_g=final_g)
    y_prompt = _trunk(x_prompt, mem_prompt, params)
    y_sample = _trunk(x_sample, mem_sample, params)
    return (y_prompt, y_sample)
```

```python
import math
from contextlib import ExitStack
import numpy as np
import concourse.bass as bass
import concourse.mybir as mybir
from concourse.bass_utils import run_bass_kernel_spmd

F32 = mybir.dt.float32
BF16 = mybir.dt.bfloat16
I32 = mybir.dt.int32
AF = mybir.ActivationFunctionType
ALU = mybir.AluOpType
AX = mybir.AxisListType

NDMA = 56
D = 2048
BR = 1024
NCOL = 24096
EPS = 1e-6
OFF = dict(aq=0, ak=1024, av=1280, az=1536, su=2560, sz=3584, gu=4608, gv=5632, gz=6656,
           dq=7680, dk=8704, dv=9728, dz=10752, dbeta=11776, da=11792, xq=11808, xz=12832,
           gates=13856)
TWO_PI = 2.0 * math.pi


class Buf:
    __slots__ = ("name", "h", "last_w", "readers")

    def __init__(self, name, h=None):
        self.name = name
        self.h = h
        self.last_w = None
        self.readers = []

    def __getitem__(self, k):
        return self.h[k]


class Prog:
    CE = ("pe", "act", "dve", "pool")

    def __init__(self, nc):
        self.nc = nc
        self.q = {e: [] for e in ("pe", "act", "dve", "pool", "sp")}
        self.sem = {e: nc.alloc_semaphore("sem_" + e) for e in self.CE}
        self.cnt = {e: 0 for e in self.CE}
        self.dsem = [nc.alloc_semaphore("dsem%d" % i) for i in range(NDMA)]
        self.dcnt = [0] * NDMA
        self.drr = 0
        self.waited = {e: {} for e in self.q}
        self.regions = {}
        self.out_events = []
        self.uid = 0
        self.n_ops = 0

    def name(self, s):
        self.uid += 1
        return "%s_%d" % (s, self.uid)

    def region(self, *key):
        b = self.regions.get(key)
        if b is None:
            b = Buf(str(key))
            self.regions[key] = b
        return b

    def _collect(self, eng, reads, writes):
        evs = []
        for b in reads:
            if b.last_w is not None:
                evs.append(b.last_w)
        for b in writes:
            if b.last_w is not None:
                evs.append(b.last_w)
            evs.extend(b.readers)
        waits = []
        wd = self.waited[eng]
        for (key, val) in evs:
            if key[0] == "c":
                if key[1] == eng and eng == "pe":
                    continue
                assert self.cnt[key[1]] >= val, "wait on not-yet-emitted inc %s %d" % (key, val)
            if wd.get(key, 0) >= val:
                continue
            wd[key] = val
            waits.append((key, val))
        return waits

    def _commit(self, ev, reads, writes):
        for b in reads:
            if len(b.readers) > 24:
                mx = {}
                for k, v in b.readers:
                    if mx.get(k, 0) < v:
                        mx[k] = v
                b.readers = list(mx.items())
            b.readers.append(ev)
        for b in writes:
            b.last_w = ev
            b.readers = []

    def op(self, eng, fn, reads=(), writes=(), inc=True):
        waits = self._collect(eng, reads, writes)
        if inc:
            self.cnt[eng] += 1
            ev = (("c", eng), self.cnt[eng])
        else:
            ev = (("c", eng), self.cnt[eng] + 1)
        self.q[eng].append((waits, fn, inc, None))
        self._commit(ev, reads, writes)
        self.n_ops += 1

    def dma(self, out_ap, in_ap, reads=(), writes=(), q="sp", is_out=False, slow=False):
        waits = self._collect(q, reads, writes)
        j = self.drr
        self.drr = (self.drr + 1) % NDMA
        prev = self.dcnt[j] * 16
        key = ("d", j)
        if prev and self.waited[q].get(key, 0) < prev:
            self.waited[q][key] = prev
            waits.append((key, prev))
        self.dcnt[j] += 1
        ev = (key, self.dcnt[j] * 16)
        self.q[q].append((waits, (out_ap, in_ap, slow), True, j))
        self._commit(ev, reads, writes)
        if is_out:
            self.out_events.append(ev)
        self.n_ops += 1

    def barrier(self):
        for e in self.q:
            wd = self.waited[e]
            waits = []
            for c in self.CE:
                if c == e:
                    continue
                v = self.cnt[c]
                if v and wd.get(("c", c), 0) < v:
                    wd[("c", c)] = v
                    waits.append((("c", c), v))
            for j in range(NDMA):
                v = self.dcnt[j] * 16
                if v and wd.get(("d", j), 0) < v:
                    wd[("d", j)] = v
                    waits.append((("d", j), v))
            if waits:
                self.q[e].append((waits, None, False, None))

    def _semof(self, key):
        return self.sem[key[1]] if key[0] == "c" else self.dsem[key[1]]

    def emit(self):
        nc = self.nc
        mx = {}
        for k, v in self.out_events:
            mx[k] = max(mx.get(k, 0), v)
        prog = self

        def run(engname, eobj):
            for (waits, fn, inc, dj) in prog.q[engname]:
                for (key, val) in waits:
                    eobj.wait_ge(prog._semof(key), val)
                if fn is None:
                    continue
                if dj is not None:
                    out_ap, in_ap, slow = fn
                    if slow:
                        eobj.dma_start(out=out_ap, in_=in_ap, allow_slow_non_contiguous=True).then_inc(prog.dsem[dj], 16)
                    else:
                        eobj.dma_start(out=out_ap, in_=in_ap).then_inc(prog.dsem[dj], 16)
                else:
                    ins = fn(eobj)
                    if inc:
                        ins.then_inc(prog.sem[engname], 1)
            if engname == "sp":
                for k, v in mx.items():
                    eobj.wait_ge(prog._semof(k), v)

        with nc.Block() as block:
            @block.tensor
            def _(e):
                run("pe", e)

            @block.scalar
            def _(e):
                run("act", e)

            @block.vector
            def _(e):
                run("dve", e)

            @block.gpsimd
            def _(e):
                run("pool", e)

            @block.sync
            def _(e):
                run("sp", e)


class Scope:
    def __init__(self, P):
        self.P = P
        self.es = ExitStack()

    def __enter__(self):
        self.es.__enter__()
        return self

    def __exit__(self, *a):
        self.P.barrier()
        return self.es.__exit__(*a)

    def sb(self, name, shape, dtype=F32):
        h = self.es.enter_context(self.P.nc.sbuf_tensor(self.P.name(name), list(shape), dtype))
        return Buf(name, h)

    def ps(self, name, dtype=F32):
        n = 512 if dtype == F32 else 1024
        h = self.es.enter_context(self.P.nc.psum_tensor(self.P.name(name), [128, n], dtype))
        return Buf(name, h)

    def pool(self, name, shape, dtype, n):
        return Rot([self.sb("%s%d" % (name, i), shape, dtype) for i in range(n)])

    def pspool(self, name, n, dtype=F32):
        return Rot([self.ps("%s%d" % (name, i), dtype) for i in range(n)])


class Rot:
    def __init__(self, tiles):
        self.tiles = tiles
        self.i = 0

    def next(self):
        t = self.tiles[self.i]
        self.i = (self.i + 1) % len(self.tiles)
        return t


class K:
    def __init__(self, nc, seqs, depth=2, debug=False, branches=(0, 1, 2, 3, 4)):
        self.nc = nc
        self.P = Prog(nc)
        self.seqs = seqs
        self.depth = depth
        self.debug = debug
        self.branches = branches
        self.Lmax = max(L for _, L in seqs)
        self.eng_rr = 0
        self.declare()

    def declare(self):
        nc = self.nc
        self.inp = {}

        def din(name, shape):
            self.inp[name] = nc.dram_tensor(name, list(shape), F32, kind="ExternalInput").ap()

        for sn, L in self.seqs:
            din("x_" + sn, [L, D])
            din("mem_" + sn, [256, D])
        dp = self.depth
        din("ln_g", [dp, D]); din("mem_ln_g", [dp, D]); din("w_in", [dp, D, NCOL])
        din("attn_qn_g", [dp, 128]); din("attn_kn_g", [dp, 128])
        din("ssm_lam_re", [dp, 2, 64, 64]); din("ssm_lam_im", [dp, 2, 64, 64]); din("ssm_log_dt", [dp, 2, 64])
        din("ssm_b_re", [dp, 2, 64, 64, 16]); din("ssm_b_im", [dp, 2, 64, 64, 16])
        din("ssm_c_re", [dp, 2, 64, 16, 64]); din("ssm_c_im", [dp, 2, 64, 16, 64])
        din("ssm_d", [dp, BR]); din("ssm_glu_w", [dp, BR, BR]); din("ssm_glu_b", [dp, BR])
        din("sgu_ln_g", [dp, BR]); din("sgu_ln_b", [dp, BR]); din("sgu_w", [dp, 8, 128, 128]); din("sgu_b", [dp, 8, 128])
        din("dn_conv_w", [dp, 3, 3 * BR]); din("dn_a_log", [dp, 2, 8]); din("dn_dt_bias", [dp, 2, 8]); din("dn_norm_g", [dp, 128])
        din("xa_w_kv", [dp, D, 2 * BR]); din("w_branch", [dp, 5, BR, D]); din("w_out", [dp, D, D]); din("final_g", [D])
        self.out = {}
        for sn, L in self.seqs:
            self.out[sn] = nc.dram_tensor("y_" + sn, [L, D], F32, kind="ExternalOutput").ap()
        Lm = self.Lmax
        kind = "ExternalOutput" if self.debug else "Internal"
        self.scr = {}

        def dscr(name, shape, dt=F32):
            self.scr[name] = nc.dram_tensor("scr_" + name, list(shape), dt, kind=kind).ap()

        for nm, rows in (("aq", 1024), ("ak", 256), ("az", 1024), ("su", 1024),
                         ("sz", 1024), ("gu", 1024), ("gz", 1024), ("dqkv", 3072), ("dz", 1024),
                         ("xq", 1024), ("xz", 1024)):
            dscr(nm, [rows, Lm])
        dscr("av", [Lm, 256]); dscr("gv", [Lm, 1024]); dscr("dba", [Lm, 32])
        dscr("brT", [5 * BR, Lm], BF16)
        dscr("xres", [Lm, D])
        dscr("syT", [BR, Lm], BF16)
        dscr("dn_of", [Lm, BR])
        if self.debug:
            dscr("dbg", [BR, Lm])
            dscr("dbg2", [BR, Lm])

    def consts(self, S):
        P = self.P
        c = {}
        c["identf"] = S.sb("identf", [128, 128], F32)
        c["identb"] = S.sb("identb", [128, 128], BF16)
        c["onesf"] = S.sb("onesf", [128, 128], F32)
        c["onesb"] = S.sb("onesb", [128, 128], BF16)
        idf, idb, of, ob = c["identf"], c["identb"], c["onesf"], c["onesb"]
        P.op("pool", lambda e: e.memset(idf[:], 0.0), writes=[idf])
        P.op("pool", lambda e: e.affine_select(out=idf[:], in_=idf[:], pattern=[[-1, 128]], compare_op=ALU.not_equal,
                                              fill=1.0, base=0, channel_multiplier=1), reads=[idf], writes=[idf])
        P.op("dve", lambda e: e.tensor_copy(out=idb[:], in_=idf[:]), reads=[idf], writes=[idb])
        P.op("pool", lambda e: e.memset(of[:], 1.0), writes=[of])
        P.op("dve", lambda e: e.tensor_copy(out=ob[:], in_=of[:]), reads=[of], writes=[ob])
        self.c = c

    def evac_eng(self):
        self.eng_rr ^= 1
        return "act" if self.eng_rr else "dve"

    def copy(self, eng, out_ap, in_ap, reads, writes):
        if eng == "act":
            self.P.op("act", lambda e: e.activation(out=out_ap, in_=in_ap, func=AF.Copy), reads=reads, writes=writes)
        else:
            self.P.op(eng, lambda e: e.tensor_copy(out=out_ap, in_=in_ap), reads=reads, writes=writes)

    def rstd_from_ss(self, ss, rs, rstd, n):
        P = self.P
        P.op("act", lambda e: e.activation(out=rs[:], in_=ss[:], func=AF.Sqrt, scale=1.0 / n, bias=EPS), reads=[ss], writes=[rs])
        P.op("dve", lambda e: e.reciprocal(out=rstd[:], in_=rs[:]), reads=[rs], writes=[rstd])

    def rmsnorm_tile(self, xt, gt, out, tp):
        P = self.P
        junk, ss, rs, rstd = tp["junk"].next(), tp["ss"].next(), tp["rs"].next(), tp["rstd"].next()
        P.op("act", lambda e: e.activation(out=junk[:], in_=xt[:], func=AF.Square, accum_out=ss[:]), reads=[xt], writes=[junk, ss])
        self.rstd_from_ss(ss, rs, rstd, D)
        P.op("dve", lambda e: e.scalar_tensor_tensor(out=out[:], in0=xt[:], scalar=rstd[:], in1=gt[:], op0=ALU.mult, op1=ALU.mult),
             reads=[xt, rstd, gt], writes=[out])

    def norm_pools(self, S):
        return dict(junk=S.pool("junk", [128, D], BF16, 1), ss=S.pool("ss", [128, 1], F32, 2),
                    rs=S.pool("rs", [128, 1], F32, 2), rstd=S.pool("rstd", [128, 1], F32, 2))

    def make_hT(self, S, src_rows, gt, hT, ntiles, tp, xpool, hbpool, pstp, src_reads):
        P = self.P
        idb = self.c["identb"]
        for tt in range(ntiles):
            xt = xpool.next()
            P.dma(xt[:], src_rows(tt), reads=src_reads(tt), writes=[xt])
            hb = hbpool.next()
            self.rmsnorm_tile(xt, gt, hb, tp)
            for k4 in range(4):
                pt = pstp.next()
                for kk in range(4):
                    k = k4 * 4 + kk
                    P.op("pe", lambda e, k=k, kk=kk, pt=pt, hb=hb: e.transpose(out=pt[:, kk * 128:(kk + 1) * 128], in_=hb[:, k * 128:(k + 1) * 128], identity=idb[:]),
                         reads=[hb, idb], writes=[pt], inc=(kk == 3))
                self.copy(self.evac_eng(), hT[:, k4 * 4:(k4 + 1) * 4, tt * 128:(tt + 1) * 128],
                          pt[:, 0:512].rearrange("p (k t) -> p k t", k=4), [pt], [hT])

    def phase1(self, l, sn, L, xsrc, xsrc_region):
        P = self.P
        S1 = min(L, 2048)
        w_in = self.inp["w_in"][l]
        fgroups = [("aq", OFF["aq"], 1024, False), ("ak", OFF["ak"], 256, False),
                   ("az", OFF["az"], 1024, False), ("su", OFF["su"], 1024, False),
                   ("sz", OFF["sz"], 1024, False), ("gu", OFF["gu"], 1024, False), ("gz", OFF["gz"], 1024, False),
                   ("dqkv", OFF["dq"], 3072, False), ("dz", OFF["dz"], 1024, False), ("xq", OFF["xq"], 1024, False),
                   ("xz", OFF["xz"], 1024, False)]
        tgroups = [("av", OFF["av"], 256), ("gv", OFF["gv"], 1024), ("dba", OFF["dbeta"], 32)]
        with Scope(P) as S:
            hT = S.sb("hT", [128, 16, S1], BF16)
            gt = S.sb("gt", [128, D], F32)
            tp = self.norm_pools(S)
            xpool = S.pool("xt", [128, D], F32, 2)
            hbpool = S.pool("hb", [128, D], BF16, 2)
            pstp = S.pspool("pst", 2, BF16)
            psm = S.pspool("psm", 4, F32)
            wpool = S.pool("wb", [128, 16, 256], BF16, 3)
            opool = S.pool("ost", [128, S1], F32, 2)
            otpool = S.pool("ott", [128, 256], F32, 2)
            P.dma(gt[:], self.inp["ln_g"][l].partition_broadcast(128), writes=[gt])
            for seg in range(L // S1):
                t0 = seg * S1
                self.make_hT(S, lambda tt: xsrc[t0 + tt * 128:t0 + (tt + 1) * 128, :], gt, hT, S1 // 128, tp, xpool, hbpool, pstp,
                             lambda tt: [xsrc_region(t0 + tt * 128)])
                for (nm, off, ncols, perm) in fgroups:
                    for c0 in range(0, ncols, 256):
                        wb = wpool.next()
                        src = w_in[:, off + c0:off + c0 + 256].rearrange("(k p) c -> p k c", p=128)
                        if not perm:
                            P.dma(wb[:], src, writes=[wb], q="pool")
                        else:
                            s4 = src.rearrange("p k (b two f) -> p k b two f", two=2, f=32)
                            d4 = wb[:].rearrange("p k (b two f) -> p k b two f", two=2, f=32)
                            for kq in range(0, 16, 4):
                                P.dma(d4[:, kq:kq + 4, :, 0, :], s4[:, kq:kq + 4, :, 1, :], writes=[wb], q="pool")
                                P.dma(d4[:, kq:kq + 4, :, 1, :], s4[:, kq:kq + 4, :, 0, :], writes=[wb], q="pool")
                        for ct in range(2):
                            ost = opool.next()
                            TS = min(512, S1)
                            for ts in range(S1 // TS):
                                ps = psm.next()
                                for k in range(16):
                                    P.op("pe", lambda e, k=k, ps=ps, wb=wb, ct=ct, ts=ts: e.matmul(
                                        ps[:, 0:TS], lhsT=wb[:, k, ct * 128:(ct + 1) * 128], rhs=hT[:, k, ts * TS:(ts + 1) * TS],
                                        start=(k == 0), stop=(k == 15)), reads=[wb, hT], writes=[ps], inc=(k == 15))
                                self.copy(self.evac_eng(), ost[:, ts * TS:(ts + 1) * TS], ps[:, 0:TS], [ps], [ost])
                            r0 = c0 + ct * 128
                            P.dma(self.scr[nm][r0:r0 + 128, t0:t0 + S1], ost[:, :], reads=[ost],
                                  writes=[self.P.region(nm, r0 // 128, seg)])
                for (nm, off, ncols) in tgroups:
                    for c0 in range(0, ncols, 256):
                        cw = min(256, ncols - c0)
                        wb = wpool.next()
                        src = w_in[:, off + c0:off + c0 + cw].rearrange("(k p) c -> p k c", p=128)
                        P.dma(wb[:, :, 0:cw], src, writes=[wb], q="pool")
                        for tt in range(S1 // 128):
                            ps = psm.next()
                            for k in range(16):
                                P.op("pe", lambda e, k=k, ps=ps, wb=wb, tt=tt, cw=cw: e.matmul(
                                    ps[:, 0:cw], lhsT=hT[:, k, tt * 128:(tt + 1) * 128], rhs=wb[:, k, 0:cw],
                                    start=(k == 0), stop=(k == 15)), reads=[wb, hT], writes=[ps], inc=(k == 15))
                            ott = otpool.next()
                            self.copy(self.evac_eng(), ott[:, 0:cw], ps[:, 0:cw], [ps], [ott])
                            tok = t0 + tt * 128
                            P.dma(self.scr[nm][tok:tok + 128, c0:c0 + cw], ott[:, 0:cw], reads=[ott],
                                  writes=[self.P.region(nm, "t", tok // 128, c0)])

    def sgu(self, l, L):
        P = self.P
        idf = self.c["identf"]
        with Scope(P) as S:
            wsn = S.sb("wsn", [128, 8, 128], F32)
            wsT = S.sb("wsT", [128, 8, 128], BF16)
            lng = S.sb("lng", [128, BR], F32)
            lnb = S.sb("lnb", [128, BR], F32)
            bsb = S.sb("bsb", [128, BR], F32)
            pst = S.pspool("pst", 2, F32)
            psm = S.pspool("psm", 4, F32)
            P.dma(wsn[:], self.inp["sgu_w"][l].rearrange("g p q -> p g q"), writes=[wsn])
            P.dma(lng[:], self.inp["sgu_ln_g"][l].partition_broadcast(128), writes=[lng])
            P.dma(lnb[:], self.inp["sgu_ln_b"][l].partition_broadcast(128), writes=[lnb])
            P.dma(bsb[:], self.inp["sgu_b"][l].rearrange("g p -> (g p)").partition_broadcast(128), writes=[bsb])
            for g in range(8):
                pt = pst.next()
                P.op("pe", lambda e, g=g, pt=pt: e.transpose(out=pt[:, 0:128], in_=wsn[:, g, :], identity=idf[:]), reads=[wsn, idf], writes=[pt])
                self.copy(self.evac_eng(), wsT[:, g, :], pt[:, 0:128], [pt], [wsT])
            vpool = S.pool("v", [128, BR], F32, 2)
            vnpool = S.pool("vn", [128, BR], F32, 2)
            vbpool = S.pool("vb", [128, BR], BF16, 2)
            stp = S.pool("st", [128, 2, 6], F32, 2)
            mvp = S.pool("mv", [128, 2], F32, 2)
            rsp = S.pool("rs", [128, 1], F32, 2)
            rstdp = S.pool("rstd", [128, 1], F32, 2)
            gup = S.pool("gu", [128, 8, 128], F32, 2)
            gzp = S.pool("gz", [128, 8, 128], F32, 2)
            szp = S.pool("sz", [128, 8, 128], F32, 2)
            t1p = S.pool("t1", [128, 8, 128], F32, 2)
            obp = S.pool("ob", [128, 8, 128], BF16, 2)
            for n in range(L // 128):
                tok = n * 128
                v = vpool.next()
                P.dma(v[:], self.scr["gv"][tok:tok + 128, :], reads=[P.region("gv", "t", n, c0) for c0 in range(0, 1024, 256)], writes=[v])
                gu = gup.next(); gz = gzp.next()
                seg = tok // min(L, 2048)
                P.dma(gu[:], self.scr["gu"][:, tok:tok + 128].rearrange("(g d) t -> d g t", d=128),
                      reads=[P.region("gu", g, seg) for g in range(8)], writes=[gu])
                P.dma(gz[:], self.scr["gz"][:, tok:tok + 128].rearrange("(g d) t -> d g t", d=128),
                      reads=[P.region("gz", g, seg) for g in range(8)], writes=[gz])
                st = stp.next(); mv = mvp.next(); rs = rsp.next(); rstd = rstdp.next()
                for hh in range(2):
                    P.op("dve", lambda e, hh=hh, st=st, v=v: e.bn_stats(out=st[:, hh, :], in_=v[:, hh * 512:(hh + 1) * 512]), reads=[v], writes=[st])
                P.op("dve", lambda e, st=st, mv=mv: e.bn_aggr(out=mv[:], in_=st[:].rearrange("p a b -> p (a b)")), reads=[st], writes=[mv])
                P.op("act", lambda e, mv=mv, rs=rs: e.activation(out=rs[:], in_=mv[:, 1:2], func=AF.Sqrt, scale=1.0, bias=EPS), reads=[mv], writes=[rs])
                P.op("dve", lambda e, rs=rs, rstd=rstd: e.reciprocal(out=rstd[:], in_=rs[:]), reads=[rs], writes=[rstd])
                vn = vnpool.next(); vb = vbpool.next()
                P.op("dve", lambda e, v=v, vn=vn, mv=mv, rstd=rstd: e.tensor_scalar(out=vn[:], in0=v[:], scalar1=mv[:, 0:1], scalar2=rstd[:],
                                                                                     op0=ALU.subtract, op1=ALU.mult), reads=[v, mv, rstd], writes=[vn])
                P.op("dve", lambda e, vn=vn: e.tensor_tensor(out=vn[:], in0=vn[:], in1=lng[:], op=ALU.mult), reads=[vn, lng], writes=[vn])
                P.op("dve", lambda e, vn=vn, vb=vb: e.tensor_tensor(out=vb[:], in0=vn[:], in1=lnb[:], op=ALU.add), reads=[vn, lnb], writes=[vb])
                sz = szp.next()
                P.op("act", lambda e, sz=sz, gz=gz: e.activation(out=sz[:], in_=gz[:], func=AF.Silu), reads=[gz], writes=[sz])
                t1 = t1p.next(); ob = obp.next()
                for hh in range(2):
                    ps = psm.next()
                    for gg in range(4):
                        g = hh * 4 + gg
                        P.op("pe", lambda e, g=g, gg=gg, ps=ps, vb=vb: e.matmul(ps[:, gg * 128:(gg + 1) * 128], lhsT=vb[:, g * 128:(g + 1) * 128], rhs=wsT[:, g, :],
                                                                              start=True, stop=True), reads=[vb, wsT], writes=[ps], inc=(gg == 3))
                    P.op("dve", lambda e, ps=ps, t1=t1, hh=hh: e.tensor_tensor(out=t1[:, hh * 4:(hh + 1) * 4, :], in0=ps[:, :].rearrange("p (g q) -> p g q", g=4),
                                                                              in1=bsb[:, hh * 512:(hh + 1) * 512].rearrange("p (g q) -> p g q", g=4), op=ALU.add),
                         reads=[ps, bsb], writes=[t1])
                P.op("dve", lambda e, t1=t1, gu=gu: e.tensor_tensor(out=t1[:], in0=t1[:], in1=gu[:], op=ALU.mult), reads=[t1, gu], writes=[t1])
                P.op("dve", lambda e, t1=t1, sz=sz, ob=ob: e.tensor_tensor(out=ob[:], in0=t1[:], in1=sz[:], op=ALU.mult), reads=[t1, sz], writes=[ob])
                P.dma(self.scr["brT"][2 * BR:3 * BR, tok:tok + 128].rearrange("(g d) t -> d g t", d=128), ob[:], reads=[ob],
                      writes=[P.region("brT", 2, n, f) for f in range(8)])

    def phase3(self, l, sn, L, xsrc, xsrc_region, last):
        P = self.P
        S3 = min(512, L)
        NT3 = S3 // 128
        w_in = self.inp["w_in"][l]
        wbr = self.inp["w_branch"][l]
        wout = self.inp["w_out"][l]
        with Scope(P) as S:
            hT = S.sb("hT", [128, 16, S3], BF16)
            gt = S.sb("gt", [128, D], F32)
            fg = S.sb("fg", [128, D], F32)
            tp = self.norm_pools(S)
            xall = [S.sb("xa%d" % i, [128, D], F32) for i in range(NT3)]
            hbpool = S.pool("hb", [128, D], BF16, 2)
            pstp = S.pspool("pst", 2, BF16)
            psg = S.pspool("psg", 2, F32)
            psb = S.pspool("psb", 2, F32)
            pso = S.pspool("pso", 2, F32)
            wgp = S.pool("wg", [128, 16, 128], BF16, 3)
            wbp = S.pool("wbb", [128, 8, 128], BF16, 3)
            wop = S.pool("wo", [128, 16, 256], BF16, 2)
            btp = S.pool("bt", [128, 8, S3], BF16, 2)
            sgp = S.pool("sg", [128, S3], F32, 2)
            tmp = S.pool("tmp", [128, S3], F32, 2)
            merged = S.sb("merged", [128, 16, S3], F32)
            mergedb = S.sb("mergedb", [128, 16, S3], BF16)
            yop = S.pool("yo", [128, D], F32, 1)
            P.dma(gt[:], self.inp["ln_g"][l].partition_broadcast(128), writes=[gt])
            if last:
                P.dma(fg[:], self.inp["final_g"].partition_broadcast(128), writes=[fg])
            for seg in range(L // S3):
                t0 = seg * S3
                xr = Rot(xall)
                self.make_hT(S, lambda tt: xsrc[t0 + tt * 128:t0 + (tt + 1) * 128, :], gt, hT, NT3, tp, xr, hbpool, pstp,
                             lambda tt: [xsrc_region(t0 + tt * 128)])
                for c in range(5):
                    bt = btp.next()
                    P.dma(bt[:], self.scr["brT"][c * BR:(c + 1) * BR, t0:t0 + S3].rearrange("(k p) t -> p k t", p=128),
                          reads=[P.region("brT", c, t0 // 128 + i, f) for i in range(NT3) for f in range(8)], writes=[bt])
                    for f in range(16):
                        wg = wgp.next(); wb = wbp.next()
                        co = OFF["gates"] + c * D + f * 128
                        P.dma(wg[:], w_in[:, co:co + 128].rearrange("(k p) c -> p k c", p=128), writes=[wg], q="pool")
                        P.dma(wb[:], wbr[c][:, f * 128:(f + 1) * 128].rearrange("(k p) c -> p k c", p=128), writes=[wb], q="pool")
                        pg = psg.next(); pb = psb.next()
                        for k in range(16):
                            P.op("pe", lambda e, k=k, pg=pg, wg=wg: e.matmul(pg[:, 0:S3], lhsT=wg[:, k, :], rhs=hT[:, k, :], start=(k == 0), stop=(k == 15)),
                                 reads=[wg, hT], writes=[pg], inc=(k == 15))
                        for k in range(8):
                            P.op("pe", lambda e, k=k, pb=pb, wb=wb, bt=bt: e.matmul(pb[:, 0:S3], lhsT=wb[:, k, :], rhs=bt[:, k, :], start=(k == 0), stop=(k == 7)),
                                 reads=[wb, bt], writes=[pb], inc=(k == 7))
                        sg = sgp.next()
                        P.op("act", lambda e, sg=sg, pg=pg: e.activation(out=sg[:], in_=pg[:, 0:S3], func=AF.Sigmoid), reads=[pg], writes=[sg])
                        if c == 0:
                            P.op("dve", lambda e, sg=sg, pb=pb, f=f: e.tensor_tensor(out=merged[:, f, :], in0=pb[:, 0:S3], in1=sg[:], op=ALU.mult),
                                 reads=[sg, pb], writes=[merged])
                        else:
                            tm = tmp.next()
                            P.op("dve", lambda e, sg=sg, pb=pb, tm=tm: e.tensor_tensor(out=tm[:], in0=pb[:, 0:S3], in1=sg[:], op=ALU.mult),
                                 reads=[sg, pb], writes=[tm])
                            P.op("dve", lambda e, tm=tm, f=f: e.tensor_tensor(out=merged[:, f, :], in0=merged[:, f, :], in1=tm[:], op=ALU.add),
                                 reads=[tm, merged], writes=[merged])
                for f4 in range(4):
                    self.copy("act", mergedb[:, f4 * 4:(f4 + 1) * 4, :], merged[:, f4 * 4:(f4 + 1) * 4, :], [merged], [mergedb])
                for nb in range(8):
                    wo = wop.next()
                    P.dma(wo[:], wout[:, nb * 256:(nb + 1) * 256].rearrange("(k p) c -> p k c", p=128), writes=[wo], q="pool")
                    for tt in range(NT3):
                        po = pso.next()
                        for k in range(16):
                            P.op("pe", lambda e, k=k, po=po, wo=wo, tt=tt: e.matmul(po[:, 0:256], lhsT=mergedb[:, k, tt * 128:(tt + 1) * 128], rhs=wo[:, k, :],
                                                                                     start=(k == 0), stop=(k == 15)), reads=[mergedb, wo], writes=[po], inc=(k == 15))
                        xt = xall[tt]
                        P.op("dve", lambda e, po=po, xt=xt, nb=nb: e.tensor_tensor(out=xt[:, nb * 256:(nb + 1) * 256], in0=po[:, 0:256], in1=xt[:, nb * 256:(nb + 1) * 256], op=ALU.add),
                             reads=[po, xt], writes=[xt])
                for tt in range(NT3):
                    tok = t0 + tt * 128
                    xt = xall[tt]
                    if not last:
                        P.dma(self.scr["xres"][tok:tok + 128, :], xt[:], reads=[xt], writes=[P.region("xres", tok // 128)])
                    else:
                        yo = yop.next()
                        self.rmsnorm_tile(xt, fg, yo, tp)
                        P.dma(self.out[sn][tok:tok + 128, :], yo[:], reads=[yo], is_out=True)

    def zero_branch(self, c, L):
        P = self.P
        with Scope(P) as S:
            z = S.sb("z", [128, 8, 128], BF16)
            P.op("pool", lambda e: e.memset(z[:], 0.0), writes=[z])
            for n in range(L // 128):
                P.dma(self.scr["brT"][c * BR:(c + 1) * BR, n * 128:(n + 1) * 128].rearrange("(g d) t -> d g t", d=128), z[:], reads=[z],
                      writes=[P.region("brT", c, n, f) for f in range(8)])

    def build(self):
        P = self.P
        with Scope(P) as S0:
            self.consts(S0)
            for sn, L in self.seqs:
                for l in range(self.depth):
                    if l == 0:
                        xsrc = self.inp["x_" + sn]
                        xreg = lambda tok: P.region("xin", sn, tok // 128)
                    else:
                        xsrc = self.scr["xres"]
                        xreg = lambda tok: P.region("xres", tok // 128)
                    self.phase1(l, sn, L, xsrc, xreg)
                    for c in range(5):
                        if c not in self.branches:
                            self.zero_branch(c, L)
                    if 2 in self.branches:
                        self.sgu(l, L)
                    if 1 in self.branches:
                        self.s5(l, L)
                    if 3 in self.branches:
                        self.deltanet(l, L)
                    if 4 in self.branches:
                        self.xattn(l, L, sn)
                    if 0 in self.branches:
                        self.attention(l, L)
                    self.phase3(l, sn, L, xsrc, xreg, last=(l == self.depth - 1))
        P.emit()


def _sincos(k, S, x, W, name):
    P = k.P
    xx = S.sb(name + "_xx", [128, 2, W], F32)
    u = S.sb(name + "_u", [128, 2, W], F32)
    ki = S.sb(name + "_ki", [128, 2, W], I32)
    kf = S.sb(name + "_kf", [128, 2, W], F32)
    out = S.sb(name + "_sc", [128, 2, W], F32)
    P.op("dve", lambda e: e.tensor_copy(out=xx[:, 0, :], in_=x[:]), reads=[x], writes=[xx])
    P.op("dve", lambda e: e.tensor_scalar(out=xx[:, 1, :], in0=x[:], scalar1=math.pi / 2, scalar2=None, op0=ALU.add), reads=[x, xx], writes=[xx])
    _reduce_sin(k, xx, u, ki, kf, out)
    return out


def _reduce_sin(k, xx, u, ki, kf, out):
    P = k.P
    C1 = float(np.float32(TWO_PI))
    C2 = float(TWO_PI - np.float64(np.float32(TWO_PI)))
    P.op("dve", lambda e: e.tensor_scalar(out=u[:], in0=xx[:], scalar1=1.0 / TWO_PI, scalar2=None, op0=ALU.mult), reads=[xx], writes=[u])
    P.op("dve", lambda e: e.tensor_copy(out=ki[:], in_=u[:]), reads=[u], writes=[ki])
    P.op("dve", lambda e: e.tensor_copy(out=kf[:], in_=ki[:]), reads=[ki], writes=[kf])
    P.op("dve", lambda e: e.scalar_tensor_tensor(out=u[:], in0=kf[:], scalar=-C1, in1=xx[:], op0=ALU.mult, op1=ALU.add), reads=[kf, xx], writes=[u])
    P.op("dve", lambda e: e.scalar_tensor_tensor(out=u[:], in0=kf[:], scalar=-C2, in1=u[:], op0=ALU.mult, op1=ALU.add), reads=[kf, u], writes=[u])
    P.op("dve", lambda e: e.tensor_scalar(out=kf[:], in0=u[:], scalar1=math.pi, scalar2=-TWO_PI, op0=ALU.is_gt, op1=ALU.mult), reads=[u], writes=[kf])
    P.op("dve", lambda e: e.tensor_tensor(out=u[:], in0=u[:], in1=kf[:], op=ALU.add), reads=[u, kf], writes=[u])
    P.op("dve", lambda e: e.tensor_scalar(out=kf[:], in0=u[:], scalar1=-math.pi, scalar2=TWO_PI, op0=ALU.is_lt, op1=ALU.mult), reads=[u], writes=[kf])
    P.op("dve", lambda e: e.tensor_tensor(out=u[:], in0=u[:], in1=kf[:], op=ALU.add), reads=[u, kf], writes=[u])
    P.op("dve", lambda e: e.tensor_scalar(out=u[:], in0=u[:], scalar1=math.pi, scalar2=-math.pi, op0=ALU.min, op1=ALU.max), reads=[u], writes=[u])
    P.op("act", lambda e: e.activation(out=out[:], in_=u[:], func=AF.Sin), reads=[u], writes=[out])


def s5(self, l, L):
    P = self.P
    idf = self.c["identf"]
    Tc = min(512, L)
    NCH = L // Tc
    inp = self.inp
    with Scope(P) as S:
        pst = S.pspool("pst", 2, F32)
        psr = S.pspool("psr", 2, F32)
        psi = S.pspool("psi", 2, F32)
        psy = S.pspool("psy", 2, F32)
        prm = []
        for d in range(2):
            lam = {}
            for nm, src in (("lr", inp["ssm_lam_re"]), ("li", inp["ssm_lam_im"])):
                nat = S.sb("nat" + nm, [32, 128], F32)
                P.dma(nat[:], src[l, d].rearrange("(j a) n -> j (a n)", a=2), writes=[nat])
                pt = pst.next()
                P.op("pe", lambda e, pt=pt, nat=nat: e.transpose(out=pt[:, 0:32], in_=nat[:, :], identity=idf[0:32, 0:32]), reads=[nat, idf], writes=[pt])
                t = S.sb(nm + str(d), [128, 32], F32)
                self.copy("dve", t[:], pt[:, 0:32], [pt], [t])
                lam[nm] = t
            ldtb = S.sb("ldtb", [128, 64], F32)
            P.dma(ldtb[:], inp["ssm_log_dt"][l, d].partition_broadcast(128), writes=[ldtb])
            ldt = S.sb("ldt", [128, 32], F32)
            P.op("dve", lambda e, ldt=ldt, ldtb=ldtb: e.tensor_copy(out=ldt[0:64, :], in_=ldtb[0:64, 0:64:2]), reads=[ldtb], writes=[ldt])
            P.op("dve", lambda e, ldt=ldt, ldtb=ldtb: e.tensor_copy(out=ldt[64:128, :], in_=ldtb[64:128, 1:64:2]), reads=[ldtb, ldt], writes=[ldt])
            dt = S.sb("dt", [128, 32], F32)
            P.op("act", lambda e, dt=dt, ldt=ldt: e.activation(out=dt[:], in_=ldt[:], func=AF.Exp), reads=[ldt], writes=[dt])
            lr, li = lam["lr"], lam["li"]
            rho = S.sb("rho%d" % d, [128, 32], F32)
            th = S.sb("th%d" % d, [128, 32], F32)
            tmp = S.sb("tmpa", [128, 32], F32)
            P.op("dve", lambda e, tmp=tmp, lr=lr, dt=dt: e.tensor_tensor(out=tmp[:], in0=lr[:], in1=dt[:], op=ALU.mult), reads=[lr, dt], writes=[tmp])
            P.op("act", lambda e, tmp=tmp, rho=rho: e.activation(out=rho[:], in_=tmp[:], func=AF.Exp), reads=[tmp], writes=[rho])
            P.op("dve", lambda e, th=th, li=li, dt=dt: e.tensor_tensor(out=th[:], in0=li[:], in1=dt[:], op=ALU.mult), reads=[li, dt], writes=[th])
            sc = _sincos(self, S, th, 32, "p%d" % d)
            ar = S.sb("ar", [128, 32], F32); ai = S.sb("ai", [128, 32], F32); am1 = S.sb("am1", [128, 32], F32)
            P.op("dve", lambda e, ar=ar, rho=rho, sc=sc: e.tensor_tensor(out=ar[:], in0=rho[:], in1=sc[:, 1, :], op=ALU.mult), reads=[rho, sc], writes=[ar])
            P.op("dve", lambda e, ai=ai, rho=rho, sc=sc: e.tensor_tensor(out=ai[:], in0=rho[:], in1=sc[:, 0, :], op=ALU.mult), reads=[rho, sc], writes=[ai])
            P.op("dve", lambda e, ar=ar, am1=am1: e.tensor_scalar(out=am1[:], in0=ar[:], scalar1=-1.0, scalar2=None, op0=ALU.add), reads=[ar], writes=[am1])
            den = S.sb("den", [128, 32], F32); t2 = S.sb("t2", [128, 32], F32)
            P.op("dve", lambda e, den=den, lr=lr: e.tensor_tensor(out=den[:], in0=lr[:], in1=lr[:], op=ALU.mult), reads=[lr], writes=[den])
            P.op("dve", lambda e, t2=t2, li=li: e.tensor_tensor(out=t2[:], in0=li[:], in1=li[:], op=ALU.mult), reads=[li], writes=[t2])
            P.op("dve", lambda e, den=den, t2=t2: e.tensor_tensor(out=den[:], in0=den[:], in1=t2[:], op=ALU.add), reads=[den, t2], writes=[den])
            rden = S.sb("rden", [128, 32], F32)
            P.op("dve", lambda e, den=den, rden=rden: e.reciprocal(out=rden[:], in_=den[:]), reads=[den], writes=[rden])
            fr = S.sb("fr%d" % d, [128, 32], F32); fi = S.sb("fi%d" % d, [128, 32], F32); nfi = S.sb("nfi%d" % d, [128, 32], F32)
            P.op("dve", lambda e, t2=t2, am1=am1, lr=lr: e.tensor_tensor(out=t2[:], in0=am1[:], in1=lr[:], op=ALU.mult), reads=[am1, lr, t2], writes=[t2])
            P.op("dve", lambda e, tmp=tmp, ai=ai, li=li: e.tensor_tensor(out=tmp[:], in0=ai[:], in1=li[:], op=ALU.mult), reads=[ai, li, tmp], writes=[tmp])
            P.op("dve", lambda e, t2=t2, tmp=tmp: e.tensor_tensor(out=t2[:], in0=t2[:], in1=tmp[:], op=ALU.add), reads=[t2, tmp], writes=[t2])
            P.op("dve", lambda e, t2=t2, fr=fr, rden=rden: e.tensor_tensor(out=fr[:], in0=t2[:], in1=rden[:], op=ALU.mult), reads=[t2, rden], writes=[fr])
            P.op("dve", lambda e, t2=t2, ai=ai, lr=lr: e.tensor_tensor(out=t2[:], in0=ai[:], in1=lr[:], op=ALU.mult), reads=[ai, lr, t2], writes=[t2])
            P.op("dve", lambda e, tmp=tmp, am1=am1, li=li: e.tensor_tensor(out=tmp[:], in0=am1[:], in1=li[:], op=ALU.mult), reads=[am1, li, tmp], writes=[tmp])
            P.op("dve", lambda e, t2=t2, tmp=tmp: e.tensor_tensor(out=t2[:], in0=t2[:], in1=tmp[:], op=ALU.subtract), reads=[t2, tmp], writes=[t2])
            P.op("dve", lambda e, t2=t2, fi=fi, rden=rden: e.tensor_tensor(out=fi[:], in0=t2[:], in1=rden[:], op=ALU.mult), reads=[t2, rden], writes=[fi])
            P.op("dve", lambda e, nfi=nfi, fi=fi: e.tensor_scalar(out=nfi[:], in0=fi[:], scalar1=-1.0, scalar2=None, op0=ALU.mult), reads=[fi], writes=[nfi])
            prm.append(dict(rho=rho, th=th, fr=fr, fi=fi, nfi=nfi))
        iot = S.sb("iot", [128, Tc], F32)
        P.op("pool", lambda e: e.iota(iot[:], pattern=[[1, Tc]], base=1, channel_multiplier=0, allow_small_or_imprecise_dtypes=True), writes=[iot])
        dcol = S.sb("dcol", [128, 8], F32)
        P.dma(dcol[:], inp["ssm_d"][l].rearrange("(o p) -> p o", p=128), writes=[dcol], slow=True)
        Bpad = [[S.sb("Bpad%d%d" % (j, ri), [128, 128], F32) for ri in range(2)] for j in range(4)]
        Cpad = [[S.sb("Cpad%d%d" % (j, ri), [128, 128], F32) for ri in range(2)] for j in range(4)]
        for j in range(4):
            for ri in range(2):
                for t in (Bpad[j][ri], Cpad[j][ri]):
                    P.op("pool", lambda e, t=t: e.memset(t[:], 0.0), writes=[t])
        BT = [[[S.sb("BT%d%d%d" % (j, d, ri), [128, 128], F32) for ri in range(2)] for d in range(2)] for j in range(4)]
        CT = [[[S.sb("CT%d%d%d" % (j, d, ri), [128, 128], F32) for ri in range(2)] for d in range(2)] for j in range(4)]
        tabs = [[S.sb("tab%d%d" % (j, d), [128, 2, Tc], F32) for d in range(2)] for j in range(4)]
        rhoT = [[S.sb("rhoT%d%d" % (j, d), [128, Tc], F32) for d in range(2)] for j in range(4)]
        state = [[S.sb("state%d%d" % (j, d), [128, 2], F32) for d in range(2)] for j in range(4)]
        tb_xx = S.sb("tb_xx", [128, 2, Tc], F32); tb_u = S.sb("tb_u", [128, 2, Tc], F32)
        tb_ki = S.sb("tb_ki", [128, 2, Tc], I32); tb_kf = S.sb("tb_kf", [128, 2, Tc], F32)
        bnat = S.pool("bnat", [128, 2, 16], F32, 2)
        bw = S.pool("bw", [128, 2, 16], F32, 2)
        yacc = S.sb("yacc", [128, L], F32)
        up = S.pool("u", [128, Tc], F32, 3)
        wk = {n: S.pool(n, [128, Tc], F32, 2) for n in ("t1", "t2", "re", "t3", "t4", "im", "gr", "gi", "h1", "h2", "hr", "h3", "h4", "hi")}
        gp = {n: S.pool("g" + n, [128, Tc], F32, 1) for n in ("y", "y2", "p", "in", "s")}
        syb = S.pool("syb", [128, Tc], BF16, 2)
        for o in range(8):
            for d in range(2):
                pr = prm[d]
                for j in range(4):
                    J = 4 * o + j
                    bn = bnat.next()
                    P.dma(bn[:, 0, :], inp["ssm_b_re"][l, d].rearrange("g n p -> (g n) p")[J * 128:(J + 1) * 128, :], writes=[bn])
                    P.dma(bn[:, 1, :], inp["ssm_b_im"][l, d].rearrange("g n p -> (g n) p")[J * 128:(J + 1) * 128, :], writes=[bn])
                    b2 = bw.next()
                    frc, fic, nfic = pr["fr"][:, J:J + 1], pr["fi"][:, J:J + 1], pr["nfi"][:, J:J + 1]
                    P.op("dve", lambda e, b2=b2, bn=bn, frc=frc: e.tensor_scalar(out=b2[:], in0=bn[:], scalar1=frc, scalar2=None, op0=ALU.mult), reads=[bn, pr["fr"]], writes=[b2])
                    for hh, c0 in ((0, (2 * j) * 16), (1, (2 * j + 1) * 16)):
                        ps_ = slice(hh * 64, (hh + 1) * 64)
                        P.op("dve", lambda e, b2=b2, bn=bn, ps_=ps_, c0=c0, j=j, sc_=pr["nfi"][ps_, J:J + 1]: e.scalar_tensor_tensor(
                            out=Bpad[j][0][ps_, c0:c0 + 16], in0=bn[ps_, 1, :], scalar=sc_, in1=b2[ps_, 0, :], op0=ALU.mult, op1=ALU.add),
                            reads=[bn, b2, pr["nfi"]], writes=[Bpad[j][0]])
                        P.op("dve", lambda e, b2=b2, bn=bn, ps_=ps_, c0=c0, j=j, sc_=pr["fi"][ps_, J:J + 1]: e.scalar_tensor_tensor(
                            out=Bpad[j][1][ps_, c0:c0 + 16], in0=bn[ps_, 0, :], scalar=sc_, in1=b2[ps_, 1, :], op0=ALU.mult, op1=ALU.add),
                            reads=[bn, b2, pr["fi"]], writes=[Bpad[j][1]])
                    for ri in range(2):
                        pt = pst.next()
                        P.op("pe", lambda e, pt=pt, j=j, ri=ri: e.transpose(out=pt[:, 0:128], in_=Bpad[j][ri][:, :], identity=idf[:]), reads=[Bpad[j][ri], idf], writes=[pt])
                        self.copy("act", BT[j][d][ri][:], pt[:, 0:128], [pt], [BT[j][d][ri]])
                    for ri, src in ((0, inp["ssm_c_re"]), (1, inp["ssm_c_im"])):
                        g0 = 8 * o + 2 * j
                        P.dma(Cpad[j][ri][(2 * j) * 16:(2 * j) * 16 + 16, 0:64], src[l, d, g0], writes=[Cpad[j][ri]])
                        P.dma(Cpad[j][ri][(2 * j + 1) * 16:(2 * j + 1) * 16 + 16, 64:128], src[l, d, g0 + 1], writes=[Cpad[j][ri]])
                        pt = pst.next()
                        P.op("pe", lambda e, pt=pt, j=j, ri=ri: e.transpose(out=pt[:, 0:128], in_=Cpad[j][ri][:, :], identity=idf[:]), reads=[Cpad[j][ri], idf], writes=[pt])
                        ct = CT[j][d][ri]
                        if ri == 0:
                            self.copy("act", ct[:], pt[:, 0:128], [pt], [ct])
                        else:
                            P.op("act", lambda e, ct=ct, pt=pt: e.activation(out=ct[:], in_=pt[:, 0:128], func=AF.Copy, scale=-1.0), reads=[pt], writes=[ct])
                    P.op("dve", lambda e, sc_=pr["th"][:, J:J + 1]: e.tensor_scalar(out=tb_xx[:, 0, :], in0=iot[:], scalar1=sc_, scalar2=None, op0=ALU.mult),
                         reads=[iot, pr["th"]], writes=[tb_xx])
                    P.op("dve", lambda e: e.tensor_scalar(out=tb_xx[:, 1, :], in0=tb_xx[:, 0, :], scalar1=math.pi / 2, scalar2=None, op0=ALU.add),
                         reads=[tb_xx], writes=[tb_xx])
                    _reduce_sin(self, tb_xx, tb_u, tb_ki, tb_kf, tabs[j][d])
                    rt = rhoT[j][d]
                    P.op("dve", lambda e, rt=rt, sc_=pr["rho"][:, J:J + 1]: e.tensor_scalar(out=rt[:], in0=iot[:], scalar1=0.0, scalar2=sc_, op0=ALU.mult, op1=ALU.add),
                         reads=[iot, pr["rho"]], writes=[rt])
                    P.op("pool", lambda e, j=j, d=d: e.memset(state[j][d][:], 0.0), writes=[state[j][d]])
            for d in range(2):
                order = range(NCH) if d == 0 else range(NCH - 1, -1, -1)
                for ch in order:
                    t0 = ch * Tc
                    u = up.next()
                    seg = t0 // min(L, 2048)
                    P.dma(u[:], self.scr["su"][o * 128:(o + 1) * 128, t0:t0 + Tc], reads=[P.region("su", o, seg)], writes=[u])
                    uap = u[:, :] if d == 0 else u[:, ::-1]
                    yp = psy.next()
                    for j in range(4):
                        tab = tabs[j][d]; rt = rhoT[j][d]; st = state[j][d]
                        sn_, cs_ = tab[:, 0, :], tab[:, 1, :]
                        pr_, pi_ = psr.next(), psi.next()
                        P.op("pe", lambda e, pr_=pr_, j=j, d=d, uap=uap: e.matmul(pr_[:, 0:Tc], lhsT=BT[j][d][0][:], rhs=uap, start=True, stop=True), reads=[BT[j][d][0], u], writes=[pr_])
                        P.op("pe", lambda e, pi_=pi_, j=j, d=d, uap=uap: e.matmul(pi_[:, 0:Tc], lhsT=BT[j][d][1][:], rhs=uap, start=True, stop=True), reads=[BT[j][d][1], u], writes=[pi_])
                        w = {n: wk[n].next() for n in wk}

                        def tt(out, a, b, op, ra, rb, eng="dve"):
                            P.op(eng, lambda e: e.tensor_tensor(out=out[:], in0=a, in1=b, op=op), reads=ra + rb, writes=[out])
                        tt(w["t1"], pr_[:, 0:Tc], cs_, ALU.mult, [pr_], [tab])
                        tt(w["t2"], pi_[:, 0:Tc], sn_, ALU.mult, [pi_], [tab])
                        tt(w["re"], w["t1"][:], w["t2"][:], ALU.add, [w["t1"]], [w["t2"]])
                        tt(w["t3"], pi_[:, 0:Tc], cs_, ALU.mult, [pi_], [tab])
                        tt(w["t4"], pr_[:, 0:Tc], sn_, ALU.mult, [pr_], [tab])
                        tt(w["im"], w["t3"][:], w["t4"][:], ALU.subtract, [w["t3"]], [w["t4"]])
                        P.op("dve", lambda e, w=w, rt=rt, st=st: e.tensor_tensor_scan(out=w["gr"][:], data0=rt[:], data1=w["re"][:], initial=st[:, 0:1], op0=ALU.mult, op1=ALU.add),
                             reads=[rt, w["re"], st], writes=[w["gr"]])
                        P.op("dve", lambda e, w=w, rt=rt, st=st: e.tensor_tensor_scan(out=w["gi"][:], data0=rt[:], data1=w["im"][:], initial=st[:, 1:2], op0=ALU.mult, op1=ALU.add),
                             reads=[rt, w["im"], st], writes=[w["gi"]])
                        tt(w["h1"], w["gr"][:], cs_, ALU.mult, [w["gr"]], [tab])
                        tt(w["h2"], w["gi"][:], sn_, ALU.mult, [w["gi"]], [tab])
                        tt(w["hr"], w["h1"][:], w["h2"][:], ALU.subtract, [w["h1"]], [w["h2"]])
                        tt(w["h3"], w["gr"][:], sn_, ALU.mult, [w["gr"]], [tab])
                        tt(w["h4"], w["gi"][:], cs_, ALU.mult, [w["gi"]], [tab])
                        tt(w["hi"], w["h3"][:], w["h4"][:], ALU.add, [w["h3"]], [w["h4"]])
                        P.op("act", lambda e, w=w, st=st: e.activation(out=st[:, 0:1], in_=w["hr"][:, Tc - 1:Tc], func=AF.Copy), reads=[w["hr"]], writes=[st])
                        P.op("act", lambda e, w=w, st=st: e.activation(out=st[:, 1:2], in_=w["hi"][:, Tc - 1:Tc], func=AF.Copy), reads=[w["hi"], st], writes=[st])
                        if self.debug and d == 0 and o < 2:
                            Jd = 4 * o + j
                            P.dma(self.scr["dbg2"][Jd * 128:(Jd + 1) * 128, t0:t0 + Tc], w["re"][:, :], reads=[w["re"]])
                        hra = w["hr"][:, :] if d == 0 else w["hr"][:, ::-1]
                        hia = w["hi"][:, :] if d == 0 else w["hi"][:, ::-1]
                        P.op("pe", lambda e, yp=yp, j=j, d=d, hra=hra: e.matmul(yp[:, 0:Tc], lhsT=CT[j][d][0][:], rhs=hra, start=(j == 0), stop=False),
                             reads=[CT[j][d][0], w["hr"]], writes=[yp], inc=False)
                        P.op("pe", lambda e, yp=yp, j=j, d=d, hia=hia: e.matmul(yp[:, 0:Tc], lhsT=CT[j][d][1][:], rhs=hia, start=False, stop=(j == 3)),
                             reads=[CT[j][d][1], w["hi"]], writes=[yp], inc=True)
                    if d == 0:
                        self.copy("act", yacc[:, t0:t0 + Tc], yp[:, 0:Tc], [yp], [yacc])
                        pass
                    else:
                        P.op("dve", lambda e, yp=yp, t0=t0: e.tensor_tensor(out=yacc[:, t0:t0 + Tc], in0=yp[:, 0:Tc], in1=yacc[:, t0:t0 + Tc], op=ALU.add), reads=[yp, yacc], writes=[yacc])
            if self.debug:
                P.dma(self.scr["dbg"][o * 128:(o + 1) * 128, 0:L], yacc[:, :], reads=[yacc])
            for ch in range(NCH):
                t0 = ch * Tc
                u = up.next()
                seg = t0 // min(L, 2048)
                P.dma(u[:], self.scr["su"][o * 128:(o + 1) * 128, t0:t0 + Tc], reads=[P.region("su", o, seg)], writes=[u])
                y = gp["y"].next(); y2 = gp["y2"].next(); pp = gp["p"].next(); inn = gp["in"].next(); sg = gp["s"].next(); ob = syb.next()
                P.op("dve", lambda e, u=u, y=y, t0=t0, o=o: e.scalar_tensor_tensor(out=y[:], in0=u[:], scalar=dcol[:, o:o + 1], in1=yacc[:, t0:t0 + Tc], op0=ALU.mult, op1=ALU.add),
                     reads=[u, dcol, yacc], writes=[y])
                P.op("dve", lambda e, y=y, y2=y2: e.tensor_tensor(out=y2[:], in0=y[:], in1=y[:], op=ALU.mult), reads=[y], writes=[y2])
                P.op("dve", lambda e, y2=y2, pp=pp: e.tensor_scalar(out=pp[:], in0=y2[:], scalar1=0.044715, scalar2=1.0, op0=ALU.mult, op1=ALU.add), reads=[y2], writes=[pp])
                P.op("dve", lambda e, pp=pp, y=y, inn=inn: e.tensor_tensor(out=inn[:], in0=pp[:], in1=y[:], op=ALU.mult), reads=[pp, y], writes=[inn])
                P.op("act", lambda e, inn=inn, sg=sg: e.activation(out=sg[:], in_=inn[:], func=AF.Sigmoid, scale=2.0 * math.sqrt(2.0 / math.pi)), reads=[inn], writes=[sg])
                P.op("dve", lambda e, sg=sg, y=y, ob=ob: e.tensor_tensor(out=ob[:], in0=sg[:], in1=y[:], op=ALU.mult), reads=[sg, y], writes=[ob])
                P.dma(self.scr["syT"][o * 128:(o + 1) * 128, t0:t0 + Tc], ob[:], reads=[ob], writes=[P.region("syT", o, ch)])
    with Scope(P) as S:
        gw = S.sb("gw", [128, 8, BR], BF16)
        gb = S.sb("gb", [128, 8], F32)
        P.dma(gw[:], inp["ssm_glu_w"][l].rearrange("(k p) c -> p k c", p=128), writes=[gw], q="pool")
        P.dma(gb[:], inp["ssm_glu_b"][l].rearrange("(f p) -> p f", p=128), writes=[gb], slow=True)
        syp = S.pool("sy", [128, 8, Tc], BF16, 2)
        szp = S.pool("sz", [128, Tc], F32, 2)
        szs = S.pool("szs", [128, Tc], F32, 2)
        sgp = S.pool("sg", [128, Tc], F32, 2)
        t1p = S.pool("t1", [128, Tc], F32, 2)
        obp = S.pool("ob", [128, Tc], BF16, 2)
        psm = S.pspool("psm", 3, F32)
        for ch in range(NCH):
            t0 = ch * Tc
            sy = syp.next()
            P.dma(sy[:], self.scr["syT"][:, t0:t0 + Tc].rearrange("(k p) t -> p k t", p=128), reads=[P.region("syT", o, ch) for o in range(8)], writes=[sy])
            seg = t0 // min(L, 2048)
            for f in range(8):
                ps = psm.next()
                for k_ in range(8):
                    P.op("pe", lambda e, ps=ps, k_=k_, f=f, sy=sy: e.matmul(ps[:, 0:Tc], lhsT=gw[:, k_, f * 128:(f + 1) * 128], rhs=sy[:, k_, :], start=(k_ == 0), stop=(k_ == 7)),
                         reads=[gw, sy], writes=[ps], inc=(k_ == 7))
                sz = szp.next(); zs = szs.next(); sg = sgp.next(); t1 = t1p.next(); ob = obp.next()
                P.dma(sz[:], self.scr["sz"][f * 128:(f + 1) * 128, t0:t0 + Tc], reads=[P.region("sz", f, seg)], writes=[sz])
                P.op("act", lambda e, sg=sg, ps=ps, f=f: e.activation(out=sg[:], in_=ps[:, 0:Tc], func=AF.Sigmoid, bias=gb[:, f:f + 1]), reads=[ps, gb], writes=[sg])
                P.op("act", lambda e, sz=sz, zs=zs: e.activation(out=zs[:], in_=sz[:], func=AF.Silu), reads=[sz], writes=[zs])
                P.op("dve", lambda e, sg=sg, sy=sy, t1=t1, f=f: e.tensor_tensor(out=t1[:], in0=sg[:], in1=sy[:, f, :], op=ALU.mult), reads=[sg, sy], writes=[t1])
                P.op("dve", lambda e, t1=t1, zs=zs, ob=ob: e.tensor_tensor(out=ob[:], in0=t1[:], in1=zs[:], op=ALU.mult), reads=[t1, zs], writes=[ob])
                for q4 in range(Tc // 128):
                    pass
                P.dma(self.scr["brT"][BR + f * 128:BR + (f + 1) * 128, t0:t0 + Tc], ob[:], reads=[ob],
                      writes=[P.region("brT", 1, t0 // 128 + i, f) for i in range(Tc // 128)])


K.s5 = s5


BRANCHES = (0, 1, 2, 3, 4)
_CACHE = {}


def kernel(**inputs):
    inputs = {k_: np.ascontiguousarray(np.asarray(v)) for k_, v in inputs.items()}
    nc = bass.Bass("TRN2", target_bir_lowering=False)
    kk = K(nc, [("p", 2048), ("s", 8192)], depth=2, debug=False, branches=BRANCHES)
    kk.build()
    wnames = [n for n in inputs if n not in ("x_prompt", "x_sample", "mem_prompt", "mem_sample")]
    in_maps = []
    for c in range(8):
        m = {n: inputs[n] for n in wnames}
        m["x_p"] = inputs["x_prompt"][c]
        m["mem_p"] = inputs["mem_prompt"][c]
        m["x_s"] = inputs["x_sample"][c // 4]
        m["mem_s"] = inputs["mem_sample"][c // 4]
        in_maps.append(m)
    res = run_bass_kernel_spmd(nc, in_maps, core_ids=list(range(8)))
    y_p = np.stack([np.asarray(res.results[c]["y_p"], dtype=np.float32) for c in range(8)])
    y_s = np.stack([np.asarray(res.results[c]["y_s"], dtype=np.float32) for c in (0, 4)])
    return (y_p, y_s)


def _tt(self, out, a, b, op, reads, writes, eng="dve"):
    self.P.op(eng, lambda e: e.tensor_tensor(out=out, in0=a, in1=b, op=op), reads=reads, writes=writes)


def _ts(self, out, a, s1, s2, op0, op1, reads, writes, eng="dve"):
    if s2 is None:
        self.P.op(eng, lambda e: e.tensor_scalar(out=out, in0=a, scalar1=s1, scalar2=None, op0=op0), reads=reads, writes=writes)
    else:
        self.P.op(eng, lambda e: e.tensor_scalar(out=out, in0=a, scalar1=s1, scalar2=s2, op0=op0, op1=op1), reads=reads, writes=writes)


def _stt(self, out, a, sc, b, op0, op1, reads, writes):
    self.P.op("dve", lambda e: e.scalar_tensor_tensor(out=out, in0=a, scalar=sc, in1=b, op0=op0, op1=op1), reads=reads, writes=writes)


def _mm(self, out, lhsT, rhs, reads, writes, start=True, stop=True, inc=True):
    self.P.op("pe", lambda e: e.matmul(out, lhsT=lhsT, rhs=rhs, start=start, stop=stop), reads=reads, writes=writes, inc=inc)


def _tr(self, out, in_, ident, reads, writes, inc=True):
    self.P.op("pe", lambda e: e.transpose(out=out, in_=in_, identity=ident), reads=reads, writes=writes, inc=inc)


def _act(self, out, in_, func, reads, writes, scale=1.0, bias=0.0):
    self.P.op("act", lambda e: e.activation(out=out, in_=in_, func=func, scale=scale, bias=bias), reads=reads, writes=writes)


def _cp(self, out, in_, reads, writes, eng=None):
    eng = eng or self.evac_eng()
    if eng == "act":
        self.P.op("act", lambda e: e.activation(out=out, in_=in_, func=AF.Copy), reads=reads, writes=writes)
    else:
        self.P.op(eng, lambda e: e.tensor_copy(out=out, in_=in_), reads=reads, writes=writes)


K.tt, K.ts, K.stt, K.mm, K.tr, K.act, K.cp = _tt, _ts, _stt, _mm, _tr, _act, _cp


def deltanet(self, l, L):
    P = self.P
    inp = self.inp
    idf = self.c["identf"]
    onesf = self.c["onesf"]
    NCHK = L // 128
    SEG = min(L, 2048)
    with Scope(P) as S:
        LOW = S.sb("LOW", [128, 128], F32); UPP = S.sb("UPP", [128, 128], F32)
        SLOW = S.sb("SLOW", [128, 128], F32); SUPP = S.sb("SUPP", [128, 128], F32)
        for t, (cm, st, op) in ((LOW, (1, -1, ALU.is_ge)), (UPP, (-1, 1, ALU.is_ge)), (SLOW, (1, -1, ALU.is_gt)), (SUPP, (-1, 1, ALU.is_gt))):
            P.op("pool", lambda e, t=t: e.memset(t[:], 1.0), writes=[t])
            P.op("pool", lambda e, t=t, cm=cm, st=st, op=op: e.affine_select(out=t[:], in_=t[:], pattern=[[st, 128]], compare_op=op, fill=0.0, base=0, channel_multiplier=cm),
                 reads=[t], writes=[t])
        cw = S.sb("cw", [128, 3, 24], F32)
        for j_ in range(3):
            P.dma(cw[:, j_, :], inp["dn_conv_w"][l, j_].rearrange("(k p) -> p k", p=128), writes=[cw], slow=True)
        alb = S.sb("alb", [128, 16], F32); dtb = S.sb("dtb", [128, 16], F32); nea = S.sb("nea", [128, 16], F32)
        P.dma(alb[:], inp["dn_a_log"][l].rearrange("a h -> (a h)").partition_broadcast(128), writes=[alb])
        P.dma(dtb[:], inp["dn_dt_bias"][l].rearrange("a h -> (a h)").partition_broadcast(128), writes=[dtb])
        self.act(nea[:], alb[:], AF.Exp, [alb], [nea])
        self.ts(nea[:], nea[:], -1.0, None, ALU.mult, None, [nea], [nea])
        ng = S.sb("ng", [128, 128], F32)
        P.dma(ng[:], inp["dn_norm_g"][l].partition_broadcast(128), writes=[ng])
        pp = S.pspool("pp", 7, F32)
        xin = S.pool("xin", [128, 130], F32, 3)
        cvp = S.pool("cv", [128, 128], F32, 3)
        sqp = S.pool("sq", [128, 128], F32, 2)
        rsp = S.pool("rsq", [128, 128], F32, 2)
        qTp = S.pool("qT", [128, 128], F32, 2); kTp = S.pool("kT", [128, 128], F32, 2)
        vTp = S.pool("vT", [128, 128], F32, 2)
        ktp = S.pool("ktm", [128, 128], F32, 2); vtp = S.pool("vtm", [128, 128], F32, 2)
        dbp = S.pool("dba", [128, 32], F32, 2)
        sm = {n: S.pool(n, [128, 16], F32, 2) for n in ("beta", "nbeta", "x", "ax", "e", "ln", "mx", "g", "gcc", "gtot", "egc", "ekd", "glast", "cb", "dif")}
        wkn = ("gU", "gcb", "egcb", "dm", "dec", "dst", "X", "Y", "X2", "Y2", "TT", "vb", "kbg", "u", "wT", "att", "attT", "qgT", "kd", "vnew", "osb", "of", "osum", "junk", "on", "onT", "zt", "zs")
        wk = {n: S.pool(n, [128, 128], F32, 2) for n in wkn}
        obp = S.pool("obf", [128, 128], BF16, 2)
        ssp = S.pool("ss1", [128, 1], F32, 2); rs1 = S.pool("rs1", [128, 1], F32, 2); rstd1 = S.pool("rstd1", [128, 1], F32, 2)
        Sst = [[S.sb("S%d%d" % (h, d), [128, 128], F32) for d in range(2)] for h in range(8)]
        ofs = self.scr["dn_of"]
        for d in range(2):
            CUM, INCL, STRICT = (UPP, LOW, SLOW) if d == 0 else (LOW, UPP, SUPP)
            for h in range(8):
                P.op("pool", lambda e, t=Sst[h][d]: e.memset(t[:], 0.0), writes=[Sst[h][d]])
            order = range(NCHK) if d == 0 else range(NCHK - 1, -1, -1)
            for n in order:
                tok = n * 128
                seg = tok // SEG
                db = dbp.next()
                P.dma(db[:], self.scr["dba"][tok:tok + 128, :], reads=[P.region("dba", "t", n, 0)], writes=[db])
                s = {k_: sm[k_].next() for k_ in sm}
                self.act(s["beta"][:], db[:, 0:16], AF.Sigmoid, [db], [s["beta"]])
                self.ts(s["nbeta"][:], s["beta"][:], -1.0, None, ALU.mult, None, [s["beta"]], [s["nbeta"]])
                self.tt(s["x"][:], db[:, 16:32], dtb[:], ALU.add, [db, dtb], [s["x"]])
                self.ts(s["mx"][:], s["x"][:], 0.0, None, ALU.max, None, [s["x"]], [s["mx"]])
                self.stt(s["ax"][:], s["mx"][:], -2.0, s["x"][:], ALU.mult, ALU.add, [s["mx"], s["x"]], [s["ax"]])
                self.act(s["e"][:], s["ax"][:], AF.Exp, [s["ax"]], [s["e"]])
                self.act(s["ln"][:], s["e"][:], AF.Ln, [s["e"]], [s["ln"]], bias=1.0)
                self.tt(s["mx"][:], s["mx"][:], s["ln"][:], ALU.add, [s["mx"], s["ln"]], [s["mx"]])
                self.tt(s["g"][:], s["mx"][:], nea[:], ALU.mult, [s["mx"], nea], [s["g"]])
                pg = pp.next()
                self.mm(pg[:, 0:8], CUM[:], s["g"][:, d * 8:(d + 1) * 8], [CUM, s["g"]], [pg])
                self.mm(pg[:, 16:24], onesf[:], s["g"][:, d * 8:(d + 1) * 8], [onesf, s["g"]], [pg])
                self.cp(s["gcc"][:, 0:8], pg[:, 0:8], [pg], [s["gcc"]], eng="dve")
                self.cp(s["gtot"][:, 0:8], pg[:, 16:24], [pg], [s["gtot"]], eng="dve")
                self.act(s["egc"][:, 0:8], s["gcc"][:, 0:8], AF.Exp, [s["gcc"]], [s["egc"]])
                self.tt(s["dif"][:, 0:8], s["gtot"][:, 0:8], s["gcc"][:, 0:8], ALU.subtract, [s["gtot"], s["gcc"]], [s["dif"]])
                self.act(s["ekd"][:, 0:8], s["dif"][:, 0:8], AF.Exp, [s["dif"]], [s["ekd"]])
                self.act(s["glast"][:, 0:8], s["gtot"][:, 0:8], AF.Exp, [s["gtot"]], [s["glast"]])
                self.tt(s["cb"][:, 0:8], s["beta"][:, d * 8:(d + 1) * 8], s["egc"][:, 0:8], ALU.mult, [s["beta"], s["egc"]], [s["cb"]])
                for h in range(8):
                    w = {k_: wk[k_].next() for k_ in wk}
                    outs = {}
                    for which, tile_i in (("q", h), ("k", 8 + h), ("v", 16 + h)):
                        xi = xin.next()
                        lo = max(tok - 1, 0); hi = min(tok + 129, L)
                        c_lo = lo - (tok - 1); c_hi = 130 - ((tok + 129) - hi)
                        if c_lo > 0 or c_hi < 130:
                            P.op("pool", lambda e, xi=xi: e.memset(xi[:], 0.0), writes=[xi])
                        rds = [P.region("dqkv", tile_i, sg_) for sg_ in sorted(set([lo // SEG, (hi - 1) // SEG]))]
                        P.dma(xi[:, c_lo:c_hi], self.scr["dqkv"][tile_i * 128:(tile_i + 1) * 128, lo:hi], reads=rds, writes=[xi])
                        cv = cvp.next()
                        self.ts(cv[:], xi[:, 0:128], cw[:, 0, tile_i:tile_i + 1], None, ALU.mult, None, [xi, cw], [cv])
                        self.stt(cv[:], xi[:, 1:129], cw[:, 1, tile_i:tile_i + 1], cv[:], ALU.mult, ALU.add, [xi, cw, cv], [cv])
                        self.stt(cv[:], xi[:, 2:130], cw[:, 2, tile_i:tile_i + 1], cv[:], ALU.mult, ALU.add, [xi, cw, cv], [cv])
                        if which == "v":
                            vT = vTp.next()
                            self.act(vT[:], cv[:], AF.Silu, [cv], [vT])
                            outs["vT"] = vT
                        else:
                            self.act(cv[:], cv[:], AF.Silu, [cv], [cv])
                            sq = sqp.next()
                            self.tt(sq[:], cv[:], cv[:], ALU.mult, [cv], [sq])
                            pn = pp.next()
                            self.mm(pn[:, 0:128], onesf[:], sq[:], [onesf, sq], [pn])
                            rq = rsp.next()
                            self.act(rq[:], pn[:, 0:128], AF.Sqrt, [pn], [rq], bias=EPS)
                            P.op("dve", lambda e, rq=rq: e.reciprocal(out=rq[:], in_=rq[:]), reads=[rq], writes=[rq])
                            dst = (qTp if which == "q" else kTp).next()
                            if which == "q":
                                self.stt(dst[:], cv[:], 128.0 ** -0.5, rq[:], ALU.mult, ALU.mult, [cv, rq], [dst])
                            else:
                                self.tt(dst[:], cv[:], rq[:], ALU.mult, [cv, rq], [dst])
                            outs[which + "T"] = dst
                    qT, kT, vT = outs["qT"], outs["kT"], outs["vT"]
                    ktm = ktp.next(); vtm = vtp.next()
                    pt = pp.next()
                    self.tr(pt[:, 0:128], kT[:], idf[:], [kT, idf], [pt])
                    self.cp(ktm[:], pt[:, 0:128], [pt], [ktm])
                    pt = pp.next()
                    self.tr(pt[:, 0:128], vT[:], idf[:], [vT, idf], [pt])
                    self.cp(vtm[:], pt[:, 0:128], [pt], [vtm])
                    gcol = s["g"][:, d * 8 + h:d * 8 + h + 1]
                    self.ts(w["gU"][:], CUM[:], gcol, None, ALU.mult, None, [CUM, s["g"]], [w["gU"]])
                    pgb = pp.next()
                    self.mm(pgb[:, 0:128], onesf[:], w["gU"][:], [onesf, w["gU"]], [pgb])
                    self.act(w["egcb"][:], pgb[:, 0:128], AF.Exp, [pgb], [w["egcb"]])
                    self.ts(w["dm"][:], pgb[:, 0:128], -1.0, s["gcc"][:, h:h + 1], ALU.mult, ALU.add, [pgb, s["gcc"]], [w["dm"]])
                    self.ts(w["dm"][:], w["dm"][:], 0.0, None, ALU.min, None, [w["dm"]], [w["dm"]])
                    self.act(w["dm"][:], w["dm"][:], AF.Exp, [w["dm"]], [w["dm"]])
                    self.tt(w["dec"][:], w["dm"][:], INCL[:], ALU.mult, [w["dm"], INCL], [w["dec"]])
                    self.tt(w["dst"][:], w["dm"][:], STRICT[:], ALU.mult, [w["dm"], STRICT], [w["dst"]])
                    pG = pp.next()
                    self.mm(pG[:, 0:128], kT[:], kT[:], [kT], [pG])
                    self.stt(w["X"][:], pG[:, 0:128], s["nbeta"][:, d * 8 + h:d * 8 + h + 1], w["dst"][:], ALU.mult, ALU.mult, [pG, s["nbeta"], w["dst"]], [w["X"]])
                    pt = pp.next()
                    self.tr(pt[:, 0:128], w["X"][:], idf[:], [w["X"], idf], [pt])
                    self.cp(w["Y"][:], pt[:, 0:128], [pt], [w["Y"]])
                    self.tt(w["TT"][:], w["Y"][:], idf[:], ALU.add, [w["Y"], idf], [w["TT"]])
                    X, Y, X2, Y2 = w["X"], w["Y"], w["X2"], w["Y2"]
                    for kk in range(1, 7):
                        pX = pp.next()
                        self.mm(pX[:, 0:128], Y[:], X[:], [X, Y], [pX])
                        self.cp(X2[:], pX[:, 0:128], [pX], [X2])
                        if kk < 6:
                            pY = pp.next()
                            self.mm(pY[:, 0:128], X[:], Y[:], [X, Y], [pY])
                            self.cp(Y2[:], pY[:, 0:128], [pY], [Y2])
                        pT = pp.next()
                        self.mm(pT[:, 0:128], X2[:], w["TT"][:], [X2, w["TT"]], [pT])
                        self.tt(w["TT"][:], w["TT"][:], pT[:, 0:128], ALU.add, [w["TT"], pT], [w["TT"]])
                        X, X2 = X2, X
                        Y, Y2 = Y2, Y
                    TT = w["TT"]
                    self.ts(w["vb"][:], vtm[:], s["beta"][:, d * 8 + h:d * 8 + h + 1], None, ALU.mult, None, [vtm, s["beta"]], [w["vb"]])
                    self.ts(w["kbg"][:], ktm[:], s["cb"][:, h:h + 1], None, ALU.mult, None, [ktm, s["cb"]], [w["kbg"]])
                    pu = pp.next()
                    self.mm(pu[:, 0:128], TT[:], w["vb"][:], [TT, w["vb"]], [pu])
                    self.cp(w["u"][:], pu[:, 0:128], [pu], [w["u"]])
                    pw = pp.next()
                    self.mm(pw[:, 0:128], w["kbg"][:], TT[:], [TT, w["kbg"]], [pw])
                    self.cp(w["wT"][:], pw[:, 0:128], [pw], [w["wT"]])
                    pq = pp.next()
                    self.mm(pq[:, 0:128], qT[:], kT[:], [qT, kT], [pq])
                    self.tt(w["att"][:], pq[:, 0:128], w["dec"][:], ALU.mult, [pq, w["dec"]], [w["att"]])
                    pt = pp.next()
                    self.tr(pt[:, 0:128], w["att"][:], idf[:], [w["att"], idf], [pt])
                    self.cp(w["attT"][:], pt[:, 0:128], [pt], [w["attT"]])
                    self.tt(w["qgT"][:], qT[:], w["egcb"][:], ALU.mult, [qT, w["egcb"]], [w["qgT"]])
                    self.ts(w["kd"][:], ktm[:], s["ekd"][:, h:h + 1], None, ALU.mult, None, [ktm, s["ekd"]], [w["kd"]])
                    St = Sst[h][d]
                    p1 = pp.next()
                    self.mm(p1[:, 0:128], w["wT"][:], St[:], [w["wT"], St], [p1])
                    self.tt(w["vnew"][:], w["u"][:], p1[:, 0:128], ALU.subtract, [w["u"], p1], [w["vnew"]])
                    p2 = pp.next()
                    self.mm(p2[:, 0:128], w["qgT"][:], St[:], [w["qgT"], St], [p2], start=True, stop=False, inc=False)
                    self.mm(p2[:, 0:128], w["attT"][:], w["vnew"][:], [w["attT"], w["vnew"]], [p2], start=False, stop=True)
                    p3 = pp.next()
                    self.mm(p3[:, 0:128], w["kd"][:], w["vnew"][:], [w["kd"], w["vnew"]], [p3])
                    self.stt(St[:], St[:], s["glast"][:, h:h + 1], p3[:, 0:128], ALU.mult, ALU.add, [St, s["glast"], p3], [St])
                    if d == 0:
                        self.cp(w["osb"][:], p2[:, 0:128], [p2], [w["osb"]])
                        P.dma(ofs[tok:tok + 128, h * 128:(h + 1) * 128], w["osb"][:], reads=[w["osb"]], writes=[P.region("dn_of", n, h)])
                    else:
                        P.dma(w["of"][:], ofs[tok:tok + 128, h * 128:(h + 1) * 128], reads=[P.region("dn_of", n, h)], writes=[w["of"]])
                        self.tt(w["osum"][:], w["of"][:], p2[:, 0:128], ALU.add, [w["of"], p2], [w["osum"]])
                        ss = ssp.next(); r1 = rs1.next(); rstd = rstd1.next()
                        P.op("act", lambda e, w=w, ss=ss: e.activation(out=w["junk"][:], in_=w["osum"][:], func=AF.Square, accum_out=ss[:]), reads=[w["osum"]], writes=[w["junk"], ss])
                        self.act(r1[:], ss[:], AF.Sqrt, [ss], [r1], scale=1.0 / 128, bias=EPS)
                        P.op("dve", lambda e, r1=r1, rstd=rstd: e.reciprocal(out=rstd[:], in_=r1[:]), reads=[r1], writes=[rstd])
                        self.stt(w["on"][:], w["osum"][:], rstd[:], ng[:], ALU.mult, ALU.mult, [w["osum"], rstd, ng], [w["on"]])
                        pt = pp.next()
                        self.tr(pt[:, 0:128], w["on"][:], idf[:], [w["on"], idf], [pt])
                        P.dma(w["zt"][:], self.scr["dz"][h * 128:(h + 1) * 128, tok:tok + 128], reads=[P.region("dz", h, seg)], writes=[w["zt"]])
                        self.act(w["zs"][:], w["zt"][:], AF.Silu, [w["zt"]], [w["zs"]])
                        ob = obp.next()
                        self.tt(ob[:], pt[:, 0:128], w["zs"][:], ALU.mult, [pt, w["zs"]], [ob])
                        P.dma(self.scr["brT"][3 * BR + h * 128:3 * BR + (h + 1) * 128, tok:tok + 128], ob[:], reads=[ob], writes=[P.region("brT", 3, n, h)])


K.deltanet = deltanet


def xattn(self, l, L, sn):
    P = self.P
    inp = self.inp
    idb = self.c["identb"]
    SEG = min(L, 2048)
    scale = 256.0 ** -0.5
    with Scope(P) as S:
        kT = S.sb("kT", [128, 8, 256], BF16)
        V = S.sb("V", [128, 2, BR], BF16)
        with Scope(P) as S2:
            memT = S2.sb("memT", [128, 16, 256], BF16)
            gt = S2.sb("gt", [128, D], F32)
            tp = self.norm_pools(S2)
            xpool = S2.pool("xt", [128, D], F32, 2)
            hbpool = S2.pool("hb", [128, D], BF16, 2)
            pstp = S2.pspool("pst", 2, BF16)
            psm = S2.pspool("psm", 3, F32)
            wpool = S2.pool("wb", [128, 16, 256], BF16, 3)
            P.dma(gt[:], inp["mem_ln_g"][l].partition_broadcast(128), writes=[gt])
            mem = inp["mem_" + sn]
            self.make_hT(S2, lambda tt: mem[tt * 128:(tt + 1) * 128, :], gt, memT, 2, tp, xpool, hbpool, pstp, lambda tt: [])
            wkv = inp["xa_w_kv"][l]
            for c0 in range(0, 1024, 256):
                wb = wpool.next()
                P.dma(wb[:], wkv[:, c0:c0 + 256].rearrange("(k p) c -> p k c", p=128), writes=[wb], q="pool")
                for ct in range(2):
                    ps = psm.next()
                    for k in range(16):
                        self.mm(ps[:, 0:256], wb[:, k, ct * 128:(ct + 1) * 128], memT[:, k, :], [wb, memT], [ps], start=(k == 0), stop=(k == 15), inc=(k == 15))
                    self.cp(kT[:, c0 // 128 + ct, :], ps[:, 0:256], [ps], [kT])
            for c0 in range(0, 1024, 256):
                wb = wpool.next()
                P.dma(wb[:], wkv[:, 1024 + c0:1024 + c0 + 256].rearrange("(k p) c -> p k c", p=128), writes=[wb], q="pool")
                for mt in range(2):
                    ps = psm.next()
                    for k in range(16):
                        self.mm(ps[:, 0:256], memT[:, k, mt * 128:(mt + 1) * 128], wb[:, k, :], [wb, memT], [ps], start=(k == 0), stop=(k == 15), inc=(k == 15))
                    self.cp(V[:, mt, c0:c0 + 256], ps[:, 0:256], [ps], [V])
        qp = S.pool("q", [128, 8, 128], BF16, 2)
        zp = S.pool("z", [128, 8, 128], F32, 2)
        zsp = S.pool("zs", [128, 8, 128], F32, 2)
        obp = S.pool("ob", [128, 8, 128], BF16, 2)
        ep = S.pool("e", [128, 256], F32, 2)
        pbp = S.pool("pb", [128, 256], BF16, 2)
        ptp = S.pool("pt", [128, 2, 128], BF16, 2)
        mxp = S.pool("mx", [128, 1], F32, 3); nbp = S.pool("nb", [128, 1], F32, 3); rsp = S.pool("rs", [128, 1], F32, 3); rip = S.pool("ri", [128, 1], F32, 3)
        pss = S.pspool("pss", 2, F32)
        pst = S.pspool("pst", 2, BF16)
        pso = S.pspool("pso", 3, F32)
        for n in range(L // 128):
            tok = n * 128
            seg = tok // SEG
            q = qp.next(); z = zp.next(); zs = zsp.next(); ob = obp.next()
            P.dma(q[:], self.scr["xq"][:, tok:tok + 128].rearrange("(k p) t -> p k t", p=128), reads=[P.region("xq", f, seg) for f in range(8)], writes=[q], q="pool")
            P.dma(z[:], self.scr["xz"][:, tok:tok + 128].rearrange("(k p) t -> p k t", p=128), reads=[P.region("xz", f, seg) for f in range(8)], writes=[z])
            self.act(zs[:], z[:], AF.Silu, [z], [zs])
            for hh in range(4):
                ps = pss.next()
                for c2 in range(2):
                    self.mm(ps[:, 0:256], q[:, hh * 2 + c2, :], kT[:, hh * 2 + c2, :], [q, kT], [ps], start=(c2 == 0), stop=(c2 == 1), inc=(c2 == 1))
                mx = mxp.next(); nb = nbp.next(); rs = rsp.next(); ri = rip.next(); e = ep.next(); pb = pbp.next(); pt_sb = ptp.next()
                P.op("dve", lambda e_, mx=mx, ps=ps: e_.reduce_max(out=mx[:], in_=ps[:, 0:256], axis=AX.X), reads=[ps], writes=[mx])
                self.ts(nb[:], mx[:], -scale, None, ALU.mult, None, [mx], [nb])
                P.op("act", lambda e_, e=e, ps=ps, nb=nb, rs=rs: e_.activation(out=e[:], in_=ps[:, 0:256], func=AF.Exp, scale=scale, bias=nb[:], accum_out=rs[:]),
                     reads=[ps, nb], writes=[e, rs])
                P.op("dve", lambda e_, rs=rs, ri=ri: e_.reciprocal(out=ri[:], in_=rs[:]), reads=[rs], writes=[ri])
                self.ts(pb[:], e[:], ri[:], None, ALU.mult, None, [e, ri], [pb])
                ptt = pst.next()
                for mt in range(2):
                    self.tr(ptt[:, mt * 128:(mt + 1) * 128], pb[:, mt * 128:(mt + 1) * 128], idb[:], [pb, idb], [ptt], inc=(mt == 1))
                self.cp(pt_sb[:], ptt[:, 0:256].rearrange("p (a b) -> p a b", a=2), [ptt], [pt_sb])
                for c2 in range(2):
                    po = pso.next()
                    for mt in range(2):
                        self.mm(po[:, 0:128], V[:, mt, hh * 256 + c2 * 128:hh * 256 + (c2 + 1) * 128], pt_sb[:, mt, :], [V, pt_sb], [po], start=(mt == 0), stop=(mt == 1), inc=(mt == 1))
                    self.tt(ob[:, hh * 2 + c2, :], po[:, 0:128], zs[:, hh * 2 + c2, :], ALU.mult, [po, zs], [ob])
            P.dma(self.scr["brT"][4 * BR:5 * BR, tok:tok + 128].rearrange("(k p) t -> p k t", p=128), ob[:], reads=[ob], writes=[P.region("brT", 4, n, f) for f in range(8)])


K.xattn = xattn


def attention(self, l, L):
    P = self.P
    inp = self.inp
    idf = self.c["identf"]
    onesf = self.c["onesf"]
    onesb = self.c["onesb"]
    SEG = min(L, 2048)
    TQ = min(512, L)
    NKC = L // 128
    with Scope(P) as S:
        cosT = S.sb("cosT", [128, L], F32)
        sinT = S.sb("sinT", [128, L], F32)
        RM = S.sb("RM", [128, 128], F32)
        qg = S.sb("qg", [128, 1], F32); kg = S.sb("kg", [128, 1], F32)
        P.dma(qg[:], inp["attn_qn_g"][l].rearrange("(p o) -> p o", o=1), writes=[qg])
        P.dma(kg[:], inp["attn_kn_g"][l].rearrange("(p o) -> p o", o=1), writes=[kg])
        for b in range(4):
            if b % 2 == 0:
                self.ts(RM[:, b * 32:(b + 1) * 32], idf[:, (b + 1) * 32:(b + 2) * 32], -1.0, None, ALU.mult, None, [idf, RM], [RM])
            else:
                self.cp(RM[:, b * 32:(b + 1) * 32], idf[:, (b - 1) * 32:b * 32], [idf, RM], [RM], eng="dve")
        with Scope(P) as S2:
            pi_ = S2.sb("pi", [128, 1], F32); f_ = S2.sb("f", [128, 1], F32); t_ = S2.sb("t", [128, 1], F32); fr = S2.sb("fr", [128, 1], F32)
            P.op("pool", lambda e: e.iota(pi_[:], pattern=[[0, 1]], base=0, channel_multiplier=1, allow_small_or_imprecise_dtypes=True), writes=[pi_])
            self.cp(f_[:], pi_[:], [pi_], [f_], eng="dve")
            for thr in (32.0, 64.0, 96.0):
                self.ts(t_[:], pi_[:], thr, -32.0, ALU.is_ge, ALU.mult, [pi_, t_], [t_])
                self.tt(f_[:], f_[:], t_[:], ALU.add, [f_, t_], [f_])
            self.act(fr[:], f_[:], AF.Exp, [f_], [fr], scale=-math.log(10000.0) / 32.0)
            CB = min(1024, L)
            pos = S2.sb("pos", [128, CB], F32)
            xx = S2.sb("xx", [128, 2, CB], F32); u = S2.sb("u", [128, 2, CB], F32); ki = S2.sb("ki", [128, 2, CB], I32); kf = S2.sb("kf", [128, 2, CB], F32)
            sc = S2.sb("sc", [128, 2, CB], F32)
            for cb in range(L // CB):
                r0 = (cb * CB) // 64
                P.op("pool", lambda e, r0=r0: e.iota(pos[0:64, :], pattern=[[1, CB // 64], [0, 64]], base=r0, channel_multiplier=0, allow_small_or_imprecise_dtypes=True), reads=[pos], writes=[pos])
                P.op("pool", lambda e: e.iota(pos[64:128, :], pattern=[[0, CB // 64], [1, 64]], base=0, channel_multiplier=0, allow_small_or_imprecise_dtypes=True), reads=[pos], writes=[pos])
                self.ts(xx[:, 0, :], pos[:], fr[:], None, ALU.mult, None, [pos, fr, xx], [xx])
                self.ts(xx[:, 1, :], xx[:, 0, :], math.pi / 2, None, ALU.add, None, [xx], [xx])
                _reduce_sin(self, xx, u, ki, kf, sc)
                self.cp(sinT[:, cb * CB:(cb + 1) * CB], sc[:, 0, :], [sc], [sinT], eng="act")
                self.cp(cosT[:, cb * CB:(cb + 1) * CB], sc[:, 1, :], [sc], [cosT], eng="dve")
        kT = S.sb("kTa", [128, 2, L], BF16)
        V = S.sb("Va", [128, NKC, 256], BF16)
        P.dma(V[:], self.scr["av"][0:L, :].rearrange("(n p) c -> p n c", p=128), reads=[P.region("av", "t", n, 0) for n in range(NKC)], writes=[V], q="pool")
        xq = S.pool("xq", [128, TQ], F32, 2)
        sqp = S.pool("sq", [128, TQ], F32, 2)
        rqp = S.pool("rq", [128, TQ], F32, 2)
        qnp = S.pool("qn", [128, TQ], F32, 2)
        t1p = S.pool("t1", [128, TQ], F32, 2)
        t2p = S.pool("t2", [128, TQ], F32, 2)
        qrp = S.pool("qr", [128, TQ], BF16, 2)
        ptp = S.pool("pt", [128, TQ], BF16, 3)
        zp = S.pool("z", [128, TQ], F32, 2); zsp = S.pool("zs", [128, TQ], F32, 2)
        rdp = S.pool("rd", [128, TQ], F32, 2); o1p = S.pool("o1", [128, TQ], F32, 2)
        obp = S.pool("ob", [128, TQ], BF16, 2)
        psx = S.pspool("psx", 1, F32)
        pst = S.pspool("pst", 3, F32)
        pso = S.pspool("pso", 2, F32)
        psd = S.pspool("psd", 2, F32)

        def norm_rope(src_name, row0, t0, gcol, region, out_ap, out_bufs, scl):
            x = xq.next()
            P.dma(x[:], self.scr[src_name][row0:row0 + 128, t0:t0 + TQ], reads=[region], writes=[x])
            sq = sqp.next(); rq = rqp.next(); qn = qnp.next(); t1 = t1p.next(); t2 = t2p.next()
            self.tt(sq[:], x[:], x[:], ALU.mult, [x], [sq])
            ps = psx.next()
            self.mm(ps[:, 0:TQ], onesf[:], sq[:], [onesf, sq], [ps])
            self.act(rq[:], ps[:, 0:TQ], AF.Sqrt, [ps], [rq], scale=1.0 / 128, bias=EPS)
            P.op("dve", lambda e: e.reciprocal(out=rq[:], in_=rq[:]), reads=[rq], writes=[rq])
            self.stt(qn[:], x[:], gcol[:], rq[:], ALU.mult, ALU.mult, [x, gcol, rq], [qn])
            ps2 = psx.next()
            self.mm(ps2[:, 0:TQ], RM[:], qn[:], [RM, qn], [ps2])
            self.tt(t1[:], qn[:], cosT[:, t0:t0 + TQ], ALU.mult, [qn, cosT], [t1])
            self.tt(t2[:], ps2[:, 0:TQ], sinT[:, t0:t0 + TQ], ALU.mult, [ps2, sinT], [t2])
            self.stt(out_ap, t1[:], scl, t2[:], ALU.mult, ALU.add, [t1, t2], out_bufs)

        def norm_rope2(src_name, row0, t0, gcol, region, out_ap, out_bufs, scl):
            x = xq.next()
            P.dma(x[:], self.scr[src_name][row0:row0 + 128, t0:t0 + TQ], reads=[region], writes=[x])
            sq = sqp.next(); rq = rqp.next(); qn = qnp.next(); t1 = t1p.next(); t2 = t2p.next()
            self.tt(sq[:], x[:], x[:], ALU.mult, [x], [sq])
            ps = psx.next()
            self.mm(ps[:, 0:TQ], onesf[:], sq[:], [onesf, sq], [ps])
            self.act(rq[:], ps[:, 0:TQ], AF.Sqrt, [ps], [rq], scale=1.0 / 128, bias=EPS)
            P.op("dve", lambda e: e.reciprocal(out=rq[:], in_=rq[:]), reads=[rq], writes=[rq])
            self.stt(qn[:], x[:], gcol[:], rq[:], ALU.mult, ALU.mult, [x, gcol, rq], [qn])
            ps2 = psx.next()
            self.mm(ps2[:, 0:TQ], RM[:], qn[:], [RM, qn], [ps2])
            self.stt(t1[:], qn[:], scl, cosT[:, t0:t0 + TQ], ALU.mult, ALU.mult, [qn, cosT], [t1])
            self.stt(t2[:], ps2[:, 0:TQ], scl, sinT[:, t0:t0 + TQ], ALU.mult, ALU.mult, [ps2, sinT], [t2])
            self.tt(out_ap, t1[:], t2[:], ALU.add, [t1, t2], out_bufs)

        for kvh in range(2):
            for tq in range(L // TQ):
                t0 = tq * TQ
                norm_rope2("ak", kvh * 128, t0, kg, P.region("ak", kvh, t0 // SEG), kT[:, kvh, t0:t0 + TQ], [kT], 1.0)
        for tq in range(L // TQ):
            t0 = tq * TQ
            seg = t0 // SEG
            for hq in range(8):
                kvh = hq // 4
                qr = qrp.next()
                norm_rope2("aq", hq * 128, t0, qg, P.region("aq", hq, seg), qr[:], [qr], 128.0 ** -0.5)
                po = pso.next(); pd = psd.next()
                for kc in range(NKC):
                    pst_ = pst.next()
                    self.mm(pst_[:, 0:TQ], kT[:, kvh, kc * 128:(kc + 1) * 128], qr[:], [kT, qr], [pst_])
                    pt = ptp.next()
                    self.act(pt[:], pst_[:, 0:TQ], AF.Exp, [pst_], [pt])
                    self.mm(po[:, 0:TQ], V[:, kc, kvh * 128:(kvh + 1) * 128], pt[:], [V, pt], [po], start=(kc == 0), stop=(kc == NKC - 1), inc=False)
                    self.mm(pd[:, 0:TQ], onesb[:], pt[:], [onesb, pt], [pd], start=(kc == 0), stop=(kc == NKC - 1), inc=True)
                z = zp.next(); zs = zsp.next(); rd = rdp.next(); o1 = o1p.next(); ob = obp.next()
                P.dma(z[:], self.scr["az"][hq * 128:(hq + 1) * 128, t0:t0 + TQ], reads=[P.region("az", hq, seg)], writes=[z])
                self.act(zs[:], z[:], AF.Silu, [z], [zs])
                P.op("dve", lambda e, rd=rd, pd=pd: e.reciprocal(out=rd[:], in_=pd[:, 0:TQ]), reads=[pd], writes=[rd])
                self.tt(o1[:], po[:, 0:TQ], rd[:], ALU.mult, [po, rd], [o1])
                self.tt(ob[:], o1[:], zs[:], ALU.mult, [o1, zs], [ob])
                P.dma(self.scr["brT"][hq * 128:(hq + 1) * 128, t0:t0 + TQ], ob[:], reads=[ob], writes=[P.region("brT", 0, t0 // 128 + i, hq) for i in range(TQ // 128)])


K.attention = attention
```

```python
import math
from contextlib import ExitStack
import numpy as np
import concourse.bass as bass
import concourse.mybir as mybir
from concourse.bass_utils import run_bass_kernel_spmd

F32 = mybir.dt.float32
BF16 = mybir.dt.bfloat16
I32 = mybir.dt.int32
AF = mybir.ActivationFunctionType
ALU = mybir.AluOpType
AX = mybir.AxisListType

NDMA = 56
D = 2048
BR = 1024
NCOL = 24096
EPS = 1e-6
OFF = dict(aq=0, ak=1024, av=1280, az=1536, su=2560, sz=3584, gu=4608, gv=5632, gz=6656,
           dq=7680, dk=8704, dv=9728, dz=10752, dbeta=11776, da=11792, xq=11808, xz=12832,
           gates=13856)
TWO_PI = 2.0 * math.pi
RR_SEQ = False


class Buf:
    __slots__ = ("name", "h", "last_w", "readers", "psum")

    def __init__(self, name, h=None):
        self.name = name
        self.h = h
        self.last_w = None
        self.readers = []
        self.psum = False

    def __getitem__(self, k):
        return self.h[k]


class Prog:
    CE = ("pe", "act", "dve", "pool")

    def __init__(self, nc):
        self.nc = nc
        self.q = {e: [] for e in ("pe", "act", "dve", "pool", "sp")}
        self.sem = {e: nc.alloc_semaphore("sem_" + e) for e in self.CE}
        self.cnt = {e: 0 for e in self.CE}
        self.dsem = [nc.alloc_semaphore("dsem%d" % i) for i in range(NDMA)]
        self.dcnt = [0] * NDMA
        self.drr = 0
        self.waited = {e: {} for e in self.q}
        self.regions = {}
        self.out_events = []
        self.uid = 0
        self.n_ops = 0

    def name(self, s):
        self.uid += 1
        return "%s_%d" % (s, self.uid)

    def region(self, *key):
        b = self.regions.get(key)
        if b is None:
            b = Buf(str(key))
            self.regions[key] = b
        return b

    def _collect(self, eng, reads, writes):
        evs = []
        for b in reads:
            if b.last_w is not None:
                evs.append(b.last_w)
        for b in writes:
            if b.last_w is not None:
                evs.append(b.last_w)
            evs.extend(b.readers)
        waits = []
        wd = self.waited[eng]
        for (key, val) in evs:
            if key[0] == "c":
                if key[1] == eng and eng == "pe":
                    continue
                assert self.cnt[key[1]] >= val, "wait on not-yet-emitted inc %s %d" % (key, val)
            if wd.get(key, 0) >= val:
                continue
            wd[key] = val
            waits.append((key, val))
        return waits

    def _commit(self, ev, reads, writes):
        for b in reads:
            if len(b.readers) > 24:
                mx = {}
                for k, v in b.readers:
                    if mx.get(k, 0) < v:
                        mx[k] = v
                b.readers = list(mx.items())
            b.readers.append(ev)
        for b in writes:
            b.last_w = ev
            b.readers = []

    def op(self, eng, fn, reads=(), writes=(), inc=True):
        pr = [b for b in reads if b.psum]
        if pr:
            reads = [b for b in reads if not b.psum]
            writes = list(writes) + pr
        waits = self._collect(eng, reads, writes)
        if inc:
            self.cnt[eng] += 1
            ev = (("c", eng), self.cnt[eng])
        else:
            ev = (("c", eng), self.cnt[eng] + 1)
        self.q[eng].append((waits, fn, inc, None))
        self._commit(ev, reads, writes)
        self.n_ops += 1

    def dma(self, out_ap, in_ap, reads=(), writes=(), q="sp", is_out=False, slow=False):
        waits = self._collect(q, reads, writes)
        j = self.drr
        self.drr = (self.drr + 1) % NDMA
        prev = self.dcnt[j] * 16
        key = ("d", j)
        if prev and self.waited[q].get(key, 0) < prev:
            self.waited[q][key] = prev
            waits.append((key, prev))
        self.dcnt[j] += 1
        ev = (key, self.dcnt[j] * 16)
        self.q[q].append((waits, (out_ap, in_ap, slow), True, j))
        self._commit(ev, reads, writes)
        if is_out:
            self.out_events.append(ev)
        self.n_ops += 1

    def barrier(self):
        for e in self.q:
            wd = self.waited[e]
            waits = []
            for c in self.CE:
                if c == e:
                    continue
                v = self.cnt[c]
                if v and wd.get(("c", c), 0) < v:
                    wd[("c", c)] = v
                    waits.append((("c", c), v))
            for j in range(NDMA):
                v = self.dcnt[j] * 16
                if v and wd.get(("d", j), 0) < v:
                    wd[("d", j)] = v
                    waits.append((("d", j), v))
            if waits:
                self.q[e].append((waits, None, False, None))

    def _semof(self, key):
        return self.sem[key[1]] if key[0] == "c" else self.dsem[key[1]]

    def emit(self):
        nc = self.nc
        mx = {}
        for k, v in self.out_events:
            mx[k] = max(mx.get(k, 0), v)
        prog = self

        def run(engname, eobj):
            for (waits, fn, inc, dj) in prog.q[engname]:
                for (key, val) in waits:
                    eobj.wait_ge(prog._semof(key), val)
                if fn is None:
                    continue
                if dj is not None:
                    out_ap, in_ap, slow = fn
                    if slow:
                        eobj.dma_start(out=out_ap, in_=in_ap, allow_slow_non_contiguous=True).then_inc(prog.dsem[dj], 16)
                    else:
                        eobj.dma_start(out=out_ap, in_=in_ap).then_inc(prog.dsem[dj], 16)
                else:
                    ins = fn(eobj)
                    if inc:
                        ins.then_inc(prog.sem[engname], 1)
            if engname == "sp":
                for k, v in mx.items():
                    eobj.wait_ge(prog._semof(k), v)

        with nc.Block() as block:
            @block.tensor
            def _(e):
                run("pe", e)

            @block.scalar
            def _(e):
                run("act", e)

            @block.vector
            def _(e):
                run("dve", e)

            @block.gpsimd
            def _(e):
                run("pool", e)

            @block.sync
            def _(e):
                run("sp", e)


class Scope:
    def __init__(self, P):
        self.P = P
        self.es = ExitStack()

    def __enter__(self):
        self.es.__enter__()
        return self

    def __exit__(self, *a):
        self.P.barrier()
        return self.es.__exit__(*a)

    def sb(self, name, shape, dtype=F32):
        h = self.es.enter_context(self.P.nc.sbuf_tensor(self.P.name(name), list(shape), dtype))
        return Buf(name, h)

    def ps(self, name, dtype=F32):
        n = 512 if dtype == F32 else 1024
        h = self.es.enter_context(self.P.nc.psum_tensor(self.P.name(name), [128, n], dtype))
        b = Buf(name, h)
        b.psum = True
        return b

    def pool(self, name, shape, dtype, n):
        return Rot([self.sb("%s%d" % (name, i), shape, dtype) for i in range(n)])

    def pspool(self, name, n, dtype=F32):
        return Rot([self.ps("%s%d" % (name, i), dtype) for i in range(n)])


class Rot:
    def __init__(self, tiles):
        self.tiles = tiles
        self.i = 0

    def next(self):
        t = self.tiles[self.i]
        self.i = (self.i + 1) % len(self.tiles)
        return t


class K:
    def __init__(self, nc, seqs, depth=2, debug=False, branches=(0, 1, 2, 3, 4)):
        self.nc = nc
        self.P = Prog(nc)
        self.seqs = seqs
        self.depth = depth
        self.debug = debug
        self.branches = branches
        self.Lmax = max(L for _, L in seqs)
        self.eng_rr = 0
        self.declare()

    def declare(self):
        nc = self.nc
        self.inp = {}

        def din(name, shape):
            self.inp[name] = nc.dram_tensor(name, list(shape), F32, kind="ExternalInput").ap()

        for sn, L in self.seqs:
            din("x_" + sn, [L, D])
            din("mem_" + sn, [256, D])
        dp = self.depth
        din("ln_g", [dp, D]); din("mem_ln_g", [dp, D]); din("w_in", [dp, D, NCOL])
        din("attn_qn_g", [dp, 128]); din("attn_kn_g", [dp, 128])
        din("ssm_lam_re", [dp, 2, 64, 64]); din("ssm_lam_im", [dp, 2, 64, 64]); din("ssm_log_dt", [dp, 2, 64])
        din("ssm_b_re", [dp, 2, 64, 64, 16]); din("ssm_b_im", [dp, 2, 64, 64, 16])
        din("ssm_c_re", [dp, 2, 64, 16, 64]); din("ssm_c_im", [dp, 2, 64, 16, 64])
        din("ssm_d", [dp, BR]); din("ssm_glu_w", [dp, BR, BR]); din("ssm_glu_b", [dp, BR])
        din("sgu_ln_g", [dp, BR]); din("sgu_ln_b", [dp, BR]); din("sgu_w", [dp, 8, 128, 128]); din("sgu_b", [dp, 8, 128])
        din("dn_conv_w", [dp, 3, 3 * BR]); din("dn_a_log", [dp, 2, 8]); din("dn_dt_bias", [dp, 2, 8]); din("dn_norm_g", [dp, 128])
        din("xa_w_kv", [dp, D, 2 * BR]); din("w_branch", [dp, 5, BR, D]); din("w_out", [dp, D, D]); din("final_g", [D])
        self.out = {}
        for sn, L in self.seqs:
            self.out[sn] = nc.dram_tensor("y_" + sn, [L, D], F32, kind="ExternalOutput").ap()
        Lm = self.Lmax
        kind = "ExternalOutput" if self.debug else "Internal"
        self.scr = {}

        def dscr(name, shape, dt=F32):
            self.scr[name] = nc.dram_tensor("scr_" + name, list(shape), dt, kind=kind).ap()

        for nm, rows in (("aq", 1024), ("ak", 256), ("az", 1024), ("su", 1024),
                         ("sz", 1024), ("gu", 1024), ("gz", 1024), ("dqkv", 3072), ("dz", 1024),
                         ("xq", 1024), ("xz", 1024)):
            dscr(nm, [rows, Lm])
        dscr("av", [Lm, 256]); dscr("gv", [Lm, 1024]); dscr("dba", [Lm, 32])
        dscr("brT", [5 * BR, Lm], BF16)
        dscr("xres", [Lm, D])
        dscr("syT", [BR, Lm], BF16)
        dscr("dn_of", [Lm, BR])
        if self.debug:
            dscr("dbg", [BR, Lm])
            dscr("dbg2", [BR, Lm])

    def consts(self, S):
        P = self.P
        c = {}
        c["identf"] = S.sb("identf", [128, 128], F32)
        c["identb"] = S.sb("identb", [128, 128], BF16)
        c["onesf"] = S.sb("onesf", [128, 128], F32)
        c["onesb"] = S.sb("onesb", [128, 128], BF16)
        idf, idb, of, ob = c["identf"], c["identb"], c["onesf"], c["onesb"]
        P.op("pool", lambda e: e.memset(idf[:], 0.0), writes=[idf])
        P.op("pool", lambda e: e.affine_select(out=idf[:], in_=idf[:], pattern=[[-1, 128]], compare_op=ALU.not_equal,
                                              fill=1.0, base=0, channel_multiplier=1), reads=[idf], writes=[idf])
        P.op("dve", lambda e: e.tensor_copy(out=idb[:], in_=idf[:]), reads=[idf], writes=[idb])
        P.op("pool", lambda e: e.memset(of[:], 1.0), writes=[of])
        P.op("dve", lambda e: e.tensor_copy(out=ob[:], in_=of[:]), reads=[of], writes=[ob])
        self.c = c

    def evac_eng(self):
        self.eng_rr ^= 1
        return "act" if self.eng_rr else "dve"

    def copy(self, eng, out_ap, in_ap, reads, writes):
        if eng == "act":
            self.P.op("act", lambda e: e.activation(out=out_ap, in_=in_ap, func=AF.Copy), reads=reads, writes=writes)
        else:
            self.P.op(eng, lambda e: e.tensor_copy(out=out_ap, in_=in_ap), reads=reads, writes=writes)

    def rstd_from_ss(self, ss, rs, rstd, n):
        P = self.P
        P.op("act", lambda e: e.activation(out=rs[:], in_=ss[:], func=AF.Sqrt, scale=1.0 / n, bias=EPS), reads=[ss], writes=[rs])
        P.op("dve", lambda e: e.reciprocal(out=rstd[:], in_=rs[:]), reads=[rs], writes=[rstd])

    def rmsnorm_tile(self, xt, gt, out, tp):
        P = self.P
        junk, ss, rs, rstd = tp["junk"].next(), tp["ss"].next(), tp["rs"].next(), tp["rstd"].next()
        P.op("act", lambda e: e.activation(out=junk[:], in_=xt[:], func=AF.Square, accum_out=ss[:]), reads=[xt], writes=[junk, ss])
        self.rstd_from_ss(ss, rs, rstd, D)
        P.op("dve", lambda e: e.scalar_tensor_tensor(out=out[:], in0=xt[:], scalar=rstd[:], in1=gt[:], op0=ALU.mult, op1=ALU.mult),
             reads=[xt, rstd, gt], writes=[out])

    def norm_pools(self, S):
        return dict(junk=S.pool("junk", [128, D], BF16, 1), ss=S.pool("ss", [128, 1], F32, 2),
                    rs=S.pool("rs", [128, 1], F32, 2), rstd=S.pool("rstd", [128, 1], F32, 2))

    def make_hT(self, S, src_rows, gt, hT, ntiles, tp, xpool, hbpool, pstp, src_reads):
        P = self.P
        idb = self.c["identb"]
        for tt in range(ntiles):
            xt = xpool.next()
            P.dma(xt[:], src_rows(tt), reads=src_reads(tt), writes=[xt])
            hb = hbpool.next()
            self.rmsnorm_tile(xt, gt, hb, tp)
            for k4 in range(4):
                pt = pstp.next()
                for kk in range(4):
                    k = k4 * 4 + kk
                    P.op("pe", lambda e, k=k, kk=kk, pt=pt, hb=hb: e.transpose(out=pt[:, kk * 128:(kk + 1) * 128], in_=hb[:, k * 128:(k + 1) * 128], identity=idb[:]),
                         reads=[hb, idb], writes=[pt], inc=(kk == 3))
                self.copy(self.evac_eng(), hT[:, k4 * 4:(k4 + 1) * 4, tt * 128:(tt + 1) * 128],
                          pt[:, 0:512].rearrange("p (k t) -> p k t", k=4), [pt], [hT])

    def phase1(self, l, sn, L, xsrc, xsrc_region):
        P = self.P
        S1 = min(L, 2048)
        w_in = self.inp["w_in"][l]
        fgroups = [("aq", OFF["aq"], 1024, False), ("ak", OFF["ak"], 256, False),
                   ("az", OFF["az"], 1024, False), ("su", OFF["su"], 1024, False),
                   ("sz", OFF["sz"], 1024, False), ("gu", OFF["gu"], 1024, False), ("gz", OFF["gz"], 1024, False),
                   ("dqkv", OFF["dq"], 3072, False), ("dz", OFF["dz"], 1024, False), ("xq", OFF["xq"], 1024, False),
                   ("xz", OFF["xz"], 1024, False)]
        tgroups = [("av", OFF["av"], 256), ("gv", OFF["gv"], 1024), ("dba", OFF["dbeta"], 32)]
        with Scope(P) as S:
            hT = S.sb("hT", [128, 16, S1], BF16)
            gt = S.sb("gt", [128, D], F32)
            tp = self.norm_pools(S)
            xpool = S.pool("xt", [128, D], F32, 2)
            hbpool = S.pool("hb", [128, D], BF16, 2)
            pstp = S.pspool("pst", 2, BF16)
            psm = S.pspool("psm", 4, F32)
            wpool = S.pool("wb", [128, 16, 256], BF16, 3)
            opool = S.pool("ost", [128, S1], F32, 2)
            otpool = S.pool("ott", [128, 256], F32, 2)
            P.dma(gt[:], self.inp["ln_g"][l].partition_broadcast(128), writes=[gt])
            for seg in range(L // S1):
                t0 = seg * S1
                self.make_hT(S, lambda tt: xsrc[t0 + tt * 128:t0 + (tt + 1) * 128, :], gt, hT, S1 // 128, tp, xpool, hbpool, pstp,
                             lambda tt: [xsrc_region(t0 + tt * 128)])
                for (nm, off, ncols, perm) in fgroups:
                    for c0 in range(0, ncols, 256):
                        wb = wpool.next()
                        src = w_in[:, off + c0:off + c0 + 256].rearrange("(k p) c -> p k c", p=128)
                        if not perm:
                            P.dma(wb[:], src, writes=[wb], q="pool")
                        else:
                            s4 = src.rearrange("p k (b two f) -> p k b two f", two=2, f=32)
                            d4 = wb[:].rearrange("p k (b two f) -> p k b two f", two=2, f=32)
                            for kq in range(0, 16, 4):
                                P.dma(d4[:, kq:kq + 4, :, 0, :], s4[:, kq:kq + 4, :, 1, :], writes=[wb], q="pool")
                                P.dma(d4[:, kq:kq + 4, :, 1, :], s4[:, kq:kq + 4, :, 0, :], writes=[wb], q="pool")
                        for ct in range(2):
                            ost = opool.next()
                            TS = min(512, S1)
                            for ts in range(S1 // TS):
                                ps = psm.next()
                                for k in range(16):
                                    P.op("pe", lambda e, k=k, ps=ps, wb=wb, ct=ct, ts=ts: e.matmul(
                                        ps[:, 0:TS], lhsT=wb[:, k, ct * 128:(ct + 1) * 128], rhs=hT[:, k, ts * TS:(ts + 1) * TS],
                                        start=(k == 0), stop=(k == 15)), reads=[wb, hT], writes=[ps], inc=(k == 15))
                                self.copy(self.evac_eng(), ost[:, ts * TS:(ts + 1) * TS], ps[:, 0:TS], [ps], [ost])
                            r0 = c0 + ct * 128
                            P.dma(self.scr[nm][r0:r0 + 128, t0:t0 + S1], ost[:, :], reads=[ost],
                                  writes=[self.P.region(nm, r0 // 128, seg)])
                for (nm, off, ncols) in tgroups:
                    for c0 in range(0, ncols, 256):
                        cw = min(256, ncols - c0)
                        wb = wpool.next()
                        src = w_in[:, off + c0:off + c0 + cw].rearrange("(k p) c -> p k c", p=128)
                        P.dma(wb[:, :, 0:cw], src, writes=[wb], q="pool")
                        for tt in range(S1 // 128):
                            ps = psm.next()
                            for k in range(16):
                                P.op("pe", lambda e, k=k, ps=ps, wb=wb, tt=tt, cw=cw: e.matmul(
                                    ps[:, 0:cw], lhsT=hT[:, k, tt * 128:(tt + 1) * 128], rhs=wb[:, k, 0:cw],
                                    start=(k == 0), stop=(k == 15)), reads=[wb, hT], writes=[ps], inc=(k == 15))
                            ott = otpool.next()
                            self.copy(self.evac_eng(), ott[:, 0:cw], ps[:, 0:cw], [ps], [ott])
                            tok = t0 + tt * 128
                            P.dma(self.scr[nm][tok:tok + 128, c0:c0 + cw], ott[:, 0:cw], reads=[ott],
                                  writes=[self.P.region(nm, "t", tok // 128, c0)])

    def sgu(self, l, L):
        P = self.P
        idf = self.c["identf"]
        with Scope(P) as S:
            wsn = S.sb("wsn", [128, 8, 128], F32)
            wsT = S.sb("wsT", [128, 8, 128], BF16)
            lng = S.sb("lng", [128, BR], F32)
            lnb = S.sb("lnb", [128, BR], F32)
            bsb = S.sb("bsb", [128, BR], F32)
            pst = S.pspool("pst", 2, F32)
            psm = S.pspool("psm", 4, F32)
            P.dma(wsn[:], self.inp["sgu_w"][l].rearrange("g p q -> p g q"), writes=[wsn])
            P.dma(lng[:], self.inp["sgu_ln_g"][l].partition_broadcast(128), writes=[lng])
            P.dma(lnb[:], self.inp["sgu_ln_b"][l].partition_broadcast(128), writes=[lnb])
            P.dma(bsb[:], self.inp["sgu_b"][l].rearrange("g p -> (g p)").partition_broadcast(128), writes=[bsb])
            for g in range(8):
                pt = pst.next()
                P.op("pe", lambda e, g=g, pt=pt: e.transpose(out=pt[:, 0:128], in_=wsn[:, g, :], identity=idf[:]), reads=[wsn, idf], writes=[pt])
                self.copy(self.evac_eng(), wsT[:, g, :], pt[:, 0:128], [pt], [wsT])
            vpool = S.pool("v", [128, BR], F32, 2)
            vnpool = S.pool("vn", [128, BR], F32, 2)
            vbpool = S.pool("vb", [128, BR], BF16, 2)
            stp = S.pool("st", [128, 2, 6], F32, 2)
            mvp = S.pool("mv", [128, 2], F32, 2)
            rsp = S.pool("rs", [128, 1], F32, 2)
            rstdp = S.pool("rstd", [128, 1], F32, 2)
            gup = S.pool("gu", [128, 8, 128], F32, 2)
            gzp = S.pool("gz", [128, 8, 128], F32, 2)
            szp = S.pool("sz", [128, 8, 128], F32, 2)
            t1p = S.pool("t1", [128, 8, 128], F32, 2)
            obp = S.pool("ob", [128, 8, 128], BF16, 2)
            for n in range(L // 128):
                tok = n * 128
                v = vpool.next()
                P.dma(v[:], self.scr["gv"][tok:tok + 128, :], reads=[P.region("gv", "t", n, c0) for c0 in range(0, 1024, 256)], writes=[v])
                gu = gup.next(); gz = gzp.next()
                seg = tok // min(L, 2048)
                P.dma(gu[:], self.scr["gu"][:, tok:tok + 128].rearrange("(g d) t -> d g t", d=128),
                      reads=[P.region("gu", g, seg) for g in range(8)], writes=[gu])
                P.dma(gz[:], self.scr["gz"][:, tok:tok + 128].rearrange("(g d) t -> d g t", d=128),
                      reads=[P.region("gz", g, seg) for g in range(8)], writes=[gz])
                st = stp.next(); mv = mvp.next(); rs = rsp.next(); rstd = rstdp.next()
                for hh in range(2):
                    P.op("dve", lambda e, hh=hh, st=st, v=v: e.bn_stats(out=st[:, hh, :], in_=v[:, hh * 512:(hh + 1) * 512]), reads=[v], writes=[st])
                P.op("dve", lambda e, st=st, mv=mv: e.bn_aggr(out=mv[:], in_=st[:].rearrange("p a b -> p (a b)")), reads=[st], writes=[mv])
                P.op("act", lambda e, mv=mv, rs=rs: e.activation(out=rs[:], in_=mv[:, 1:2], func=AF.Sqrt, scale=1.0, bias=EPS), reads=[mv], writes=[rs])
                P.op("dve", lambda e, rs=rs, rstd=rstd: e.reciprocal(out=rstd[:], in_=rs[:]), reads=[rs], writes=[rstd])
                vn = vnpool.next(); vb = vbpool.next()
                P.op("dve", lambda e, v=v, vn=vn, mv=mv, rstd=rstd: e.tensor_scalar(out=vn[:], in0=v[:], scalar1=mv[:, 0:1], scalar2=rstd[:],
                                                                                     op0=ALU.subtract, op1=ALU.mult), reads=[v, mv, rstd], writes=[vn])
                P.op("dve", lambda e, vn=vn: e.tensor_tensor(out=vn[:], in0=vn[:], in1=lng[:], op=ALU.mult), reads=[vn, lng], writes=[vn])
                P.op("dve", lambda e, vn=vn, vb=vb: e.tensor_tensor(out=vb[:], in0=vn[:], in1=lnb[:], op=ALU.add), reads=[vn, lnb], writes=[vb])
                sz = szp.next()
                P.op("act", lambda e, sz=sz, gz=gz: e.activation(out=sz[:], in_=gz[:], func=AF.Silu), reads=[gz], writes=[sz])
                t1 = t1p.next(); ob = obp.next()
                for hh in range(2):
                    ps = psm.next()
                    for gg in range(4):
                        g = hh * 4 + gg
                        P.op("pe", lambda e, g=g, gg=gg, ps=ps, vb=vb: e.matmul(ps[:, gg * 128:(gg + 1) * 128], lhsT=vb[:, g * 128:(g + 1) * 128], rhs=wsT[:, g, :],
                                                                              start=True, stop=True), reads=[vb, wsT], writes=[ps], inc=(gg == 3))
                    P.op("dve", lambda e, ps=ps, t1=t1, hh=hh: e.tensor_tensor(out=t1[:, hh * 4:(hh + 1) * 4, :], in0=ps[:, :].rearrange("p (g q) -> p g q", g=4),
                                                                              in1=bsb[:, hh * 512:(hh + 1) * 512].rearrange("p (g q) -> p g q", g=4), op=ALU.add),
                         reads=[ps, bsb], writes=[t1])
                P.op("dve", lambda e, t1=t1, gu=gu: e.tensor_tensor(out=t1[:], in0=t1[:], in1=gu[:], op=ALU.mult), reads=[t1, gu], writes=[t1])
                P.op("dve", lambda e, t1=t1, sz=sz, ob=ob: e.tensor_tensor(out=ob[:], in0=t1[:], in1=sz[:], op=ALU.mult), reads=[t1, sz], writes=[ob])
                P.dma(self.scr["brT"][2 * BR:3 * BR, tok:tok + 128].rearrange("(g d) t -> d g t", d=128), ob[:], reads=[ob],
                      writes=[P.region("brT", 2, n, f) for f in range(8)])

    def phase3(self, l, sn, L, xsrc, xsrc_region, last):
        P = self.P
        S3 = min(512, L)
        NT3 = S3 // 128
        w_in = self.inp["w_in"][l]
        wbr = self.inp["w_branch"][l]
        wout = self.inp["w_out"][l]
        with Scope(P) as S:
            hT = S.sb("hT", [128, 16, S3], BF16)
            gt = S.sb("gt", [128, D], F32)
            fg = S.sb("fg", [128, D], F32)
            tp = self.norm_pools(S)
            xall = [S.sb("xa%d" % i, [128, D], F32) for i in range(NT3)]
            hbpool = S.pool("hb", [128, D], BF16, 2)
            pstp = S.pspool("pst", 2, BF16)
            psg = S.pspool("psg", 2, F32)
            psb = S.pspool("psb", 2, F32)
            pso = S.pspool("pso", 2, F32)
            wgp = S.pool("wg", [128, 16, 128], BF16, 3)
            wbp = S.pool("wbb", [128, 8, 128], BF16, 3)
            wop = S.pool("wo", [128, 16, 256], BF16, 2)
            btp = S.pool("bt", [128, 8, S3], BF16, 2)
            sgp = S.pool("sg", [128, S3], F32, 2)
            tmp = S.pool("tmp", [128, S3], F32, 2)
            merged = S.sb("merged", [128, 16, S3], F32)
            mergedb = S.sb("mergedb", [128, 16, S3], BF16)
            yop = S.pool("yo", [128, D], F32, 1)
            P.dma(gt[:], self.inp["ln_g"][l].partition_broadcast(128), writes=[gt])
            if last:
                P.dma(fg[:], self.inp["final_g"].partition_broadcast(128), writes=[fg])
            for seg in range(L // S3):
                t0 = seg * S3
                xr = Rot(xall)
                self.make_hT(S, lambda tt: xsrc[t0 + tt * 128:t0 + (tt + 1) * 128, :], gt, hT, NT3, tp, xr, hbpool, pstp,
                             lambda tt: [xsrc_region(t0 + tt * 128)])
                for c in range(5):
                    bt = btp.next()
                    P.dma(bt[:], self.scr["brT"][c * BR:(c + 1) * BR, t0:t0 + S3].rearrange("(k p) t -> p k t", p=128),
                          reads=[P.region("brT", c, t0 // 128 + i, f) for i in range(NT3) for f in range(8)], writes=[bt])
                    for f in range(16):
                        wg = wgp.next(); wb = wbp.next()
                        co = OFF["gates"] + c * D + f * 128
                        P.dma(wg[:], w_in[:, co:co + 128].rearrange("(k p) c -> p k c", p=128), writes=[wg], q="pool")
                        P.dma(wb[:], wbr[c][:, f * 128:(f + 1) * 128].rearrange("(k p) c -> p k c", p=128), writes=[wb], q="pool")
                        pg = psg.next(); pb = psb.next()
                        for k in range(16):
                            P.op("pe", lambda e, k=k, pg=pg, wg=wg: e.matmul(pg[:, 0:S3], lhsT=wg[:, k, :], rhs=hT[:, k, :], start=(k == 0), stop=(k == 15)),
                                 reads=[wg, hT], writes=[pg], inc=(k == 15))
                        for k in range(8):
                            P.op("pe", lambda e, k=k, pb=pb, wb=wb, bt=bt: e.matmul(pb[:, 0:S3], lhsT=wb[:, k, :], rhs=bt[:, k, :], start=(k == 0), stop=(k == 7)),
                                 reads=[wb, bt], writes=[pb], inc=(k == 7))
                        sg = sgp.next()
                        P.op("act", lambda e, sg=sg, pg=pg: e.activation(out=sg[:], in_=pg[:, 0:S3], func=AF.Sigmoid), reads=[pg], writes=[sg])
                        if c == 0:
                            P.op("dve", lambda e, sg=sg, pb=pb, f=f: e.tensor_tensor(out=merged[:, f, :], in0=pb[:, 0:S3], in1=sg[:], op=ALU.mult),
                                 reads=[sg, pb], writes=[merged])
                        else:
                            tm = tmp.next()
                            P.op("dve", lambda e, sg=sg, pb=pb, tm=tm: e.tensor_tensor(out=tm[:], in0=pb[:, 0:S3], in1=sg[:], op=ALU.mult),
                                 reads=[sg, pb], writes=[tm])
                            P.op("dve", lambda e, tm=tm, f=f: e.tensor_tensor(out=merged[:, f, :], in0=merged[:, f, :], in1=tm[:], op=ALU.add),
                                 reads=[tm, merged], writes=[merged])
                for f4 in range(4):
                    self.copy("act", mergedb[:, f4 * 4:(f4 + 1) * 4, :], merged[:, f4 * 4:(f4 + 1) * 4, :], [merged], [mergedb])
                for nb in range(8):
                    wo = wop.next()
                    P.dma(wo[:], wout[:, nb * 256:(nb + 1) * 256].rearrange("(k p) c -> p k c", p=128), writes=[wo], q="pool")
                    for tt in range(NT3):
                        po = pso.next()
                        for k in range(16):
                            P.op("pe", lambda e, k=k, po=po, wo=wo, tt=tt: e.matmul(po[:, 0:256], lhsT=mergedb[:, k, tt * 128:(tt + 1) * 128], rhs=wo[:, k, :],
                                                                                     start=(k == 0), stop=(k == 15)), reads=[mergedb, wo], writes=[po], inc=(k == 15))
                        xt = xall[tt]
                        P.op("dve", lambda e, po=po, xt=xt, nb=nb: e.tensor_tensor(out=xt[:, nb * 256:(nb + 1) * 256], in0=po[:, 0:256], in1=xt[:, nb * 256:(nb + 1) * 256], op=ALU.add),
                             reads=[po, xt], writes=[xt])
                for tt in range(NT3):
                    tok = t0 + tt * 128
                    xt = xall[tt]
                    if not last:
                        P.dma(self.scr["xres"][tok:tok + 128, :], xt[:], reads=[xt], writes=[P.region("xres", tok // 128)])
                    else:
                        yo = yop.next()
                        self.rmsnorm_tile(xt, fg, yo, tp)
                        P.dma(self.out[sn][tok:tok + 128, :], yo[:], reads=[yo], is_out=True)

    def zero_branch(self, c, L):
        P = self.P
        with Scope(P) as S:
            z = S.sb("z", [128, 8, 128], BF16)
            P.op("pool", lambda e: e.memset(z[:], 0.0), writes=[z])
            for n in range(L // 128):
                P.dma(self.scr["brT"][c * BR:(c + 1) * BR, n * 128:(n + 1) * 128].rearrange("(g d) t -> d g t", d=128), z[:], reads=[z],
                      writes=[P.region("brT", c, n, f) for f in range(8)])

    def build(self):
        P = self.P
        with Scope(P) as S0:
            self.consts(S0)
            for sn, L in self.seqs:
                for l in range(self.depth):
                    if l == 0:
                        xsrc = self.inp["x_" + sn]
                        xreg = lambda tok: P.region("xin", sn, tok // 128)
                    else:
                        xsrc = self.scr["xres"]
                        xreg = lambda tok: P.region("xres", tok // 128)
                    self.phase1(l, sn, L, xsrc, xreg)
                    for c in range(5):
                        if c not in self.branches:
                            self.zero_branch(c, L)
                    if 2 in self.branches:
                        self.sgu(l, L)
                    if 1 in self.branches:
                        self.s5(l, L)
                    if 3 in self.branches:
                        self.deltanet(l, L)
                    if 4 in self.branches:
                        self.xattn(l, L, sn)
                    if 0 in self.branches:
                        self.attention(l, L)
                    self.phase3(l, sn, L, xsrc, xreg, last=(l == self.depth - 1))
        P.emit()


def _sincos(k, S, x, W, name):
    P = k.P
    xx = S.sb(name + "_xx", [128, 2, W], F32)
    u = S.sb(name + "_u", [128, 2, W], F32)
    ki = S.sb(name + "_ki", [128, 2, W], I32)
    kf = S.sb(name + "_kf", [128, 2, W], F32)
    out = S.sb(name + "_sc", [128, 2, W], F32)
    P.op("dve", lambda e: e.tensor_copy(out=xx[:, 0, :], in_=x[:]), reads=[x], writes=[xx])
    P.op("dve", lambda e: e.tensor_scalar(out=xx[:, 1, :], in0=x[:], scalar1=math.pi / 2, scalar2=None, op0=ALU.add), reads=[x, xx], writes=[xx])
    _reduce_sin(k, xx, u, ki, kf, out)
    return out


def _reduce_sin(k, xx, u, ki, kf, out):
    P = k.P
    C1 = float(np.float32(TWO_PI))
    C2 = float(TWO_PI - np.float64(np.float32(TWO_PI)))
    P.op("dve", lambda e: e.tensor_scalar(out=u[:], in0=xx[:], scalar1=1.0 / TWO_PI, scalar2=None, op0=ALU.mult), reads=[xx], writes=[u])
    P.op("dve", lambda e: e.tensor_copy(out=ki[:], in_=u[:]), reads=[u], writes=[ki])
    P.op("dve", lambda e: e.tensor_copy(out=kf[:], in_=ki[:]), reads=[ki], writes=[kf])
    P.op("dve", lambda e: e.scalar_tensor_tensor(out=u[:], in0=kf[:], scalar=-C1, in1=xx[:], op0=ALU.mult, op1=ALU.add), reads=[kf, xx], writes=[u])
    P.op("dve", lambda e: e.scalar_tensor_tensor(out=u[:], in0=kf[:], scalar=-C2, in1=u[:], op0=ALU.mult, op1=ALU.add), reads=[kf, u], writes=[u])
    P.op("dve", lambda e: e.tensor_scalar(out=kf[:], in0=u[:], scalar1=math.pi, scalar2=-TWO_PI, op0=ALU.is_gt, op1=ALU.mult), reads=[u], writes=[kf])
    P.op("dve", lambda e: e.tensor_tensor(out=u[:], in0=u[:], in1=kf[:], op=ALU.add), reads=[u, kf], writes=[u])
    P.op("dve", lambda e: e.tensor_scalar(out=kf[:], in0=u[:], scalar1=-math.pi, scalar2=TWO_PI, op0=ALU.is_lt, op1=ALU.mult), reads=[u], writes=[kf])
    P.op("dve", lambda e: e.tensor_tensor(out=u[:], in0=u[:], in1=kf[:], op=ALU.add), reads=[u, kf], writes=[u])
    P.op("dve", lambda e: e.tensor_scalar(out=u[:], in0=u[:], scalar1=math.pi, scalar2=-math.pi, op0=ALU.min, op1=ALU.max), reads=[u], writes=[u])
    P.op("act", lambda e: e.activation(out=out[:], in_=u[:], func=AF.Sin), reads=[u], writes=[out])


def s5(self, l, L):
    P = self.P
    idf = self.c["identf"]
    Tc = min(512, L)
    NCH = L // Tc
    inp = self.inp
    with Scope(P) as S:
        pst = S.pspool("pst", 2, F32)
        psr = S.pspool("psr", 1, F32)
        psi = S.pspool("psi", 1, F32)
        pgr = S.pspool("pgr", 1, F32)
        pgi = S.pspool("pgi", 1, F32)
        psy = S.pspool("psy", 2, F32)
        prm = []
        for d in range(2):
            lam = {}
            for nm, src in (("lr", inp["ssm_lam_re"]), ("li", inp["ssm_lam_im"])):
                nat = S.sb("nat" + nm, [32, 128], F32)
                P.dma(nat[:], src[l, d].rearrange("(j a) n -> j (a n)", a=2), writes=[nat])
                pt = pst.next()
                P.op("pe", lambda e, pt=pt, nat=nat: e.transpose(out=pt[:, 0:32], in_=nat[:, :], identity=idf[0:32, 0:32]), reads=[nat, idf], writes=[pt])
                t = S.sb(nm + str(d), [128, 32], F32)
                self.copy("dve", t[:], pt[:, 0:32], [pt], [t])
                lam[nm] = t
            ldtb = S.sb("ldtb", [128, 64], F32)
            P.dma(ldtb[:], inp["ssm_log_dt"][l, d].partition_broadcast(128), writes=[ldtb])
            ldt = S.sb("ldt", [128, 32], F32)
            P.op("dve", lambda e, ldt=ldt, ldtb=ldtb: e.tensor_copy(out=ldt[0:64, :], in_=ldtb[0:64, 0:64:2]), reads=[ldtb], writes=[ldt])
            P.op("dve", lambda e, ldt=ldt, ldtb=ldtb: e.tensor_copy(out=ldt[64:128, :], in_=ldtb[64:128, 1:64:2]), reads=[ldtb, ldt], writes=[ldt])
            dt = S.sb("dt", [128, 32], F32)
            P.op("act", lambda e, dt=dt, ldt=ldt: e.activation(out=dt[:], in_=ldt[:], func=AF.Exp), reads=[ldt], writes=[dt])
            lr, li = lam["lr"], lam["li"]
            rho = S.sb("rho%d" % d, [128, 32], F32)
            th = S.sb("th%d" % d, [128, 32], F32)
            tmp = S.sb("tmpa", [128, 32], F32)
            P.op("dve", lambda e, tmp=tmp, lr=lr, dt=dt: e.tensor_tensor(out=tmp[:], in0=lr[:], in1=dt[:], op=ALU.mult), reads=[lr, dt], writes=[tmp])
            P.op("act", lambda e, tmp=tmp, rho=rho: e.activation(out=rho[:], in_=tmp[:], func=AF.Exp), reads=[tmp], writes=[rho])
            P.op("dve", lambda e, th=th, li=li, dt=dt: e.tensor_tensor(out=th[:], in0=li[:], in1=dt[:], op=ALU.mult), reads=[li, dt], writes=[th])
            sc = _sincos(self, S, th, 32, "p%d" % d)
            ar = S.sb("ar", [128, 32], F32); ai = S.sb("ai", [128, 32], F32); am1 = S.sb("am1", [128, 32], F32)
            P.op("dve", lambda e, ar=ar, rho=rho, sc=sc: e.tensor_tensor(out=ar[:], in0=rho[:], in1=sc[:, 1, :], op=ALU.mult), reads=[rho, sc], writes=[ar])
            P.op("dve", lambda e, ai=ai, rho=rho, sc=sc: e.tensor_tensor(out=ai[:], in0=rho[:], in1=sc[:, 0, :], op=ALU.mult), reads=[rho, sc], writes=[ai])
            P.op("dve", lambda e, ar=ar, am1=am1: e.tensor_scalar(out=am1[:], in0=ar[:], scalar1=-1.0, scalar2=None, op0=ALU.add), reads=[ar], writes=[am1])
            den = S.sb("den", [128, 32], F32); t2 = S.sb("t2", [128, 32], F32)
            P.op("dve", lambda e, den=den, lr=lr: e.tensor_tensor(out=den[:], in0=lr[:], in1=lr[:], op=ALU.mult), reads=[lr], writes=[den])
            P.op("dve", lambda e, t2=t2, li=li: e.tensor_tensor(out=t2[:], in0=li[:], in1=li[:], op=ALU.mult), reads=[li], writes=[t2])
            P.op("dve", lambda e, den=den, t2=t2: e.tensor_tensor(out=den[:], in0=den[:], in1=t2[:], op=ALU.add), reads=[den, t2], writes=[den])
            rden = S.sb("rden", [128, 32], F32)
            P.op("dve", lambda e, den=den, rden=rden: e.reciprocal(out=rden[:], in_=den[:]), reads=[den], writes=[rden])
            fr = S.sb("fr%d" % d, [128, 32], F32); fi = S.sb("fi%d" % d, [128, 32], F32); nfi = S.sb("nfi%d" % d, [128, 32], F32)
            P.op("dve", lambda e, t2=t2, am1=am1, lr=lr: e.tensor_tensor(out=t2[:], in0=am1[:], in1=lr[:], op=ALU.mult), reads=[am1, lr, t2], writes=[t2])
            P.op("dve", lambda e, tmp=tmp, ai=ai, li=li: e.tensor_tensor(out=tmp[:], in0=ai[:], in1=li[:], op=ALU.mult), reads=[ai, li, tmp], writes=[tmp])
            P.op("dve", lambda e, t2=t2, tmp=tmp: e.tensor_tensor(out=t2[:], in0=t2[:], in1=tmp[:], op=ALU.add), reads=[t2, tmp], writes=[t2])
            P.op("dve", lambda e, t2=t2, fr=fr, rden=rden: e.tensor_tensor(out=fr[:], in0=t2[:], in1=rden[:], op=ALU.mult), reads=[t2, rden], writes=[fr])
            P.op("dve", lambda e, t2=t2, ai=ai, lr=lr: e.tensor_tensor(out=t2[:], in0=ai[:], in1=lr[:], op=ALU.mult), reads=[ai, lr, t2], writes=[t2])
            P.op("dve", lambda e, tmp=tmp, am1=am1, li=li: e.tensor_tensor(out=tmp[:], in0=am1[:], in1=li[:], op=ALU.mult), reads=[am1, li, tmp], writes=[tmp])
            P.op("dve", lambda e, t2=t2, tmp=tmp: e.tensor_tensor(out=t2[:], in0=t2[:], in1=tmp[:], op=ALU.subtract), reads=[t2, tmp], writes=[t2])
            P.op("dve", lambda e, t2=t2, fi=fi, rden=rden: e.tensor_tensor(out=fi[:], in0=t2[:], in1=rden[:], op=ALU.mult), reads=[t2, rden], writes=[fi])
            P.op("dve", lambda e, nfi=nfi, fi=fi: e.tensor_scalar(out=nfi[:], in0=fi[:], scalar1=-1.0, scalar2=None, op0=ALU.mult), reads=[fi], writes=[nfi])
            prm.append(dict(rho=rho, th=th, fr=fr, fi=fi, nfi=nfi))
        iot = S.sb("iot", [128, Tc], F32)
        P.op("pool", lambda e: e.iota(iot[:], pattern=[[1, Tc]], base=1, channel_multiplier=0, allow_small_or_imprecise_dtypes=True), writes=[iot])
        dcol = S.sb("dcol", [128, 8], F32)
        P.dma(dcol[:], inp["ssm_d"][l].rearrange("(o p) -> p o", p=128), writes=[dcol], slow=True)
        Bpad = [[S.sb("Bpad%d%d" % (j, ri), [128, 128], F32) for ri in range(2)] for j in range(4)]
        Cpad = [[S.sb("Cpad%d%d" % (j, ri), [128, 128], F32) for ri in range(2)] for j in range(4)]
        for j in range(4):
            for ri in range(2):
                for t in (Bpad[j][ri], Cpad[j][ri]):
                    P.op("pool", lambda e, t=t: e.memset(t[:], 0.0), writes=[t])
        BT = [[[S.sb("BT%d%d%d" % (j, d, ri), [128, 128], F32) for ri in range(2)] for d in range(2)] for j in range(4)]
        CT = [[[S.sb("CT%d%d%d" % (j, d, ri), [128, 128], F32) for ri in range(3)] for d in range(2)] for j in range(4)]
        tabs = [[S.sb("tab%d%d" % (j, d), [128, 2, Tc], F32) for d in range(2)] for j in range(4)]
        rhoT = [[S.sb("rhoT%d%d" % (j, d), [128, Tc], F32) for d in range(2)] for j in range(4)]
        state = [[S.sb("state%d%d" % (j, d), [128, 2], F32) for d in range(2)] for j in range(4)]
        tb_xx = S.sb("tb_xx", [128, 2, Tc], F32); tb_u = S.sb("tb_u", [128, 2, Tc], F32)
        tb_ki = S.sb("tb_ki", [128, 2, Tc], I32); tb_kf = S.sb("tb_kf", [128, 2, Tc], F32)
        bnat = S.pool("bnat", [128, 2, 16], F32, 2)
        bw = S.pool("bw", [128, 2, 16], F32, 2)
        yacc = S.sb("yacc", [128, L], F32)
        up = S.pool("u", [128, Tc], F32, 3)
        wk = {n: S.pool(n, [128, Tc], F32, 2) for n in ("t1", "t2", "re", "t3", "t4", "im", "h1", "h2", "h3", "h4")}
        gp = {n: S.pool("g" + n, [128, Tc], F32, 1) for n in ("y", "y2", "p", "in", "s")}
        syb = S.pool("syb", [128, Tc], BF16, 2)
        for o in range(8):
            for d in range(2):
                pr = prm[d]
                for j in range(4):
                    J = 4 * o + j
                    bn = bnat.next()
                    P.dma(bn[:, 0, :], inp["ssm_b_re"][l, d].rearrange("g n p -> (g n) p")[J * 128:(J + 1) * 128, :], writes=[bn])
                    P.dma(bn[:, 1, :], inp["ssm_b_im"][l, d].rearrange("g n p -> (g n) p")[J * 128:(J + 1) * 128, :], writes=[bn])
                    b2 = bw.next()
                    frc, fic, nfic = pr["fr"][:, J:J + 1], pr["fi"][:, J:J + 1], pr["nfi"][:, J:J + 1]
                    P.op("dve", lambda e, b2=b2, bn=bn, frc=frc: e.tensor_scalar(out=b2[:], in0=bn[:], scalar1=frc, scalar2=None, op0=ALU.mult), reads=[bn, pr["fr"]], writes=[b2])
                    for hh, c0 in ((0, (2 * j) * 16), (1, (2 * j + 1) * 16)):
                        ps_ = slice(hh * 64, (hh + 1) * 64)
                        P.op("dve", lambda e, b2=b2, bn=bn, ps_=ps_, c0=c0, j=j, sc_=pr["nfi"][ps_, J:J + 1]: e.scalar_tensor_tensor(
                            out=Bpad[j][0][ps_, c0:c0 + 16], in0=bn[ps_, 1, :], scalar=sc_, in1=b2[ps_, 0, :], op0=ALU.mult, op1=ALU.add),
                            reads=[bn, b2, pr["nfi"]], writes=[Bpad[j][0]])
                        P.op("dve", lambda e, b2=b2, bn=bn, ps_=ps_, c0=c0, j=j, sc_=pr["fi"][ps_, J:J + 1]: e.scalar_tensor_tensor(
                            out=Bpad[j][1][ps_, c0:c0 + 16], in0=bn[ps_, 0, :], scalar=sc_, in1=b2[ps_, 1, :], op0=ALU.mult, op1=ALU.add),
                            reads=[bn, b2, pr["fi"]], writes=[Bpad[j][1]])
                    for ri in range(2):
                        pt = pst.next()
                        P.op("pe", lambda e, pt=pt, j=j, ri=ri: e.transpose(out=pt[:, 0:128], in_=Bpad[j][ri][:, :], identity=idf[:]), reads=[Bpad[j][ri], idf], writes=[pt])
                        self.copy("act", BT[j][d][ri][:], pt[:, 0:128], [pt], [BT[j][d][ri]])
                    for ri, src in ((0, inp["ssm_c_re"]), (1, inp["ssm_c_im"])):
                        g0 = 8 * o + 2 * j
                        P.dma(Cpad[j][ri][(2 * j) * 16:(2 * j) * 16 + 16, 0:64], src[l, d, g0], writes=[Cpad[j][ri]])
                        P.dma(Cpad[j][ri][(2 * j + 1) * 16:(2 * j + 1) * 16 + 16, 64:128], src[l, d, g0 + 1], writes=[Cpad[j][ri]])
                        pt = pst.next()
                        P.op("pe", lambda e, pt=pt, j=j, ri=ri: e.transpose(out=pt[:, 0:128], in_=Cpad[j][ri][:, :], identity=idf[:]), reads=[Cpad[j][ri], idf], writes=[pt])
                        ct = CT[j][d][ri]
                        if ri == 0:
                            self.copy("act", ct[:], pt[:, 0:128], [pt], [ct])
                            ct2 = CT[j][d][2]
                            P.op("act", lambda e, ct2=ct2, pt=pt: e.activation(out=ct2[:], in_=pt[:, 0:128], func=AF.Copy, scale=-1.0), reads=[pt], writes=[ct2])
                        else:
                            P.op("act", lambda e, ct=ct, pt=pt: e.activation(out=ct[:], in_=pt[:, 0:128], func=AF.Copy, scale=-1.0), reads=[pt], writes=[ct])
                    P.op("dve", lambda e, sc_=pr["th"][:, J:J + 1]: e.tensor_scalar(out=tb_xx[:, 0, :], in0=iot[:], scalar1=sc_, scalar2=None, op0=ALU.mult),
                         reads=[iot, pr["th"]], writes=[tb_xx])
                    P.op("dve", lambda e: e.tensor_scalar(out=tb_xx[:, 1, :], in0=tb_xx[:, 0, :], scalar1=math.pi / 2, scalar2=None, op0=ALU.add),
                         reads=[tb_xx], writes=[tb_xx])
                    _reduce_sin(self, tb_xx, tb_u, tb_ki, tb_kf, tabs[j][d])
                    rt = rhoT[j][d]
                    P.op("dve", lambda e, rt=rt, sc_=pr["rho"][:, J:J + 1]: e.tensor_scalar(out=rt[:], in0=iot[:], scalar1=0.0, scalar2=sc_, op0=ALU.mult, op1=ALU.add),
                         reads=[iot, pr["rho"]], writes=[rt])
                    P.op("pool", lambda e, j=j, d=d: e.memset(state[j][d][:], 0.0), writes=[state[j][d]])
            for d in range(2):
                order = range(NCH) if d == 0 else range(NCH - 1, -1, -1)
                for ch in order:
                    t0 = ch * Tc
                    u = up.next()
                    seg = t0 // min(L, 2048)
                    P.dma(u[:], self.scr["su"][o * 128:(o + 1) * 128, t0:t0 + Tc], reads=[P.region("su", o, seg)], writes=[u])
                    uap = u[:, :] if d == 0 else u[:, ::-1]
                    yp = psy.next()
                    for j in range(4):
                        tab = tabs[j][d]; rt = rhoT[j][d]; st = state[j][d]
                        sn_, cs_ = tab[:, 0, :], tab[:, 1, :]
                        pr_, pi_ = psr.next(), psi.next()
                        P.op("pe", lambda e, pr_=pr_, j=j, d=d, uap=uap: e.matmul(pr_[:, 0:Tc], lhsT=BT[j][d][0][:], rhs=uap, start=True, stop=True), reads=[BT[j][d][0], u], writes=[pr_])
                        P.op("pe", lambda e, pi_=pi_, j=j, d=d, uap=uap: e.matmul(pi_[:, 0:Tc], lhsT=BT[j][d][1][:], rhs=uap, start=True, stop=True), reads=[BT[j][d][1], u], writes=[pi_])
                        w = {n: wk[n].next() for n in wk}

                        def tt(out, a, b, op, ra, rb, eng="dve"):
                            P.op(eng, lambda e: e.tensor_tensor(out=out[:], in0=a, in1=b, op=op), reads=ra + rb, writes=[out])
                        tt(w["t1"], pr_[:, 0:Tc], cs_, ALU.mult, [pr_], [tab])
                        tt(w["t2"], pi_[:, 0:Tc], sn_, ALU.mult, [pi_], [tab])
                        tt(w["re"], w["t1"][:], w["t2"][:], ALU.add, [w["t1"]], [w["t2"]])
                        tt(w["t3"], pi_[:, 0:Tc], cs_, ALU.mult, [pi_], [tab])
                        tt(w["t4"], pr_[:, 0:Tc], sn_, ALU.mult, [pr_], [tab])
                        tt(w["im"], w["t3"][:], w["t4"][:], ALU.subtract, [w["t3"]], [w["t4"]])
                        gr_, gi_ = pgr.next(), pgi.next()
                        P.op("dve", lambda e, w=w, rt=rt, st=st, gr_=gr_: e.tensor_tensor_scan(out=gr_[:, 0:Tc], data0=rt[:], data1=w["re"][:], initial=st[:, 0:1], op0=ALU.mult, op1=ALU.add),
                             reads=[rt, w["re"], st], writes=[gr_])
                        P.op("dve", lambda e, w=w, rt=rt, st=st, gi_=gi_: e.tensor_tensor_scan(out=gi_[:, 0:Tc], data0=rt[:], data1=w["im"][:], initial=st[:, 1:2], op0=ALU.mult, op1=ALU.add),
                             reads=[rt, w["im"], st], writes=[gi_])
                        tt(w["h1"], gr_[:, 0:Tc], cs_, ALU.mult, [gr_], [tab])
                        tt(w["h2"], gi_[:, 0:Tc], sn_, ALU.mult, [gi_], [tab])
                        tt(w["h3"], gr_[:, 0:Tc], sn_, ALU.mult, [gr_], [tab])
                        tt(w["h4"], gi_[:, 0:Tc], cs_, ALU.mult, [gi_], [tab])
                        self.tt(st[:, 0:1], w["h1"][:, Tc - 1:Tc], w["h2"][:, Tc - 1:Tc], ALU.subtract, [w["h1"], w["h2"]], [st])
                        self.tt(st[:, 1:2], w["h3"][:, Tc - 1:Tc], w["h4"][:, Tc - 1:Tc], ALU.add, [w["h3"], w["h4"], st], [st])
                        rv = (lambda b: b[:, :]) if d == 0 else (lambda b: b[:, ::-1])
                        self.mm(yp[:, 0:Tc], CT[j][d][0][:], rv(w["h1"]), [CT[j][d][0], w["h1"]], [yp], start=(j == 0), stop=False, inc=False)
                        self.mm(yp[:, 0:Tc], CT[j][d][2][:], rv(w["h2"]), [CT[j][d][2], w["h2"]], [yp], start=False, stop=False, inc=False)
                        self.mm(yp[:, 0:Tc], CT[j][d][1][:], rv(w["h3"]), [CT[j][d][1], w["h3"]], [yp], start=False, stop=False, inc=False)
                        self.mm(yp[:, 0:Tc], CT[j][d][1][:], rv(w["h4"]), [CT[j][d][1], w["h4"]], [yp], start=False, stop=(j == 3), inc=True)
                    if d == 0:
                        self.copy("act", yacc[:, t0:t0 + Tc], yp[:, 0:Tc], [yp], [yacc])
                        pass
                    else:
                        P.op("dve", lambda e, yp=yp, t0=t0: e.tensor_tensor(out=yacc[:, t0:t0 + Tc], in0=yp[:, 0:Tc], in1=yacc[:, t0:t0 + Tc], op=ALU.add), reads=[yp, yacc], writes=[yacc])
            if self.debug:
                P.dma(self.scr["dbg"][o * 128:(o + 1) * 128, 0:L], yacc[:, :], reads=[yacc])
            for ch in range(NCH):
                t0 = ch * Tc
                u = up.next()
                seg = t0 // min(L, 2048)
                P.dma(u[:], self.scr["su"][o * 128:(o + 1) * 128, t0:t0 + Tc], reads=[P.region("su", o, seg)], writes=[u])
                y = gp["y"].next(); y2 = gp["y2"].next(); pp = gp["p"].next(); inn = gp["in"].next(); sg = gp["s"].next(); ob = syb.next()
                P.op("dve", lambda e, u=u, y=y, t0=t0, o=o: e.scalar_tensor_tensor(out=y[:], in0=u[:], scalar=dcol[:, o:o + 1], in1=yacc[:, t0:t0 + Tc], op0=ALU.mult, op1=ALU.add),
                     reads=[u, dcol, yacc], writes=[y])
                P.op("dve", lambda e, y=y, y2=y2: e.tensor_tensor(out=y2[:], in0=y[:], in1=y[:], op=ALU.mult), reads=[y], writes=[y2])
                P.op("dve", lambda e, y2=y2, pp=pp: e.tensor_scalar(out=pp[:], in0=y2[:], scalar1=0.044715, scalar2=1.0, op0=ALU.mult, op1=ALU.add), reads=[y2], writes=[pp])
                P.op("dve", lambda e, pp=pp, y=y, inn=inn: e.tensor_tensor(out=inn[:], in0=pp[:], in1=y[:], op=ALU.mult), reads=[pp, y], writes=[inn])
                P.op("act", lambda e, inn=inn, sg=sg: e.activation(out=sg[:], in_=inn[:], func=AF.Sigmoid, scale=2.0 * math.sqrt(2.0 / math.pi)), reads=[inn], writes=[sg])
                P.op("dve", lambda e, sg=sg, y=y, ob=ob: e.tensor_tensor(out=ob[:], in0=sg[:], in1=y[:], op=ALU.mult), reads=[sg, y], writes=[ob])
                P.dma(self.scr["syT"][o * 128:(o + 1) * 128, t0:t0 + Tc], ob[:], reads=[ob], writes=[P.region("syT", o, ch)])
    with Scope(P) as S:
        gw = S.sb("gw", [128, 8, BR], BF16)
        gb = S.sb("gb", [128, 8], F32)
        P.dma(gw[:], inp["ssm_glu_w"][l].rearrange("(k p) c -> p k c", p=128), writes=[gw], q="pool")
        P.dma(gb[:], inp["ssm_glu_b"][l].rearrange("(f p) -> p f", p=128), writes=[gb], slow=True)
        syp = S.pool("sy", [128, 8, Tc], BF16, 2)
        szp = S.pool("sz", [128, Tc], F32, 2)
        szs = S.pool("szs", [128, Tc], F32, 2)
        sgp = S.pool("sg", [128, Tc], F32, 2)
        t1p = S.pool("t1", [128, Tc], F32, 2)
        obp = S.pool("ob", [128, Tc], BF16, 2)
        psm = S.pspool("psm", 3, F32)
        for ch in range(NCH):
            t0 = ch * Tc
            sy = syp.next()
            P.dma(sy[:], self.scr["syT"][:, t0:t0 + Tc].rearrange("(k p) t -> p k t", p=128), reads=[P.region("syT", o, ch) for o in range(8)], writes=[sy])
            seg = t0 // min(L, 2048)
            for f in range(8):
                ps = psm.next()
                for k_ in range(8):
                    P.op("pe", lambda e, ps=ps, k_=k_, f=f, sy=sy: e.matmul(ps[:, 0:Tc], lhsT=gw[:, k_, f * 128:(f + 1) * 128], rhs=sy[:, k_, :], start=(k_ == 0), stop=(k_ == 7)),
                         reads=[gw, sy], writes=[ps], inc=(k_ == 7))
                sz = szp.next(); zs = szs.next(); sg = sgp.next(); t1 = t1p.next(); ob = obp.next()
                P.dma(sz[:], self.scr["sz"][f * 128:(f + 1) * 128, t0:t0 + Tc], reads=[P.region("sz", f, seg)], writes=[sz])
                P.op("act", lambda e, sg=sg, ps=ps, f=f: e.activation(out=sg[:], in_=ps[:, 0:Tc], func=AF.Sigmoid, bias=gb[:, f:f + 1]), reads=[ps, gb], writes=[sg])
                P.op("act", lambda e, sz=sz, zs=zs: e.activation(out=zs[:], in_=sz[:], func=AF.Silu), reads=[sz], writes=[zs])
                P.op("dve", lambda e, sg=sg, sy=sy, t1=t1, f=f: e.tensor_tensor(out=t1[:], in0=sg[:], in1=sy[:, f, :], op=ALU.mult), reads=[sg, sy], writes=[t1])
                P.op("dve", lambda e, t1=t1, zs=zs, ob=ob: e.tensor_tensor(out=ob[:], in0=t1[:], in1=zs[:], op=ALU.mult), reads=[t1, zs], writes=[ob])
                for q4 in range(Tc // 128):
                    pass
                P.dma(self.scr["brT"][BR + f * 128:BR + (f + 1) * 128, t0:t0 + Tc], ob[:], reads=[ob],
                      writes=[P.region("brT", 1, t0 // 128 + i, f) for i in range(Tc // 128)])


K.s5 = s5


BRANCHES = (0, 1, 2, 3, 4)
_CACHE = {}


def kernel(**inputs):
    inputs = {k_: np.ascontiguousarray(np.asarray(v)) for k_, v in inputs.items()}
    nc = bass.Bass("TRN2", target_bir_lowering=False)
    kk = K(nc, [("p", 2048), ("s", 8192)], depth=2, debug=False, branches=BRANCHES)
    kk.build()
    wnames = [n for n in inputs if n not in ("x_prompt", "x_sample", "mem_prompt", "mem_sample")]
    in_maps = []
    for c in range(8):
        m = {n: inputs[n] for n in wnames}
        m["x_p"] = inputs["x_prompt"][c]
        m["mem_p"] = inputs["mem_prompt"][c]
        m["x_s"] = inputs["x_sample"][c // 4]
        m["mem_s"] = inputs["mem_sample"][c // 4]
        in_maps.append(m)
    res = run_bass_kernel_spmd(nc, in_maps, core_ids=list(range(8)))
    y_p = np.stack([np.asarray(res.results[c]["y_p"], dtype=np.float32) for c in range(8)])
    y_s = np.stack([np.asarray(res.results[c]["y_s"], dtype=np.float32) for c in (0, 4)])
    return (y_p, y_s)


def _tt(self, out, a, b, op, reads, writes, eng="dve"):
    self.P.op(eng, lambda e: e.tensor_tensor(out=out, in0=a, in1=b, op=op), reads=reads, writes=writes)


def _ts(self, out, a, s1, s2, op0, op1, reads, writes, eng="dve"):
    if s2 is None:
        self.P.op(eng, lambda e: e.tensor_scalar(out=out, in0=a, scalar1=s1, scalar2=None, op0=op0), reads=reads, writes=writes)
    else:
        self.P.op(eng, lambda e: e.tensor_scalar(out=out, in0=a, scalar1=s1, scalar2=s2, op0=op0, op1=op1), reads=reads, writes=writes)


def _stt(self, out, a, sc, b, op0, op1, reads, writes):
    self.P.op("dve", lambda e: e.scalar_tensor_tensor(out=out, in0=a, scalar=sc, in1=b, op0=op0, op1=op1), reads=reads, writes=writes)


def _mm(self, out, lhsT, rhs, reads, writes, start=True, stop=True, inc=True):
    self.P.op("pe", lambda e: e.matmul(out, lhsT=lhsT, rhs=rhs, start=start, stop=stop), reads=reads, writes=writes, inc=inc)


def _tr(self, out, in_, ident, reads, writes, inc=True):
    self.P.op("pe", lambda e: e.transpose(out=out, in_=in_, identity=ident), reads=reads, writes=writes, inc=inc)


def _act(self, out, in_, func, reads, writes, scale=1.0, bias=0.0):
    self.P.op("act", lambda e: e.activation(out=out, in_=in_, func=func, scale=scale, bias=bias), reads=reads, writes=writes)


def _cp(self, out, in_, reads, writes, eng=None):
    eng = eng or self.evac_eng()
    if eng == "act":
        self.P.op("act", lambda e: e.activation(out=out, in_=in_, func=AF.Copy), reads=reads, writes=writes)
    else:
        self.P.op(eng, lambda e: e.tensor_copy(out=out, in_=in_), reads=reads, writes=writes)


K.tt, K.ts, K.stt, K.mm, K.tr, K.act, K.cp = _tt, _ts, _stt, _mm, _tr, _act, _cp


class SlotBuf:
    psum = True

    def __init__(self, name, parent, ap):
        self.name = name
        self.parent = parent
        self.ap = ap

    def __getitem__(self, k):
        return self.ap[k]

    @property
    def last_w(self):
        return self.parent.last_w

    @last_w.setter
    def last_w(self, v):
        self.parent.last_w = v

    @property
    def readers(self):
        return self.parent.readers

    @readers.setter
    def readers(self, v):
        self.parent.readers = v


def _rr(gens):
    gens = list(gens)
    if RR_SEQ:
        for g in gens:
            for _ in g:
                pass
        return
    while gens:
        nxt = []
        for g in gens:
            try:
                next(g)
                nxt.append(g)
            except StopIteration:
                pass
        gens = nxt


def deltanet(self, l, L):
    P = self.P
    inp = self.inp
    idf = self.c["identf"]
    onesf = self.c["onesf"]
    NCHK = L // 128
    SEG = min(L, 2048)
    NH = 8
    with Scope(P) as S:
        LOW = S.sb("LOW", [128, 128], F32); UPP = S.sb("UPP", [128, 128], F32)
        SLOW = S.sb("SLOW", [128, 128], F32); SUPP = S.sb("SUPP", [128, 128], F32)
        for t, (cm, st, op) in ((LOW, (1, -1, ALU.is_ge)), (UPP, (-1, 1, ALU.is_ge)), (SLOW, (1, -1, ALU.is_gt)), (SUPP, (-1, 1, ALU.is_gt))):
            P.op("pool", lambda e, t=t: e.memset(t[:], 1.0), writes=[t])
            P.op("pool", lambda e, t=t, cm=cm, st=st, op=op: e.affine_select(out=t[:], in_=t[:], pattern=[[st, 128]], compare_op=op, fill=0.0, base=0, channel_multiplier=cm),
                 reads=[t], writes=[t])
        cw = S.sb("cw", [128, 3, 24], F32)
        for j_ in range(3):
            P.dma(cw[:, j_, :], inp["dn_conv_w"][l, j_].rearrange("(k p) -> p k", p=128), writes=[cw], slow=True)
        alb = S.sb("alb", [128, 16], F32); dtb = S.sb("dtb", [128, 16], F32); nea = S.sb("nea", [128, 16], F32)
        P.dma(alb[:], inp["dn_a_log"][l].rearrange("a h -> (a h)").partition_broadcast(128), writes=[alb])
        P.dma(dtb[:], inp["dn_dt_bias"][l].rearrange("a h -> (a h)").partition_broadcast(128), writes=[dtb])
        self.act(nea[:], alb[:], AF.Exp, [alb], [nea])
        self.ts(nea[:], nea[:], -1.0, None, ALU.mult, None, [nea], [nea])
        ng = S.sb("ng", [128, 128], F32)
        P.dma(ng[:], inp["dn_norm_g"][l].partition_broadcast(128), writes=[ng])
        slots = []
        for b in range(8):
            pb = S.ps("ppb%d" % b, F32)
            for q4 in range(4):
                slots.append(SlotBuf("pps%d_%d" % (b, q4), pb, pb.h[:, q4 * 128:(q4 + 1) * 128]))
        pp = Rot(slots)
        ppA = [Rot([slots[4 * (i // 3) + (i % 3)]]) for i in range(24)]
        ppH = [Rot(slots[4 * h:4 * h + 4]) for h in range(8)]
        xin = S.pool("xin", [128, 130], F32, 24)
        cvp = S.pool("cv", [128, 128], F32, 24)
        sqp = S.pool("sq", [128, 128], F32, 16)
        rsp = S.pool("rsq", [128, 128], F32, 16)
        qTp = S.pool("qT", [128, 128], F32, NH); kTp = S.pool("kT", [128, 128], F32, NH)
        ktp = S.pool("ktm", [128, 128], F32, NH); vtp = S.pool("vtm", [128, 128], F32, NH)
        dbp = S.pool("dba", [128, 32], F32, 2)
        sm = {n: S.pool(n, [128, 16], F32, 2) for n in ("beta", "nbeta", "x", "ax", "e", "ln", "mx", "g", "gcc", "gtot", "egc", "ekd", "glast", "cb", "dif")}
        wkn = ("gU", "egcb", "dm", "dec", "dst", "X", "Y", "X2", "Y2", "TT", "u", "wT", "att", "attT", "qgT", "kd", "vnew", "osb")
        alias = {"vb": "gU", "kbg": "dm", "of": "dst", "osum": "X", "junk": "Y", "on": "X2", "zt": "Y2", "zs": "dec"}
        wk = {n: S.pool(n, [128, 128], F32, NH) for n in wkn}
        obp = S.pool("obf", [128, 128], BF16, NH)
        ssp = S.pool("ss1", [128, 1], F32, NH); rs1 = S.pool("rs1", [128, 1], F32, NH); rstd1 = S.pool("rstd1", [128, 1], F32, NH)
        Sst = [[S.sb("S%d%d" % (h, d), [128, 128], F32) for d in range(2)] for h in range(8)]
        ofs = self.scr["dn_of"]
        for d in range(2):
            CUM, INCL, STRICT = (UPP, LOW, SLOW) if d == 0 else (LOW, UPP, SUPP)
            for h in range(8):
                P.op("pool", lambda e, t=Sst[h][d]: e.memset(t[:], 0.0), writes=[Sst[h][d]])
            order = range(NCHK) if d == 0 else range(NCHK - 1, -1, -1)
            for n in order:
                tok = n * 128
                seg = tok // SEG
                db = dbp.next()
                P.dma(db[:], self.scr["dba"][tok:tok + 128, :], reads=[P.region("dba", "t", n, 0)], writes=[db])
                s = {k_: sm[k_].next() for k_ in sm}
                self.act(s["beta"][:], db[:, 0:16], AF.Sigmoid, [db], [s["beta"]])
                self.ts(s["nbeta"][:], s["beta"][:], -1.0, None, ALU.mult, None, [s["beta"]], [s["nbeta"]])
                self.tt(s["x"][:], db[:, 16:32], dtb[:], ALU.add, [db, dtb], [s["x"]])
                self.ts(s["mx"][:], s["x"][:], 0.0, None, ALU.max, None, [s["x"]], [s["mx"]])
                self.stt(s["ax"][:], s["mx"][:], -2.0, s["x"][:], ALU.mult, ALU.add, [s["mx"], s["x"]], [s["ax"]])
                self.act(s["e"][:], s["ax"][:], AF.Exp, [s["ax"]], [s["e"]])
                self.act(s["ln"][:], s["e"][:], AF.Ln, [s["e"]], [s["ln"]], bias=1.0)
                self.tt(s["mx"][:], s["mx"][:], s["ln"][:], ALU.add, [s["mx"], s["ln"]], [s["mx"]])
                self.tt(s["g"][:], s["mx"][:], nea[:], ALU.mult, [s["mx"], nea], [s["g"]])
                pg = pp.next()
                self.mm(pg[:, 0:8], CUM[:], s["g"][:, d * 8:(d + 1) * 8], [CUM, s["g"]], [pg])
                self.mm(pg[:, 16:24], onesf[:], s["g"][:, d * 8:(d + 1) * 8], [onesf, s["g"]], [pg])
                self.cp(s["gcc"][:, 0:8], pg[:, 0:8], [pg], [s["gcc"]], eng="dve")
                self.cp(s["gtot"][:, 0:8], pg[:, 16:24], [pg], [s["gtot"]], eng="dve")
                self.act(s["egc"][:, 0:8], s["gcc"][:, 0:8], AF.Exp, [s["gcc"]], [s["egc"]])
                self.tt(s["dif"][:, 0:8], s["gtot"][:, 0:8], s["gcc"][:, 0:8], ALU.subtract, [s["gtot"], s["gcc"]], [s["dif"]])
                self.act(s["ekd"][:, 0:8], s["dif"][:, 0:8], AF.Exp, [s["dif"]], [s["ekd"]])
                self.act(s["glast"][:, 0:8], s["gtot"][:, 0:8], AF.Exp, [s["gtot"]], [s["glast"]])
                self.tt(s["cb"][:, 0:8], s["beta"][:, d * 8:(d + 1) * 8], s["egc"][:, 0:8], ALU.mult, [s["beta"], s["egc"]], [s["cb"]])
                heads = [dict() for _ in range(8)]

                def stageA(which, tile_i, h, pp):
                    xi = xin.next()
                    lo = max(tok - 1, 0); hi = min(tok + 129, L)
                    c_lo = lo - (tok - 1); c_hi = 130 - ((tok + 129) - hi)
                    if c_lo > 0 or c_hi < 130:
                        P.op("pool", lambda e: e.memset(xi[:], 0.0), writes=[xi])
                    rds = [P.region("dqkv", tile_i, sg_) for sg_ in sorted(set([lo // SEG, (hi - 1) // SEG]))]
                    P.dma(xi[:, c_lo:c_hi], self.scr["dqkv"][tile_i * 128:(tile_i + 1) * 128, lo:hi], reads=rds, writes=[xi])
                    cv = cvp.next()
                    self.ts(cv[:], xi[:, 0:128], cw[:, 0, tile_i:tile_i + 1], None, ALU.mult, None, [xi, cw], [cv])
                    self.stt(cv[:], xi[:, 1:129], cw[:, 1, tile_i:tile_i + 1], cv[:], ALU.mult, ALU.add, [xi, cw, cv], [cv])
                    self.stt(cv[:], xi[:, 2:130], cw[:, 2, tile_i:tile_i + 1], cv[:], ALU.mult, ALU.add, [xi, cw, cv], [cv])
                    yield
                    self.act(cv[:], cv[:], AF.Silu, [cv], [cv])
                    yield
                    if which == "v":
                        pt = pp.next()
                        self.tr(pt[:, :], cv[:], idf[:], [cv, idf], [pt])
                        yield
                        vtm = vtp.next()
                        self.cp(vtm[:], pt[:, :], [pt], [vtm])
                        heads[h]["vtm"] = vtm
                        return
                    sq = sqp.next()
                    self.tt(sq[:], cv[:], cv[:], ALU.mult, [cv], [sq])
                    pn = pp.next()
                    self.mm(pn[:, :], onesf[:], sq[:], [onesf, sq], [pn])
                    yield
                    rq = rsp.next()
                    self.act(rq[:], pn[:, :], AF.Sqrt, [pn], [rq], bias=EPS)
                    yield
                    P.op("dve", lambda e: e.reciprocal(out=rq[:], in_=rq[:]), reads=[rq], writes=[rq])
                    dst = (qTp if which == "q" else kTp).next()
                    if which == "q":
                        self.stt(dst[:], cv[:], 128.0 ** -0.5, rq[:], ALU.mult, ALU.mult, [cv, rq], [dst])
                        heads[h]["qT"] = dst
                        return
                    self.tt(dst[:], cv[:], rq[:], ALU.mult, [cv, rq], [dst])
                    heads[h]["kT"] = dst
                    pt = pp.next()
                    self.tr(pt[:, :], dst[:], idf[:], [dst, idf], [pt])
                    yield
                    ktm = ktp.next()
                    self.cp(ktm[:], pt[:, :], [pt], [ktm])
                    heads[h]["ktm"] = ktm

                _rr([stageA(wh, off + h, h, ppA[3 * h + i_]) for h in range(8) for i_, (wh, off) in enumerate((("q", 0), ("k", 8), ("v", 16)))])

                def unit(h):
                    pp = ppH[h]
                    w = {k_: wk[k_].next() for k_ in wk}
                    for a_, b_ in alias.items():
                        w[a_] = w[b_]
                    qT, kT, ktm, vtm = heads[h]["qT"], heads[h]["kT"], heads[h]["ktm"], heads[h]["vtm"]
                    gcol = s["g"][:, d * 8 + h:d * 8 + h + 1]
                    self.act(w["gU"][:], CUM[:], AF.Copy, [CUM, s["g"]], [w["gU"]], scale=gcol)
                    pgb = pp.next()
                    self.mm(pgb[:, :], onesf[:], w["gU"][:], [onesf, w["gU"]], [pgb])
                    pG = pp.next()
                    self.mm(pG[:, :], kT[:], kT[:], [kT], [pG])
                    pq = pp.next()
                    self.mm(pq[:, :], qT[:], kT[:], [qT, kT], [pq])
                    yield
                    self.act(w["egcb"][:], pgb[:, :], AF.Exp, [pgb], [w["egcb"]])
                    self.ts(w["dm"][:], pgb[:, :], -1.0, s["gcc"][:, h:h + 1], ALU.mult, ALU.add, [pgb, s["gcc"]], [w["dm"]])
                    self.ts(w["dm"][:], w["dm"][:], 0.0, None, ALU.min, None, [w["dm"]], [w["dm"]])
                    yield
                    self.act(w["dm"][:], w["dm"][:], AF.Exp, [w["dm"]], [w["dm"]])
                    yield
                    self.tt(w["dst"][:], w["dm"][:], STRICT[:], ALU.mult, [w["dm"], STRICT], [w["dst"]])
                    self.stt(w["X"][:], pG[:, :], s["nbeta"][:, d * 8 + h:d * 8 + h + 1], w["dst"][:], ALU.mult, ALU.mult, [pG, s["nbeta"], w["dst"]], [w["X"]])
                    pt = pp.next()
                    self.tr(pt[:, :], w["X"][:], idf[:], [w["X"], idf], [pt])
                    self.tt(w["dec"][:], w["dm"][:], INCL[:], ALU.mult, [w["dm"], INCL], [w["dec"]])
                    self.tt(w["att"][:], pq[:, :], w["dec"][:], ALU.mult, [pq, w["dec"]], [w["att"]])
                    pat = pp.next()
                    self.tr(pat[:, :], w["att"][:], idf[:], [w["att"], idf], [pat])
                    yield
                    self.cp(w["Y"][:], pt[:, :], [pt], [w["Y"]], eng="act")
                    self.cp(w["attT"][:], pat[:, :], [pat], [w["attT"]], eng="act")
                    self.tt(w["TT"][:], pt[:, :], idf[:], ALU.add, [pt, idf], [w["TT"]])
                    self.tt(w["qgT"][:], qT[:], w["egcb"][:], ALU.mult, [qT, w["egcb"]], [w["qgT"]])
                    yield
                    X, Y, X2, Y2 = w["X"], w["Y"], w["X2"], w["Y2"]
                    for kk in range(1, 7):
                        pX = pp.next()
                        self.mm(pX[:, :], Y[:], X[:], [X, Y], [pX])
                        if kk < 6:
                            pY = pp.next()
                            self.mm(pY[:, :], X[:], Y[:], [X, Y], [pY])
                        yield
                        self.cp(X2[:], pX[:, :], [pX], [X2], eng="act")
                        if kk < 6:
                            self.cp(Y2[:], pY[:, :], [pY], [Y2], eng="dve")
                        yield
                        pT = pp.next()
                        self.mm(pT[:, :], X2[:], w["TT"][:], [X2, w["TT"]], [pT])
                        yield
                        self.tt(w["TT"][:], w["TT"][:], pT[:, :], ALU.add, [w["TT"], pT], [w["TT"]])
                        X, X2 = X2, X
                        Y, Y2 = Y2, Y
                    TT = w["TT"]
                    self.act(w["vb"][:], vtm[:], AF.Copy, [vtm, s["beta"]], [w["vb"]], scale=s["beta"][:, d * 8 + h:d * 8 + h + 1])
                    self.act(w["kbg"][:], ktm[:], AF.Copy, [ktm, s["cb"]], [w["kbg"]], scale=s["cb"][:, h:h + 1])
                    self.act(w["kd"][:], ktm[:], AF.Copy, [ktm, s["ekd"]], [w["kd"]], scale=s["ekd"][:, h:h + 1])
                    yield
                    pu = pp.next()
                    self.mm(pu[:, :], TT[:], w["vb"][:], [TT, w["vb"]], [pu])
                    pw = pp.next()
                    self.mm(pw[:, :], w["kbg"][:], TT[:], [TT, w["kbg"]], [pw])
                    yield
                    self.cp(w["u"][:], pu[:, :], [pu], [w["u"]], eng="act")
                    self.cp(w["wT"][:], pw[:, :], [pw], [w["wT"]], eng="dve")
                    yield
                    St = Sst[h][d]
                    p1 = pp.next()
                    self.mm(p1[:, :], w["wT"][:], St[:], [w["wT"], St], [p1])
                    yield
                    self.tt(w["vnew"][:], w["u"][:], p1[:, :], ALU.subtract, [w["u"], p1], [w["vnew"]])
                    yield
                    p2 = pp.next()
                    self.mm(p2[:, 0:128], w["qgT"][:], St[:], [w["qgT"], St], [p2], start=True, stop=False, inc=False)
                    self.mm(p2[:, 0:128], w["attT"][:], w["vnew"][:], [w["attT"], w["vnew"]], [p2], start=False, stop=True)
                    p3 = pp.next()
                    self.mm(p3[:, :], w["kd"][:], w["vnew"][:], [w["kd"], w["vnew"]], [p3])
                    yield
                    self.stt(St[:], St[:], s["glast"][:, h:h + 1], p3[:, :], ALU.mult, ALU.add, [St, s["glast"], p3], [St])
                    if d == 0:
                        self.cp(w["osb"][:], p2[:, 0:128], [p2], [w["osb"]], eng="act")
                        P.dma(ofs[tok:tok + 128, h * 128:(h + 1) * 128], w["osb"][:], reads=[w["osb"]], writes=[P.region("dn_of", n, h)])
                        return
                    P.dma(w["of"][:], ofs[tok:tok + 128, h * 128:(h + 1) * 128], reads=[P.region("dn_of", n, h)], writes=[w["of"]])
                    P.dma(w["zt"][:], self.scr["dz"][h * 128:(h + 1) * 128, tok:tok + 128], reads=[P.region("dz", h, seg)], writes=[w["zt"]])
                    self.tt(w["osum"][:], w["of"][:], p2[:, 0:128], ALU.add, [w["of"], p2], [w["osum"]])
                    yield
                    ss = ssp.next(); r1 = rs1.next(); rstd = rstd1.next()
                    P.op("act", lambda e: e.activation(out=w["junk"][:], in_=w["osum"][:], func=AF.Square, accum_out=ss[:]), reads=[w["osum"]], writes=[w["junk"], ss])
                    self.act(r1[:], ss[:], AF.Sqrt, [ss], [r1], scale=1.0 / 128, bias=EPS)
                    yield
                    P.op("dve", lambda e: e.reciprocal(out=rstd[:], in_=r1[:]), reads=[r1], writes=[rstd])
                    self.stt(w["on"][:], w["osum"][:], rstd[:], ng[:], ALU.mult, ALU.mult, [w["osum"], rstd, ng], [w["on"]])
                    pt2 = pp.next()
                    self.tr(pt2[:, :], w["on"][:], idf[:], [w["on"], idf], [pt2])
                    self.act(w["zs"][:], w["zt"][:], AF.Silu, [w["zt"]], [w["zs"]])
                    yield
                    ob = obp.next()
                    self.tt(ob[:], pt2[:, :], w["zs"][:], ALU.mult, [pt2, w["zs"]], [ob])
                    P.dma(self.scr["brT"][3 * BR + h * 128:3 * BR + (h + 1) * 128, tok:tok + 128], ob[:], reads=[ob], writes=[P.region("brT", 3, n, h)])

                _rr([unit(h) for h in range(8)])


K.deltanet = deltanet


def xattn(self, l, L, sn):
    P = self.P
    inp = self.inp
    idb = self.c["identb"]
    SEG = min(L, 2048)
    scale = 256.0 ** -0.5
    with Scope(P) as S:
        kT = S.sb("kT", [128, 8, 256], BF16)
        V = S.sb("V", [128, 2, BR], BF16)
        with Scope(P) as S2:
            memT = S2.sb("memT", [128, 16, 256], BF16)
            gt = S2.sb("gt", [128, D], F32)
            tp = self.norm_pools(S2)
            xpool = S2.pool("xt", [128, D], F32, 2)
            hbpool = S2.pool("hb", [128, D], BF16, 2)
            pstp = S2.pspool("pst", 2, BF16)
            psm = S2.pspool("psm", 3, F32)
            wpool = S2.pool("wb", [128, 16, 256], BF16, 3)
            P.dma(gt[:], inp["mem_ln_g"][l].partition_broadcast(128), writes=[gt])
            mem = inp["mem_" + sn]
            self.make_hT(S2, lambda tt: mem[tt * 128:(tt + 1) * 128, :], gt, memT, 2, tp, xpool, hbpool, pstp, lambda tt: [])
            wkv = inp["xa_w_kv"][l]
            for c0 in range(0, 1024, 256):
                wb = wpool.next()
                P.dma(wb[:], wkv[:, c0:c0 + 256].rearrange("(k p) c -> p k c", p=128), writes=[wb], q="pool")
                for ct in range(2):
                    ps = psm.next()
                    for k in range(16):
                        self.mm(ps[:, 0:256], wb[:, k, ct * 128:(ct + 1) * 128], memT[:, k, :], [wb, memT], [ps], start=(k == 0), stop=(k == 15), inc=(k == 15))
                    self.cp(kT[:, c0 // 128 + ct, :], ps[:, 0:256], [ps], [kT])
            for c0 in range(0, 1024, 256):
                wb = wpool.next()
                P.dma(wb[:], wkv[:, 1024 + c0:1024 + c0 + 256].rearrange("(k p) c -> p k c", p=128), writes=[wb], q="pool")
                for mt in range(2):
                    ps = psm.next()
                    for k in range(16):
                        self.mm(ps[:, 0:256], memT[:, k, mt * 128:(mt + 1) * 128], wb[:, k, :], [wb, memT], [ps], start=(k == 0), stop=(k == 15), inc=(k == 15))
                    self.cp(V[:, mt, c0:c0 + 256], ps[:, 0:256], [ps], [V])
        qp = S.pool("q", [128, 8, 128], BF16, 2)
        zp = S.pool("z", [128, 8, 128], F32, 2)
        zsp = S.pool("zs", [128, 8, 128], F32, 2)
        obp = S.pool("ob", [128, 8, 128], BF16, 2)
        ep = S.pool("e", [128, 256], F32, 2)
        pbp = S.pool("pb", [128, 256], BF16, 2)
        ptp = S.pool("pt", [128, 2, 128], BF16, 2)
        mxp = S.pool("mx", [128, 1], F32, 3); nbp = S.pool("nb", [128, 1], F32, 3); rsp = S.pool("rs", [128, 1], F32, 3); rip = S.pool("ri", [128, 1], F32, 3)
        pss = S.pspool("pss", 2, F32)
        pst = S.pspool("pst", 2, BF16)
        pso = S.pspool("pso", 3, F32)
        for n in range(L // 128):
            tok = n * 128
            seg = tok // SEG
            q = qp.next(); z = zp.next(); zs = zsp.next(); ob = obp.next()
            P.dma(q[:], self.scr["xq"][:, tok:tok + 128].rearrange("(k p) t -> p k t", p=128), reads=[P.region("xq", f, seg) for f in range(8)], writes=[q], q="pool")
            P.dma(z[:], self.scr["xz"][:, tok:tok + 128].rearrange("(k p) t -> p k t", p=128), reads=[P.region("xz", f, seg) for f in range(8)], writes=[z])
            self.act(zs[:], z[:], AF.Silu, [z], [zs])
            for hh in range(4):
                ps = pss.next()
                for c2 in range(2):
                    self.mm(ps[:, 0:256], q[:, hh * 2 + c2, :], kT[:, hh * 2 + c2, :], [q, kT], [ps], start=(c2 == 0), stop=(c2 == 1), inc=(c2 == 1))
                mx = mxp.next(); nb = nbp.next(); rs = rsp.next(); ri = rip.next(); e = ep.next(); pb = pbp.next(); pt_sb = ptp.next()
                P.op("dve", lambda e_, mx=mx, ps=ps: e_.reduce_max(out=mx[:], in_=ps[:, 0:256], axis=AX.X), reads=[ps], writes=[mx])
                self.ts(nb[:], mx[:], -scale, None, ALU.mult, None, [mx], [nb])
                P.op("act", lambda e_, e=e, ps=ps, nb=nb, rs=rs: e_.activation(out=e[:], in_=ps[:, 0:256], func=AF.Exp, scale=scale, bias=nb[:], accum_out=rs[:]),
                     reads=[ps, nb], writes=[e, rs])
                P.op("dve", lambda e_, rs=rs, ri=ri: e_.reciprocal(out=ri[:], in_=rs[:]), reads=[rs], writes=[ri])
                self.ts(pb[:], e[:], ri[:], None, ALU.mult, None, [e, ri], [pb])
                ptt = pst.next()
                for mt in range(2):
                    self.tr(ptt[:, mt * 128:(mt + 1) * 128], pb[:, mt * 128:(mt + 1) * 128], idb[:], [pb, idb], [ptt], inc=(mt == 1))
                self.cp(pt_sb[:], ptt[:, 0:256].rearrange("p (a b) -> p a b", a=2), [ptt], [pt_sb])
                for c2 in range(2):
                    po = pso.next()
                    for mt in range(2):
                        self.mm(po[:, 0:128], V[:, mt, hh * 256 + c2 * 128:hh * 256 + (c2 + 1) * 128], pt_sb[:, mt, :], [V, pt_sb], [po], start=(mt == 0), stop=(mt == 1), inc=(mt == 1))
                    self.tt(ob[:, hh * 2 + c2, :], po[:, 0:128], zs[:, hh * 2 + c2, :], ALU.mult, [po, zs], [ob])
            P.dma(self.scr["brT"][4 * BR:5 * BR, tok:tok + 128].rearrange("(k p) t -> p k t", p=128), ob[:], reads=[ob], writes=[P.region("brT", 4, n, f) for f in range(8)])


K.xattn = xattn


def attention(self, l, L):
    P = self.P
    inp = self.inp
    idf = self.c["identf"]
    onesf = self.c["onesf"]
    onesb = self.c["onesb"]
    SEG = min(L, 2048)
    TQ = min(512, L)
    NKC = L // 128
    with Scope(P) as S:
        cosT = S.sb("cosT", [128, L], F32)
        sinT = S.sb("sinT", [128, L], F32)
        RM = S.sb("RM", [128, 128], F32)
        qg = S.sb("qg", [128, 1], F32); kg = S.sb("kg", [128, 1], F32)
        P.dma(qg[:], inp["attn_qn_g"][l].rearrange("(p o) -> p o", o=1), writes=[qg])
        P.dma(kg[:], inp["attn_kn_g"][l].rearrange("(p o) -> p o", o=1), writes=[kg])
        for b in range(4):
            if b % 2 == 0:
                self.ts(RM[:, b * 32:(b + 1) * 32], idf[:, (b + 1) * 32:(b + 2) * 32], -1.0, None, ALU.mult, None, [idf, RM], [RM])
            else:
                self.cp(RM[:, b * 32:(b + 1) * 32], idf[:, (b - 1) * 32:b * 32], [idf, RM], [RM], eng="dve")
        with Scope(P) as S2:
            pi_ = S2.sb("pi", [128, 1], F32); f_ = S2.sb("f", [128, 1], F32); t_ = S2.sb("t", [128, 1], F32); fr = S2.sb("fr", [128, 1], F32)
            P.op("pool", lambda e: e.iota(pi_[:], pattern=[[0, 1]], base=0, channel_multiplier=1, allow_small_or_imprecise_dtypes=True), writes=[pi_])
            self.cp(f_[:], pi_[:], [pi_], [f_], eng="dve")
            for thr in (32.0, 64.0, 96.0):
                self.ts(t_[:], pi_[:], thr, -32.0, ALU.is_ge, ALU.mult, [pi_, t_], [t_])
                self.tt(f_[:], f_[:], t_[:], ALU.add, [f_, t_], [f_])
            self.act(fr[:], f_[:], AF.Exp, [f_], [fr], scale=-math.log(10000.0) / 32.0)
            CB = min(1024, L)
            pos = S2.sb("pos", [128, CB], F32)
            xx = S2.sb("xx", [128, 2, CB], F32); u = S2.sb("u", [128, 2, CB], F32); ki = S2.sb("ki", [128, 2, CB], I32); kf = S2.sb("kf", [128, 2, CB], F32)
            sc = S2.sb("sc", [128, 2, CB], F32)
            for cb in range(L // CB):
                r0 = (cb * CB) // 64
                P.op("pool", lambda e, r0=r0: e.iota(pos[0:64, :], pattern=[[1, CB // 64], [0, 64]], base=r0, channel_multiplier=0, allow_small_or_imprecise_dtypes=True), reads=[pos], writes=[pos])
                P.op("pool", lambda e: e.iota(pos[64:128, :], pattern=[[0, CB // 64], [1, 64]], base=0, channel_multiplier=0, allow_small_or_imprecise_dtypes=True), reads=[pos], writes=[pos])
                self.ts(xx[:, 0, :], pos[:], fr[:], None, ALU.mult, None, [pos, fr, xx], [xx])
                self.ts(xx[:, 1, :], xx[:, 0, :], math.pi / 2, None, ALU.add, None, [xx], [xx])
                _reduce_sin(self, xx, u, ki, kf, sc)
                self.cp(sinT[:, cb * CB:(cb + 1) * CB], sc[:, 0, :], [sc], [sinT], eng="act")
                self.cp(cosT[:, cb * CB:(cb + 1) * CB], sc[:, 1, :], [sc], [cosT], eng="dve")
        kT = S.sb("kTa", [128, 2, L], BF16)
        V = S.sb("Va", [128, NKC, 256], BF16)
        P.dma(V[:], self.scr["av"][0:L, :].rearrange("(n p) c -> p n c", p=128), reads=[P.region("av", "t", n, 0) for n in range(NKC)], writes=[V], q="pool")
        xq = S.pool("xq", [128, TQ], F32, 2)
        sqp = S.pool("sq", [128, TQ], F32, 2)
        rqp = S.pool("rq", [128, TQ], F32, 2)
        qnp = S.pool("qn", [128, TQ], F32, 2)
        t1p = S.pool("t1", [128, TQ], F32, 2)
        t2p = S.pool("t2", [128, TQ], F32, 2)
        qrp = S.pool("qr", [128, TQ], BF16, 2)
        ptp = S.pool("pt", [128, TQ], BF16, 4)
        zp = S.pool("z", [128, TQ], F32, 2); zsp = S.pool("zs", [128, TQ], F32, 2)
        rdp = S.pool("rd", [128, TQ], F32, 2); o1p = S.pool("o1", [128, TQ], F32, 2)
        obp = S.pool("ob", [128, TQ], BF16, 2)
        psx = S.pspool("psx", 1, F32)
        pst = S.pspool("pst", 3, F32)
        pso = S.pspool("pso", 2, F32)
        psd = S.pspool("psd", 2, F32)

        def norm_rope(src_name, row0, t0, gcol, region, out_ap, out_bufs, scl):
            x = xq.next()
            P.dma(x[:], self.scr[src_name][row0:row0 + 128, t0:t0 + TQ], reads=[region], writes=[x])
            sq = sqp.next(); rq = rqp.next(); qn = qnp.next(); t1 = t1p.next(); t2 = t2p.next()
            self.tt(sq[:], x[:], x[:], ALU.mult, [x], [sq])
            ps = psx.next()
            self.mm(ps[:, 0:TQ], onesf[:], sq[:], [onesf, sq], [ps])
            self.act(rq[:], ps[:, 0:TQ], AF.Sqrt, [ps], [rq], scale=1.0 / 128, bias=EPS)
            P.op("dve", lambda e: e.reciprocal(out=rq[:], in_=rq[:]), reads=[rq], writes=[rq])
            self.stt(qn[:], x[:], gcol[:], rq[:], ALU.mult, ALU.mult, [x, gcol, rq], [qn])
            ps2 = psx.next()
            self.mm(ps2[:, 0:TQ], RM[:], qn[:], [RM, qn], [ps2])
            self.tt(t1[:], qn[:], cosT[:, t0:t0 + TQ], ALU.mult, [qn, cosT], [t1])
            self.tt(t2[:], ps2[:, 0:TQ], sinT[:, t0:t0 + TQ], ALU.mult, [ps2, sinT], [t2])
            self.stt(out_ap, t1[:], scl, t2[:], ALU.mult, ALU.add, [t1, t2], out_bufs)

        def norm_rope2(src_name, row0, t0, gcol, region, out_ap, out_bufs, scl):
            x = xq.next()
            P.dma(x[:], self.scr[src_name][row0:row0 + 128, t0:t0 + TQ], reads=[region], writes=[x])
            sq = sqp.next(); rq = rqp.next(); qn = qnp.next(); t1 = t1p.next(); t2 = t2p.next()
            self.tt(sq[:], x[:], x[:], ALU.mult, [x], [sq])
            ps = psx.next()
            self.mm(ps[:, 0:TQ], onesf[:], sq[:], [onesf, sq], [ps])
            self.act(rq[:], ps[:, 0:TQ], AF.Sqrt, [ps], [rq], scale=1.0 / 128, bias=EPS)
            P.op("dve", lambda e: e.reciprocal(out=rq[:], in_=rq[:]), reads=[rq], writes=[rq])
            self.stt(qn[:], x[:], gcol[:], rq[:], ALU.mult, ALU.mult, [x, gcol, rq], [qn])
            ps2 = psx.next()
            self.mm(ps2[:, 0:TQ], RM[:], qn[:], [RM, qn], [ps2])
            self.stt(t1[:], qn[:], scl, cosT[:, t0:t0 + TQ], ALU.mult, ALU.mult, [qn, cosT], [t1])
            self.stt(t2[:], ps2[:, 0:TQ], scl, sinT[:, t0:t0 + TQ], ALU.mult, ALU.mult, [ps2, sinT], [t2])
            self.tt(out_ap, t1[:], t2[:], ALU.add, [t1, t2], out_bufs)

        for kvh in range(2):
            for tq in range(L // TQ):
                t0 = tq * TQ
                norm_rope2("ak", kvh * 128, t0, kg, P.region("ak", kvh, t0 // SEG), kT[:, kvh, t0:t0 + TQ], [kT], 1.0)
        for tq in range(L // TQ):
            t0 = tq * TQ
            seg = t0 // SEG
            for hq in range(8):
                kvh = hq // 4
                qr = qrp.next()
                norm_rope2("aq", hq * 128, t0, qg, P.region("aq", hq, seg), qr[:], [qr], 128.0 ** -0.5)
                po = pso.next(); pd = psd.next()
                PRE = 2
                sts = {}

                def issue_st(kc_):
                    pst_ = pst.next()
                    self.mm(pst_[:, 0:TQ], kT[:, kvh, kc_ * 128:(kc_ + 1) * 128], qr[:], [kT, qr], [pst_])
                    sts[kc_] = pst_
                for kc_ in range(min(PRE, NKC)):
                    issue_st(kc_)
                for kc in range(NKC):
                    if kc + PRE < NKC:
                        issue_st(kc + PRE)
                    pst_ = sts.pop(kc)
                    pt = ptp.next()
                    self.act(pt[:], pst_[:, 0:TQ], AF.Exp, [pst_], [pt])
                    self.mm(po[:, 0:TQ], V[:, kc, kvh * 128:(kvh + 1) * 128], pt[:], [V, pt], [po], start=(kc == 0), stop=(kc == NKC - 1), inc=False)
                    self.mm(pd[:, 0:TQ], onesb[:], pt[:], [onesb, pt], [pd], start=(kc == 0), stop=(kc == NKC - 1), inc=True)
                z = zp.next(); zs = zsp.next(); rd = rdp.next(); o1 = o1p.next(); ob = obp.next()
                P.dma(z[:], self.scr["az"][hq * 128:(hq + 1) * 128, t0:t0 + TQ], reads=[P.region("az", hq, seg)], writes=[z])
                self.act(zs[:], z[:], AF.Silu, [z], [zs])
                P.op("dve", lambda e, rd=rd, pd=pd: e.reciprocal(out=rd[:], in_=pd[:, 0:TQ]), reads=[pd], writes=[rd])
                self.tt(o1[:], po[:, 0:TQ], rd[:], ALU.mult, [po, rd], [o1])
                self.tt(ob[:], o1[:], zs[:], ALU.mult, [o1, zs], [ob])
                P.dma(self.scr["brT"][hq * 128:(hq + 1) * 128, t0:t0 + TQ], ob[:], reads=[ob], writes=[P.region("brT", 0, t0 // 128 + i, hq) for i in range(TQ // 128)])


K.attention = attention
```

```python
import math
from contextlib import ExitStack
import numpy as np
import concourse.bass as bass
import concourse.mybir as mybir
from concourse.bass_utils import run_bass_kernel_spmd

F32 = mybir.dt.float32
BF16 = mybir.dt.bfloat16
I32 = mybir.dt.int32
AF = mybir.ActivationFunctionType
ALU = mybir.AluOpType
AX = mybir.AxisListType

NDMA = 56
D = 2048
BR = 1024
NCOL = 24096
EPS = 1e-6
OFF = dict(aq=0, ak=1024, av=1280, az=1536, su=2560, sz=3584, gu=4608, gv=5632, gz=6656,
           dq=7680, dk=8704, dv=9728, dz=10752, dbeta=11776, da=11792, xq=11808, xz=12832,
           gates=13856)
TWO_PI = 2.0 * math.pi
RR_SEQ = False


class Buf:
    __slots__ = ("name", "h", "last_w", "readers", "psum")

    def __init__(self, name, h=None):
        self.name = name
        self.h = h
        self.last_w = None
        self.readers = []
        self.psum = False

    def __getitem__(self, k):
        return self.h[k]


class Prog:
    CE = ("pe", "act", "dve", "pool")

    def __init__(self, nc):
        self.nc = nc
        self.q = {e: [] for e in ("pe", "act", "dve", "pool", "sp")}
        self.sem = {e: nc.alloc_semaphore("sem_" + e) for e in self.CE}
        self.cnt = {e: 0 for e in self.CE}
        self.dsem = [nc.alloc_semaphore("dsem%d" % i) for i in range(NDMA)]
        self.dcnt = [0] * NDMA
        self.drr = 0
        self.waited = {e: {} for e in self.q}
        self.regions = {}
        self.out_events = []
        self.uid = 0
        self.n_ops = 0

    def name(self, s):
        self.uid += 1
        return "%s_%d" % (s, self.uid)

    def region(self, *key):
        b = self.regions.get(key)
        if b is None:
            b = Buf(str(key))
            self.regions[key] = b
        return b

    def _collect(self, eng, reads, writes):
        evs = []
        for b in reads:
            if b.last_w is not None:
                evs.append(b.last_w)
        for b in writes:
            if b.last_w is not None:
                evs.append(b.last_w)
            evs.extend(b.readers)
        waits = []
        wd = self.waited[eng]
        for (key, val) in evs:
            if key[0] == "c":
                if key[1] == eng and eng == "pe":
                    continue
                assert self.cnt[key[1]] >= val, "wait on not-yet-emitted inc %s %d" % (key, val)
            if wd.get(key, 0) >= val:
                continue
            wd[key] = val
            waits.append((key, val))
        return waits

    def _commit(self, ev, reads, writes):
        for b in reads:
            if len(b.readers) > 24:
                mx = {}
                for k, v in b.readers:
                    if mx.get(k, 0) < v:
                        mx[k] = v
                b.readers = list(mx.items())
            b.readers.append(ev)
        for b in writes:
            b.last_w = ev
            b.readers = []

    def op(self, eng, fn, reads=(), writes=(), inc=True):
        pr = [b for b in reads if b.psum]
        if pr:
            reads = [b for b in reads if not b.psum]
            writes = list(writes) + pr
        waits = self._collect(eng, reads, writes)
        if inc:
            self.cnt[eng] += 1
            ev = (("c", eng), self.cnt[eng])
        else:
            ev = (("c", eng), self.cnt[eng] + 1)
        self.q[eng].append((waits, fn, inc, None))
        self._commit(ev, reads, writes)
        self.n_ops += 1

    def dma(self, out_ap, in_ap, reads=(), writes=(), q="sp", is_out=False, slow=False):
        waits = self._collect(q, reads, writes)
        j = self.drr
        self.drr = (self.drr + 1) % NDMA
        prev = self.dcnt[j] * 16
        key = ("d", j)
        if prev and self.waited[q].get(key, 0) < prev:
            self.waited[q][key] = prev
            waits.append((key, prev))
        self.dcnt[j] += 1
        ev = (key, self.dcnt[j] * 16)
        self.q[q].append((waits, (out_ap, in_ap, slow), True, j))
        self._commit(ev, reads, writes)
        if is_out:
            self.out_events.append(ev)
        self.n_ops += 1

    def barrier(self):
        for e in self.q:
            wd = self.waited[e]
            waits = []
            for c in self.CE:
                if c == e:
                    continue
                v = self.cnt[c]
                if v and wd.get(("c", c), 0) < v:
                    wd[("c", c)] = v
                    waits.append((("c", c), v))
            for j in range(NDMA):
                v = self.dcnt[j] * 16
                if v and wd.get(("d", j), 0) < v:
                    wd[("d", j)] = v
                    waits.append((("d", j), v))
            if waits:
                self.q[e].append((waits, None, False, None))

    def _semof(self, key):
        return self.sem[key[1]] if key[0] == "c" else self.dsem[key[1]]

    def emit(self):
        nc = self.nc
        mx = {}
        for k, v in self.out_events:
            mx[k] = max(mx.get(k, 0), v)
        prog = self

        def run(engname, eobj):
            for (waits, fn, inc, dj) in prog.q[engname]:
                for (key, val) in waits:
                    eobj.wait_ge(prog._semof(key), val)
                if fn is None:
                    continue
                if dj is not None:
                    out_ap, in_ap, slow = fn
                    if slow:
                        eobj.dma_start(out=out_ap, in_=in_ap, allow_slow_non_contiguous=True).then_inc(prog.dsem[dj], 16)
                    else:
                        eobj.dma_start(out=out_ap, in_=in_ap).then_inc(prog.dsem[dj], 16)
                else:
                    ins = fn(eobj)
                    if inc:
                        ins.then_inc(prog.sem[engname], 1)
            if engname == "sp":
                for k, v in mx.items():
                    eobj.wait_ge(prog._semof(k), v)

        with nc.Block() as block:
            @block.tensor
            def _(e):
                run("pe", e)

            @block.scalar
            def _(e):
                run("act", e)

            @block.vector
            def _(e):
                run("dve", e)

            @block.gpsimd
            def _(e):
                run("pool", e)

            @block.sync
            def _(e):
                run("sp", e)


class Scope:
    def __init__(self, P):
        self.P = P
        self.es = ExitStack()

    def __enter__(self):
        self.es.__enter__()
        return self

    def __exit__(self, *a):
        self.P.barrier()
        return self.es.__exit__(*a)

    def sb(self, name, shape, dtype=F32):
        h = self.es.enter_context(self.P.nc.sbuf_tensor(self.P.name(name), list(shape), dtype))
        return Buf(name, h)

    def ps(self, name, dtype=F32):
        n = 512 if dtype == F32 else 1024
        h = self.es.enter_context(self.P.nc.psum_tensor(self.P.name(name), [128, n], dtype))
        b = Buf(name, h)
        b.psum = True
        return b

    def pool(self, name, shape, dtype, n):
        return Rot([self.sb("%s%d" % (name, i), shape, dtype) for i in range(n)])

    def pspool(self, name, n, dtype=F32):
        return Rot([self.ps("%s%d" % (name, i), dtype) for i in range(n)])


class Rot:
    def __init__(self, tiles):
        self.tiles = tiles
        self.i = 0

    def next(self):
        t = self.tiles[self.i]
        self.i = (self.i + 1) % len(self.tiles)
        return t


class K:
    def __init__(self, nc, seqs, depth=2, debug=False, branches=(0, 1, 2, 3, 4)):
        self.nc = nc
        self.P = Prog(nc)
        self.seqs = seqs
        self.depth = depth
        self.debug = debug
        self.branches = branches
        self.Lmax = max(L for _, L in seqs)
        self.eng_rr = 0
        self.declare()

    def declare(self):
        nc = self.nc
        self.inp = {}

        def din(name, shape):
            self.inp[name] = nc.dram_tensor(name, list(shape), F32, kind="ExternalInput").ap()

        for sn, L in self.seqs:
            din("x_" + sn, [L, D])
            din("mem_" + sn, [256, D])
        dp = self.depth
        din("ln_g", [dp, D]); din("mem_ln_g", [dp, D]); din("w_in", [dp, D, NCOL])
        din("attn_qn_g", [dp, 128]); din("attn_kn_g", [dp, 128])
        din("ssm_lam_re", [dp, 2, 64, 64]); din("ssm_lam_im", [dp, 2, 64, 64]); din("ssm_log_dt", [dp, 2, 64])
        din("ssm_b_re", [dp, 2, 64, 64, 16]); din("ssm_b_im", [dp, 2, 64, 64, 16])
        din("ssm_c_re", [dp, 2, 64, 16, 64]); din("ssm_c_im", [dp, 2, 64, 16, 64])
        din("ssm_d", [dp, BR]); din("ssm_glu_w", [dp, BR, BR]); din("ssm_glu_b", [dp, BR])
        din("sgu_ln_g", [dp, BR]); din("sgu_ln_b", [dp, BR]); din("sgu_w", [dp, 8, 128, 128]); din("sgu_b", [dp, 8, 128])
        din("dn_conv_w", [dp, 3, 3 * BR]); din("dn_a_log", [dp, 2, 8]); din("dn_dt_bias", [dp, 2, 8]); din("dn_norm_g", [dp, 128])
        din("xa_w_kv", [dp, D, 2 * BR]); din("w_branch", [dp, 5, BR, D]); din("w_out", [dp, D, D]); din("final_g", [D])
        self.out = {}
        for sn, L in self.seqs:
            self.out[sn] = nc.dram_tensor("y_" + sn, [L, D], F32, kind="ExternalOutput").ap()
        Lm = self.Lmax
        kind = "ExternalOutput" if self.debug else "Internal"
        self.scr = {}

        def dscr(name, shape, dt=F32):
            self.scr[name] = nc.dram_tensor("scr_" + name, list(shape), dt, kind=kind).ap()

        for nm, rows in (("aq", 1024), ("ak", 256), ("az", 1024), ("su", 1024),
                         ("sz", 1024), ("gu", 1024), ("gz", 1024), ("dqkv", 3072), ("dz", 1024),
                         ("xq", 1024), ("xz", 1024)):
            dscr(nm, [rows, Lm])
        dscr("av", [Lm, 256]); dscr("gv", [Lm, 1024]); dscr("dba", [Lm, 32])
        dscr("brT", [5 * BR, Lm], BF16)
        dscr("xres", [Lm, D])
        dscr("syT", [BR, Lm], BF16)
        dscr("dn_of", [Lm, BR])
        if self.debug:
            dscr("dbg", [BR, Lm])
            dscr("dbg2", [BR, Lm])

    def consts(self, S):
        P = self.P
        c = {}
        c["identf"] = S.sb("identf", [128, 128], F32)
        c["identb"] = S.sb("identb", [128, 128], BF16)
        c["onesf"] = S.sb("onesf", [128, 128], F32)
        c["onesb"] = S.sb("onesb", [128, 128], BF16)
        idf, idb, of, ob = c["identf"], c["identb"], c["onesf"], c["onesb"]
        P.op("pool", lambda e: e.memset(idf[:], 0.0), writes=[idf])
        P.op("pool", lambda e: e.affine_select(out=idf[:], in_=idf[:], pattern=[[-1, 128]], compare_op=ALU.not_equal,
                                              fill=1.0, base=0, channel_multiplier=1), reads=[idf], writes=[idf])
        P.op("dve", lambda e: e.tensor_copy(out=idb[:], in_=idf[:]), reads=[idf], writes=[idb])
        P.op("pool", lambda e: e.memset(of[:], 1.0), writes=[of])
        P.op("dve", lambda e: e.tensor_copy(out=ob[:], in_=of[:]), reads=[of], writes=[ob])
        self.c = c

    def evac_eng(self):
        self.eng_rr ^= 1
        return "act" if self.eng_rr else "dve"

    def copy(self, eng, out_ap, in_ap, reads, writes):
        if eng == "act":
            self.P.op("act", lambda e: e.activation(out=out_ap, in_=in_ap, func=AF.Copy), reads=reads, writes=writes)
        else:
            self.P.op(eng, lambda e: e.tensor_copy(out=out_ap, in_=in_ap), reads=reads, writes=writes)

    def rstd_from_ss(self, ss, rs, rstd, n):
        P = self.P
        P.op("act", lambda e: e.activation(out=rs[:], in_=ss[:], func=AF.Sqrt, scale=1.0 / n, bias=EPS), reads=[ss], writes=[rs])
        P.op("dve", lambda e: e.reciprocal(out=rstd[:], in_=rs[:]), reads=[rs], writes=[rstd])

    def rmsnorm_tile(self, xt, gt, out, tp):
        P = self.P
        junk, ss, rs, rstd = tp["junk"].next(), tp["ss"].next(), tp["rs"].next(), tp["rstd"].next()
        P.op("act", lambda e: e.activation(out=junk[:], in_=xt[:], func=AF.Square, accum_out=ss[:]), reads=[xt], writes=[junk, ss])
        self.rstd_from_ss(ss, rs, rstd, D)
        P.op("dve", lambda e: e.scalar_tensor_tensor(out=out[:], in0=xt[:], scalar=rstd[:], in1=gt[:], op0=ALU.mult, op1=ALU.mult),
             reads=[xt, rstd, gt], writes=[out])

    def norm_pools(self, S):
        return dict(junk=S.pool("junk", [128, D], BF16, 1), ss=S.pool("ss", [128, 1], F32, 2),
                    rs=S.pool("rs", [128, 1], F32, 2), rstd=S.pool("rstd", [128, 1], F32, 2))

    def make_hT(self, S, src_rows, gt, hT, ntiles, tp, xpool, hbpool, pstp, src_reads):
        P = self.P
        idb = self.c["identb"]
        for tt in range(ntiles):
            xt = xpool.next()
            P.dma(xt[:], src_rows(tt), reads=src_reads(tt), writes=[xt])
            hb = hbpool.next()
            self.rmsnorm_tile(xt, gt, hb, tp)
            for k4 in range(4):
                pt = pstp.next()
                for kk in range(4):
                    k = k4 * 4 + kk
                    P.op("pe", lambda e, k=k, kk=kk, pt=pt, hb=hb: e.transpose(out=pt[:, kk * 128:(kk + 1) * 128], in_=hb[:, k * 128:(k + 1) * 128], identity=idb[:]),
                         reads=[hb, idb], writes=[pt], inc=(kk == 3))
                self.copy(self.evac_eng(), hT[:, k4 * 4:(k4 + 1) * 4, tt * 128:(tt + 1) * 128],
                          pt[:, 0:512].rearrange("p (k t) -> p k t", k=4), [pt], [hT])

    def phase1(self, l, sn, L, xsrc, xsrc_region):
        P = self.P
        S1 = min(L, 2048)
        w_in = self.inp["w_in"][l]
        fgroups = [("aq", OFF["aq"], 1024, False), ("ak", OFF["ak"], 256, False),
                   ("az", OFF["az"], 1024, False), ("su", OFF["su"], 1024, False),
                   ("sz", OFF["sz"], 1024, False), ("gu", OFF["gu"], 1024, False), ("gz", OFF["gz"], 1024, False),
                   ("dqkv", OFF["dq"], 3072, False), ("dz", OFF["dz"], 1024, False), ("xq", OFF["xq"], 1024, False),
                   ("xz", OFF["xz"], 1024, False)]
        tgroups = [("av", OFF["av"], 256), ("gv", OFF["gv"], 1024), ("dba", OFF["dbeta"], 32)]
        with Scope(P) as S:
            hT = S.sb("hT", [128, 16, S1], BF16)
            gt = S.sb("gt", [128, D], F32)
            tp = self.norm_pools(S)
            xpool = S.pool("xt", [128, D], F32, 2)
            hbpool = S.pool("hb", [128, D], BF16, 2)
            pstp = S.pspool("pst", 2, BF16)
            psm = S.pspool("psm", 4, F32)
            wpool = S.pool("wb", [128, 16, 256], BF16, 3)
            opool = S.pool("ost", [128, S1], F32, 2)
            otpool = S.pool("ott", [128, 256], F32, 2)
            P.dma(gt[:], self.inp["ln_g"][l].partition_broadcast(128), writes=[gt])
            for seg in range(L // S1):
                t0 = seg * S1
                self.make_hT(S, lambda tt: xsrc[t0 + tt * 128:t0 + (tt + 1) * 128, :], gt, hT, S1 // 128, tp, xpool, hbpool, pstp,
                             lambda tt: [xsrc_region(t0 + tt * 128)])
                for (nm, off, ncols, perm) in fgroups:
                    for c0 in range(0, ncols, 256):
                        wb = wpool.next()
                        src = w_in[:, off + c0:off + c0 + 256].rearrange("(k p) c -> p k c", p=128)
                        if not perm:
                            P.dma(wb[:], src, writes=[wb], q="pool")
                        else:
                            s4 = src.rearrange("p k (b two f) -> p k b two f", two=2, f=32)
                            d4 = wb[:].rearrange("p k (b two f) -> p k b two f", two=2, f=32)
                            for kq in range(0, 16, 4):
                                P.dma(d4[:, kq:kq + 4, :, 0, :], s4[:, kq:kq + 4, :, 1, :], writes=[wb], q="pool")
                                P.dma(d4[:, kq:kq + 4, :, 1, :], s4[:, kq:kq + 4, :, 0, :], writes=[wb], q="pool")
                        for ct in range(2):
                            ost = opool.next()
                            TS = min(512, S1)
                            for ts in range(S1 // TS):
                                ps = psm.next()
                                for k in range(16):
                                    P.op("pe", lambda e, k=k, ps=ps, wb=wb, ct=ct, ts=ts: e.matmul(
                                        ps[:, 0:TS], lhsT=wb[:, k, ct * 128:(ct + 1) * 128], rhs=hT[:, k, ts * TS:(ts + 1) * TS],
                                        start=(k == 0), stop=(k == 15)), reads=[wb, hT], writes=[ps], inc=(k == 15))
                                self.copy(self.evac_eng(), ost[:, ts * TS:(ts + 1) * TS], ps[:, 0:TS], [ps], [ost])
                            r0 = c0 + ct * 128
                            P.dma(self.scr[nm][r0:r0 + 128, t0:t0 + S1], ost[:, :], reads=[ost],
                                  writes=[self.P.region(nm, r0 // 128, seg)])
                for (nm, off, ncols) in tgroups:
                    for c0 in range(0, ncols, 256):
                        cw = min(256, ncols - c0)
                        wb = wpool.next()
                        src = w_in[:, off + c0:off + c0 + cw].rearrange("(k p) c -> p k c", p=128)
                        P.dma(wb[:, :, 0:cw], src, writes=[wb], q="pool")
                        for tt in range(S1 // 128):
                            ps = psm.next()
                            for k in range(16):
                                P.op("pe", lambda e, k=k, ps=ps, wb=wb, tt=tt, cw=cw: e.matmul(
                                    ps[:, 0:cw], lhsT=hT[:, k, tt * 128:(tt + 1) * 128], rhs=wb[:, k, 0:cw],
                                    start=(k == 0), stop=(k == 15)), reads=[wb, hT], writes=[ps], inc=(k == 15))
                            ott = otpool.next()
                            self.copy(self.evac_eng(), ott[:, 0:cw], ps[:, 0:cw], [ps], [ott])
                            tok = t0 + tt * 128
                            P.dma(self.scr[nm][tok:tok + 128, c0:c0 + cw], ott[:, 0:cw], reads=[ott],
                                  writes=[self.P.region(nm, "t", tok // 128, c0)])

    def sgu(self, l, L):
        P = self.P
        idf = self.c["identf"]
        with Scope(P) as S:
            wsn = S.sb("wsn", [128, 8, 128], F32)
            wsT = S.sb("wsT", [128, 8, 128], BF16)
            lng = S.sb("lng", [128, BR], F32)
            lnb = S.sb("lnb", [128, BR], F32)
            bsb = S.sb("bsb", [128, BR], F32)
            pst = S.pspool("pst", 2, F32)
            psm = S.pspool("psm", 4, F32)
            P.dma(wsn[:], self.inp["sgu_w"][l].rearrange("g p q -> p g q"), writes=[wsn])
            P.dma(lng[:], self.inp["sgu_ln_g"][l].partition_broadcast(128), writes=[lng])
            P.dma(lnb[:], self.inp["sgu_ln_b"][l].partition_broadcast(128), writes=[lnb])
            P.dma(bsb[:], self.inp["sgu_b"][l].rearrange("g p -> (g p)").partition_broadcast(128), writes=[bsb])
            for g in range(8):
                pt = pst.next()
                P.op("pe", lambda e, g=g, pt=pt: e.transpose(out=pt[:, 0:128], in_=wsn[:, g, :], identity=idf[:]), reads=[wsn, idf], writes=[pt])
                self.copy(self.evac_eng(), wsT[:, g, :], pt[:, 0:128], [pt], [wsT])
            vpool = S.pool("v", [128, BR], F32, 2)
            vnpool = S.pool("vn", [128, BR], F32, 2)
            vbpool = S.pool("vb", [128, BR], BF16, 2)
            stp = S.pool("st", [128, 2, 6], F32, 2)
            mvp = S.pool("mv", [128, 2], F32, 2)
            rsp = S.pool("rs", [128, 1], F32, 2)
            rstdp = S.pool("rstd", [128, 1], F32, 2)
            gup = S.pool("gu", [128, 8, 128], F32, 2)
            gzp = S.pool("gz", [128, 8, 128], F32, 2)
            szp = S.pool("sz", [128, 8, 128], F32, 2)
            t1p = S.pool("t1", [128, 8, 128], F32, 2)
            obp = S.pool("ob", [128, 8, 128], BF16, 2)
            for n in range(L // 128):
                tok = n * 128
                v = vpool.next()
                P.dma(v[:], self.scr["gv"][tok:tok + 128, :], reads=[P.region("gv", "t", n, c0) for c0 in range(0, 1024, 256)], writes=[v])
                gu = gup.next(); gz = gzp.next()
                seg = tok // min(L, 2048)
                P.dma(gu[:], self.scr["gu"][:, tok:tok + 128].rearrange("(g d) t -> d g t", d=128),
                      reads=[P.region("gu", g, seg) for g in range(8)], writes=[gu])
                P.dma(gz[:], self.scr["gz"][:, tok:tok + 128].rearrange("(g d) t -> d g t", d=128),
                      reads=[P.region("gz", g, seg) for g in range(8)], writes=[gz])
                st = stp.next(); mv = mvp.next(); rs = rsp.next(); rstd = rstdp.next()
                for hh in range(2):
                    P.op("dve", lambda e, hh=hh, st=st, v=v: e.bn_stats(out=st[:, hh, :], in_=v[:, hh * 512:(hh + 1) * 512]), reads=[v], writes=[st])
                P.op("dve", lambda e, st=st, mv=mv: e.bn_aggr(out=mv[:], in_=st[:].rearrange("p a b -> p (a b)")), reads=[st], writes=[mv])
                P.op("act", lambda e, mv=mv, rs=rs: e.activation(out=rs[:], in_=mv[:, 1:2], func=AF.Sqrt, scale=1.0, bias=EPS), reads=[mv], writes=[rs])
                P.op("dve", lambda e, rs=rs, rstd=rstd: e.reciprocal(out=rstd[:], in_=rs[:]), reads=[rs], writes=[rstd])
                vn = vnpool.next(); vb = vbpool.next()
                P.op("dve", lambda e, v=v, vn=vn, mv=mv, rstd=rstd: e.tensor_scalar(out=vn[:], in0=v[:], scalar1=mv[:, 0:1], scalar2=rstd[:],
                                                                                     op0=ALU.subtract, op1=ALU.mult), reads=[v, mv, rstd], writes=[vn])
                P.op("dve", lambda e, vn=vn: e.tensor_tensor(out=vn[:], in0=vn[:], in1=lng[:], op=ALU.mult), reads=[vn, lng], writes=[vn])
                P.op("dve", lambda e, vn=vn, vb=vb: e.tensor_tensor(out=vb[:], in0=vn[:], in1=lnb[:], op=ALU.add), reads=[vn, lnb], writes=[vb])
                sz = szp.next()
                P.op("act", lambda e, sz=sz, gz=gz: e.activation(out=sz[:], in_=gz[:], func=AF.Silu), reads=[gz], writes=[sz])
                t1 = t1p.next(); ob = obp.next()
                for hh in range(2):
                    ps = psm.next()
                    for gg in range(4):
                        g = hh * 4 + gg
                        P.op("pe", lambda e, g=g, gg=gg, ps=ps, vb=vb: e.matmul(ps[:, gg * 128:(gg + 1) * 128], lhsT=vb[:, g * 128:(g + 1) * 128], rhs=wsT[:, g, :],
                                                                              start=True, stop=True), reads=[vb, wsT], writes=[ps], inc=(gg == 3))
                    P.op("dve", lambda e, ps=ps, t1=t1, hh=hh: e.tensor_tensor(out=t1[:, hh * 4:(hh + 1) * 4, :], in0=ps[:, :].rearrange("p (g q) -> p g q", g=4),
                                                                              in1=bsb[:, hh * 512:(hh + 1) * 512].rearrange("p (g q) -> p g q", g=4), op=ALU.add),
                         reads=[ps, bsb], writes=[t1])
                P.op("dve", lambda e, t1=t1, gu=gu: e.tensor_tensor(out=t1[:], in0=t1[:], in1=gu[:], op=ALU.mult), reads=[t1, gu], writes=[t1])
                P.op("dve", lambda e, t1=t1, sz=sz, ob=ob: e.tensor_tensor(out=ob[:], in0=t1[:], in1=sz[:], op=ALU.mult), reads=[t1, sz], writes=[ob])
                P.dma(self.scr["brT"][2 * BR:3 * BR, tok:tok + 128].rearrange("(g d) t -> d g t", d=128), ob[:], reads=[ob],
                      writes=[P.region("brT", 2, n, f) for f in range(8)])

    def phase3(self, l, sn, L, xsrc, xsrc_region, last):
        P = self.P
        S3 = min(512, L)
        NT3 = S3 // 128
        w_in = self.inp["w_in"][l]
        wbr = self.inp["w_branch"][l]
        wout = self.inp["w_out"][l]
        with Scope(P) as S:
            hT = S.sb("hT", [128, 16, S3], BF16)
            gt = S.sb("gt", [128, D], F32)
            fg = S.sb("fg", [128, D], F32)
            tp = self.norm_pools(S)
            xall = [S.sb("xa%d" % i, [128, D], F32) for i in range(NT3)]
            hbpool = S.pool("hb", [128, D], BF16, 2)
            pstp = S.pspool("pst", 2, BF16)
            psg = S.pspool("psg", 2, F32)
            psb = S.pspool("psb", 2, F32)
            pso = S.pspool("pso", 2, F32)
            wgp = S.pool("wg", [128, 16, 128], BF16, 3)
            wbp = S.pool("wbb", [128, 8, 128], BF16, 3)
            wop = S.pool("wo", [128, 16, 256], BF16, 2)
            btp = S.pool("bt", [128, 8, S3], BF16, 2)
            sgp = S.pool("sg", [128, S3], F32, 2)
            tmp = S.pool("tmp", [128, S3], F32, 2)
            merged = S.sb("merged", [128, 16, S3], F32)
            mergedb = S.sb("mergedb", [128, 16, S3], BF16)
            yop = S.pool("yo", [128, D], F32, 1)
            P.dma(gt[:], self.inp["ln_g"][l].partition_broadcast(128), writes=[gt])
            if last:
                P.dma(fg[:], self.inp["final_g"].partition_broadcast(128), writes=[fg])
            for seg in range(L // S3):
                t0 = seg * S3
                xr = Rot(xall)
                self.make_hT(S, lambda tt: xsrc[t0 + tt * 128:t0 + (tt + 1) * 128, :], gt, hT, NT3, tp, xr, hbpool, pstp,
                             lambda tt: [xsrc_region(t0 + tt * 128)])
                for c in range(5):
                    bt = btp.next()
                    P.dma(bt[:], self.scr["brT"][c * BR:(c + 1) * BR, t0:t0 + S3].rearrange("(k p) t -> p k t", p=128),
                          reads=[P.region("brT", c, t0 // 128 + i, f) for i in range(NT3) for f in range(8)], writes=[bt])
                    for f in range(16):
                        wg = wgp.next(); wb = wbp.next()
                        co = OFF["gates"] + c * D + f * 128
                        P.dma(wg[:], w_in[:, co:co + 128].rearrange("(k p) c -> p k c", p=128), writes=[wg], q="pool")
                        P.dma(wb[:], wbr[c][:, f * 128:(f + 1) * 128].rearrange("(k p) c -> p k c", p=128), writes=[wb], q="pool")
                        pg = psg.next(); pb = psb.next()
                        for k in range(16):
                            P.op("pe", lambda e, k=k, pg=pg, wg=wg: e.matmul(pg[:, 0:S3], lhsT=wg[:, k, :], rhs=hT[:, k, :], start=(k == 0), stop=(k == 15)),
                                 reads=[wg, hT], writes=[pg], inc=(k == 15))
                        for k in range(8):
                            P.op("pe", lambda e, k=k, pb=pb, wb=wb, bt=bt: e.matmul(pb[:, 0:S3], lhsT=wb[:, k, :], rhs=bt[:, k, :], start=(k == 0), stop=(k == 7)),
                                 reads=[wb, bt], writes=[pb], inc=(k == 7))
                        sg = sgp.next()
                        P.op("act", lambda e, sg=sg, pg=pg: e.activation(out=sg[:], in_=pg[:, 0:S3], func=AF.Sigmoid), reads=[pg], writes=[sg])
                        if c == 0:
                            P.op("dve", lambda e, sg=sg, pb=pb, f=f: e.tensor_tensor(out=merged[:, f, :], in0=pb[:, 0:S3], in1=sg[:], op=ALU.mult),
                                 reads=[sg, pb], writes=[merged])
                        else:
                            tm = tmp.next()
                            P.op("dve", lambda e, sg=sg, pb=pb, tm=tm: e.tensor_tensor(out=tm[:], in0=pb[:, 0:S3], in1=sg[:], op=ALU.mult),
                                 reads=[sg, pb], writes=[tm])
                            P.op("dve", lambda e, tm=tm, f=f: e.tensor_tensor(out=merged[:, f, :], in0=merged[:, f, :], in1=tm[:], op=ALU.add),
                                 reads=[tm, merged], writes=[merged])
                for f4 in range(4):
                    self.copy("act", mergedb[:, f4 * 4:(f4 + 1) * 4, :], merged[:, f4 * 4:(f4 + 1) * 4, :], [merged], [mergedb])
                for nb in range(8):
                    wo = wop.next()
                    P.dma(wo[:], wout[:, nb * 256:(nb + 1) * 256].rearrange("(k p) c -> p k c", p=128), writes=[wo], q="pool")
                    for tt in range(NT3):
                        po = pso.next()
                        for k in range(16):
                            P.op("pe", lambda e, k=k, po=po, wo=wo, tt=tt: e.matmul(po[:, 0:256], lhsT=mergedb[:, k, tt * 128:(tt + 1) * 128], rhs=wo[:, k, :],
                                                                                     start=(k == 0), stop=(k == 15)), reads=[mergedb, wo], writes=[po], inc=(k == 15))
                        xt = xall[tt]
                        P.op("dve", lambda e, po=po, xt=xt, nb=nb: e.tensor_tensor(out=xt[:, nb * 256:(nb + 1) * 256], in0=po[:, 0:256], in1=xt[:, nb * 256:(nb + 1) * 256], op=ALU.add),
                             reads=[po, xt], writes=[xt])
                for tt in range(NT3):
                    tok = t0 + tt * 128
                    xt = xall[tt]
                    if not last:
                        P.dma(self.scr["xres"][tok:tok + 128, :], xt[:], reads=[xt], writes=[P.region("xres", tok // 128)])
                    else:
                        yo = yop.next()
                        self.rmsnorm_tile(xt, fg, yo, tp)
                        P.dma(self.out[sn][tok:tok + 128, :], yo[:], reads=[yo], is_out=True)

    def zero_branch(self, c, L):
        P = self.P
        with Scope(P) as S:
            z = S.sb("z", [128, 8, 128], BF16)
            P.op("pool", lambda e: e.memset(z[:], 0.0), writes=[z])
            for n in range(L // 128):
                P.dma(self.scr["brT"][c * BR:(c + 1) * BR, n * 128:(n + 1) * 128].rearrange("(g d) t -> d g t", d=128), z[:], reads=[z],
                      writes=[P.region("brT", c, n, f) for f in range(8)])

    def build(self):
        P = self.P
        with Scope(P) as S0:
            self.consts(S0)
            for sn, L in self.seqs:
                for l in range(self.depth):
                    if l == 0:
                        xsrc = self.inp["x_" + sn]
                        xreg = lambda tok: P.region("xin", sn, tok // 128)
                    else:
                        xsrc = self.scr["xres"]
                        xreg = lambda tok: P.region("xres", tok // 128)
                    self.phase1(l, sn, L, xsrc, xreg)
                    for c in range(5):
                        if c not in self.branches:
                            self.zero_branch(c, L)
                    if 2 in self.branches:
                        self.sgu(l, L)
                    if 1 in self.branches:
                        self.s5(l, L)
                    if 3 in self.branches:
                        self.deltanet(l, L)
                    if 4 in self.branches:
                        self.xattn(l, L, sn)
                    if 0 in self.branches:
                        self.attention(l, L)
                    self.phase3(l, sn, L, xsrc, xreg, last=(l == self.depth - 1))
        P.emit()


def _sincos(k, S, x, W, name):
    P = k.P
    xx = S.sb(name + "_xx", [128, 2, W], F32)
    u = S.sb(name + "_u", [128, 2, W], F32)
    ki = S.sb(name + "_ki", [128, 2, W], I32)
    kf = S.sb(name + "_kf", [128, 2, W], F32)
    out = S.sb(name + "_sc", [128, 2, W], F32)
    P.op("dve", lambda e: e.tensor_copy(out=xx[:, 0, :], in_=x[:]), reads=[x], writes=[xx])
    P.op("dve", lambda e: e.tensor_scalar(out=xx[:, 1, :], in0=x[:], scalar1=math.pi / 2, scalar2=None, op0=ALU.add), reads=[x, xx], writes=[xx])
    _reduce_sin(k, xx, u, ki, kf, out)
    return out


def _reduce_sin(k, xx, u, ki, kf, out):
    P = k.P
    C1 = float(np.float32(TWO_PI))
    C2 = float(TWO_PI - np.float64(np.float32(TWO_PI)))
    P.op("dve", lambda e: e.tensor_scalar(out=u[:], in0=xx[:], scalar1=1.0 / TWO_PI, scalar2=None, op0=ALU.mult), reads=[xx], writes=[u])
    P.op("dve", lambda e: e.tensor_copy(out=ki[:], in_=u[:]), reads=[u], writes=[ki])
    P.op("dve", lambda e: e.tensor_copy(out=kf[:], in_=ki[:]), reads=[ki], writes=[kf])
    P.op("dve", lambda e: e.scalar_tensor_tensor(out=u[:], in0=kf[:], scalar=-C1, in1=xx[:], op0=ALU.mult, op1=ALU.add), reads=[kf, xx], writes=[u])
    P.op("dve", lambda e: e.scalar_tensor_tensor(out=u[:], in0=kf[:], scalar=-C2, in1=u[:], op0=ALU.mult, op1=ALU.add), reads=[kf, u], writes=[u])
    P.op("dve", lambda e: e.tensor_scalar(out=kf[:], in0=u[:], scalar1=math.pi, scalar2=-TWO_PI, op0=ALU.is_gt, op1=ALU.mult), reads=[u], writes=[kf])
    P.op("dve", lambda e: e.tensor_tensor(out=u[:], in0=u[:], in1=kf[:], op=ALU.add), reads=[u, kf], writes=[u])
    P.op("dve", lambda e: e.tensor_scalar(out=kf[:], in0=u[:], scalar1=-math.pi, scalar2=TWO_PI, op0=ALU.is_lt, op1=ALU.mult), reads=[u], writes=[kf])
    P.op("dve", lambda e: e.tensor_tensor(out=u[:], in0=u[:], in1=kf[:], op=ALU.add), reads=[u, kf], writes=[u])
    P.op("dve", lambda e: e.tensor_scalar(out=u[:], in0=u[:], scalar1=math.pi, scalar2=-math.pi, op0=ALU.min, op1=ALU.max), reads=[u], writes=[u])
    P.op("act", lambda e: e.activation(out=out[:], in_=u[:], func=AF.Sin), reads=[u], writes=[out])


def s5(self, l, L):
    P = self.P
    idf = self.c["identf"]
    Tc = min(512, L)
    NCH = L // Tc
    inp = self.inp
    with Scope(P) as S:
        pst = S.pspool("pst", 2, F32)
        psr = S.pspool("psr", 1, F32)
        psi = S.pspool("psi", 1, F32)
        pgr = S.pspool("pgr", 1, F32)
        pgi = S.pspool("pgi", 1, F32)
        psy = S.pspool("psy", 2, F32)
        prm = []
        for d in range(2):
            lam = {}
            for nm, src in (("lr", inp["ssm_lam_re"]), ("li", inp["ssm_lam_im"])):
                nat = S.sb("nat" + nm, [32, 128], F32)
                P.dma(nat[:], src[l, d].rearrange("(j a) n -> j (a n)", a=2), writes=[nat])
                pt = pst.next()
                P.op("pe", lambda e, pt=pt, nat=nat: e.transpose(out=pt[:, 0:32], in_=nat[:, :], identity=idf[0:32, 0:32]), reads=[nat, idf], writes=[pt])
                t = S.sb(nm + str(d), [128, 32], F32)
                self.copy("dve", t[:], pt[:, 0:32], [pt], [t])
                lam[nm] = t
            ldtb = S.sb("ldtb", [128, 64], F32)
            P.dma(ldtb[:], inp["ssm_log_dt"][l, d].partition_broadcast(128), writes=[ldtb])
            ldt = S.sb("ldt", [128, 32], F32)
            P.op("dve", lambda e, ldt=ldt, ldtb=ldtb: e.tensor_copy(out=ldt[0:64, :], in_=ldtb[0:64, 0:64:2]), reads=[ldtb], writes=[ldt])
            P.op("dve", lambda e, ldt=ldt, ldtb=ldtb: e.tensor_copy(out=ldt[64:128, :], in_=ldtb[64:128, 1:64:2]), reads=[ldtb, ldt], writes=[ldt])
            dt = S.sb("dt", [128, 32], F32)
            P.op("act", lambda e, dt=dt, ldt=ldt: e.activation(out=dt[:], in_=ldt[:], func=AF.Exp), reads=[ldt], writes=[dt])
            lr, li = lam["lr"], lam["li"]
            rho = S.sb("rho%d" % d, [128, 32], F32)
            th = S.sb("th%d" % d, [128, 32], F32)
            tmp = S.sb("tmpa", [128, 32], F32)
            P.op("dve", lambda e, tmp=tmp, lr=lr, dt=dt: e.tensor_tensor(out=tmp[:], in0=lr[:], in1=dt[:], op=ALU.mult), reads=[lr, dt], writes=[tmp])
            P.op("act", lambda e, tmp=tmp, rho=rho: e.activation(out=rho[:], in_=tmp[:], func=AF.Exp), reads=[tmp], writes=[rho])
            P.op("dve", lambda e, th=th, li=li, dt=dt: e.tensor_tensor(out=th[:], in0=li[:], in1=dt[:], op=ALU.mult), reads=[li, dt], writes=[th])
            sc = _sincos(self, S, th, 32, "p%d" % d)
            ar = S.sb("ar", [128, 32], F32); ai = S.sb("ai", [128, 32], F32); am1 = S.sb("am1", [128, 32], F32)
            P.op("dve", lambda e, ar=ar, rho=rho, sc=sc: e.tensor_tensor(out=ar[:], in0=rho[:], in1=sc[:, 1, :], op=ALU.mult), reads=[rho, sc], writes=[ar])
            P.op("dve", lambda e, ai=ai, rho=rho, sc=sc: e.tensor_tensor(out=ai[:], in0=rho[:], in1=sc[:, 0, :], op=ALU.mult), reads=[rho, sc], writes=[ai])
            P.op("dve", lambda e, ar=ar, am1=am1: e.tensor_scalar(out=am1[:], in0=ar[:], scalar1=-1.0, scalar2=None, op0=ALU.add), reads=[ar], writes=[am1])
            den = S.sb("den", [128, 32], F32); t2 = S.sb("t2", [128, 32], F32)
            P.op("dve", lambda e, den=den, lr=lr: e.tensor_tensor(out=den[:], in0=lr[:], in1=lr[:], op=ALU.mult), reads=[lr], writes=[den])
            P.op("dve", lambda e, t2=t2, li=li: e.tensor_tensor(out=t2[:], in0=li[:], in1=li[:], op=ALU.mult), reads=[li], writes=[t2])
            P.op("dve", lambda e, den=den, t2=t2: e.tensor_tensor(out=den[:], in0=den[:], in1=t2[:], op=ALU.add), reads=[den, t2], writes=[den])
            rden = S.sb("rden", [128, 32], F32)
            P.op("dve", lambda e, den=den, rden=rden: e.reciprocal(out=rden[:], in_=den[:]), reads=[den], writes=[rden])
            fr = S.sb("fr%d" % d, [128, 32], F32); fi = S.sb("fi%d" % d, [128, 32], F32); nfi = S.sb("nfi%d" % d, [128, 32], F32)
            P.op("dve", lambda e, t2=t2, am1=am1, lr=lr: e.tensor_tensor(out=t2[:], in0=am1[:], in1=lr[:], op=ALU.mult), reads=[am1, lr, t2], writes=[t2])
            P.op("dve", lambda e, tmp=tmp, ai=ai, li=li: e.tensor_tensor(out=tmp[:], in0=ai[:], in1=li[:], op=ALU.mult), reads=[ai, li, tmp], writes=[tmp])
            P.op("dve", lambda e, t2=t2, tmp=tmp: e.tensor_tensor(out=t2[:], in0=t2[:], in1=tmp[:], op=ALU.add), reads=[t2, tmp], writes=[t2])
            P.op("dve", lambda e, t2=t2, fr=fr, rden=rden: e.tensor_tensor(out=fr[:], in0=t2[:], in1=rden[:], op=ALU.mult), reads=[t2, rden], writes=[fr])
            P.op("dve", lambda e, t2=t2, ai=ai, lr=lr: e.tensor_tensor(out=t2[:], in0=ai[:], in1=lr[:], op=ALU.mult), reads=[ai, lr, t2], writes=[t2])
            P.op("dve", lambda e, tmp=tmp, am1=am1, li=li: e.tensor_tensor(out=tmp[:], in0=am1[:], in1=li[:], op=ALU.mult), reads=[am1, li, tmp], writes=[tmp])
            P.op("dve", lambda e, t2=t2, tmp=tmp: e.tensor_tensor(out=t2[:], in0=t2[:], in1=tmp[:], op=ALU.subtract), reads=[t2, tmp], writes=[t2])
            P.op("dve", lambda e, t2=t2, fi=fi, rden=rden: e.tensor_tensor(out=fi[:], in0=t2[:], in1=rden[:], op=ALU.mult), reads=[t2, rden], writes=[fi])
            P.op("dve", lambda e, nfi=nfi, fi=fi: e.tensor_scalar(out=nfi[:], in0=fi[:], scalar1=-1.0, scalar2=None, op0=ALU.mult), reads=[fi], writes=[nfi])
            prm.append(dict(rho=rho, th=th, fr=fr, fi=fi, nfi=nfi))
        iot = S.sb("iot", [128, Tc], F32)
        P.op("pool", lambda e: e.iota(iot[:], pattern=[[1, Tc]], base=1, channel_multiplier=0, allow_small_or_imprecise_dtypes=True), writes=[iot])
        dcol = S.sb("dcol", [128, 8], F32)
        P.dma(dcol[:], inp["ssm_d"][l].rearrange("(o p) -> p o", p=128), writes=[dcol], slow=True)
        Bpad = [[S.sb("Bpad%d%d" % (j, ri), [128, 128], F32) for ri in range(2)] for j in range(4)]
        Cpad = [[S.sb("Cpad%d%d" % (j, ri), [128, 128], F32) for ri in range(2)] for j in range(4)]
        for j in range(4):
            for ri in range(2):
                for t in (Bpad[j][ri], Cpad[j][ri]):
                    P.op("pool", lambda e, t=t: e.memset(t[:], 0.0), writes=[t])
        BT = [[[S.sb("BT%d%d%d" % (j, d, ri), [128, 128], F32) for ri in range(2)] for d in range(2)] for j in range(4)]
        CT = [[[S.sb("CT%d%d%d" % (j, d, ri), [128, 128], F32) for ri in range(3)] for d in range(2)] for j in range(4)]
        tabs = [[S.sb("tab%d%d" % (j, d), [128, 2, Tc], F32) for d in range(2)] for j in range(4)]
        rhoT = [[S.sb("rhoT%d%d" % (j, d), [128, Tc], F32) for d in range(2)] for j in range(4)]
        state = [[S.sb("state%d%d" % (j, d), [128, 2], F32) for d in range(2)] for j in range(4)]
        tb_xx = S.sb("tb_xx", [128, 2, Tc], F32); tb_u = S.sb("tb_u", [128, 2, Tc], F32)
        tb_ki = S.sb("tb_ki", [128, 2, Tc], I32); tb_kf = S.sb("tb_kf", [128, 2, Tc], F32)
        bnat = S.pool("bnat", [128, 2, 16], F32, 2)
        bw = S.pool("bw", [128, 2, 16], F32, 2)
        yacc = S.sb("yacc", [128, L], F32)
        up = S.pool("u", [128, Tc], F32, 3)
        wk = {n: S.pool(n, [128, Tc], F32, 2) for n in ("t1", "t2", "re", "t3", "t4", "im", "h1", "h2", "h3", "h4")}
        gp = {n: S.pool("g" + n, [128, Tc], F32, 1) for n in ("y", "y2", "p", "in", "s")}
        syb = S.pool("syb", [128, Tc], BF16, 2)
        for o in range(8):
            for d in range(2):
                pr = prm[d]
                for j in range(4):
                    J = 4 * o + j
                    bn = bnat.next()
                    P.dma(bn[:, 0, :], inp["ssm_b_re"][l, d].rearrange("g n p -> (g n) p")[J * 128:(J + 1) * 128, :], writes=[bn])
                    P.dma(bn[:, 1, :], inp["ssm_b_im"][l, d].rearrange("g n p -> (g n) p")[J * 128:(J + 1) * 128, :], writes=[bn])
                    b2 = bw.next()
                    frc, fic, nfic = pr["fr"][:, J:J + 1], pr["fi"][:, J:J + 1], pr["nfi"][:, J:J + 1]
                    P.op("dve", lambda e, b2=b2, bn=bn, frc=frc: e.tensor_scalar(out=b2[:], in0=bn[:], scalar1=frc, scalar2=None, op0=ALU.mult), reads=[bn, pr["fr"]], writes=[b2])
                    for hh, c0 in ((0, (2 * j) * 16), (1, (2 * j + 1) * 16)):
                        ps_ = slice(hh * 64, (hh + 1) * 64)
                        P.op("dve", lambda e, b2=b2, bn=bn, ps_=ps_, c0=c0, j=j, sc_=pr["nfi"][ps_, J:J + 1]: e.scalar_tensor_tensor(
                            out=Bpad[j][0][ps_, c0:c0 + 16], in0=bn[ps_, 1, :], scalar=sc_, in1=b2[ps_, 0, :], op0=ALU.mult, op1=ALU.add),
                            reads=[bn, b2, pr["nfi"]], writes=[Bpad[j][0]])
                        P.op("dve", lambda e, b2=b2, bn=bn, ps_=ps_, c0=c0, j=j, sc_=pr["fi"][ps_, J:J + 1]: e.scalar_tensor_tensor(
                            out=Bpad[j][1][ps_, c0:c0 + 16], in0=bn[ps_, 0, :], scalar=sc_, in1=b2[ps_, 1, :], op0=ALU.mult, op1=ALU.add),
                            reads=[bn, b2, pr["fi"]], writes=[Bpad[j][1]])
                    for ri in range(2):
                        pt = pst.next()
                        P.op("pe", lambda e, pt=pt, j=j, ri=ri: e.transpose(out=pt[:, 0:128], in_=Bpad[j][ri][:, :], identity=idf[:]), reads=[Bpad[j][ri], idf], writes=[pt])
                        self.copy("act", BT[j][d][ri][:], pt[:, 0:128], [pt], [BT[j][d][ri]])
                    for ri, src in ((0, inp["ssm_c_re"]), (1, inp["ssm_c_im"])):
                        g0 = 8 * o + 2 * j
                        P.dma(Cpad[j][ri][(2 * j) * 16:(2 * j) * 16 + 16, 0:64], src[l, d, g0], writes=[Cpad[j][ri]])
                        P.dma(Cpad[j][ri][(2 * j + 1) * 16:(2 * j + 1) * 16 + 16, 64:128], src[l, d, g0 + 1], writes=[Cpad[j][ri]])
                        pt = pst.next()
                        P.op("pe", lambda e, pt=pt, j=j, ri=ri: e.transpose(out=pt[:, 0:128], in_=Cpad[j][ri][:, :], identity=idf[:]), reads=[Cpad[j][ri], idf], writes=[pt])
                        ct = CT[j][d][ri]
                        if ri == 0:
                            self.copy("act", ct[:], pt[:, 0:128], [pt], [ct])
                            ct2 = CT[j][d][2]
                            P.op("act", lambda e, ct2=ct2, pt=pt: e.activation(out=ct2[:], in_=pt[:, 0:128], func=AF.Copy, scale=-1.0), reads=[pt], writes=[ct2])
                        else:
                            P.op("act", lambda e, ct=ct, pt=pt: e.activation(out=ct[:], in_=pt[:, 0:128], func=AF.Copy, scale=-1.0), reads=[pt], writes=[ct])
                    P.op("dve", lambda e, sc_=pr["th"][:, J:J + 1]: e.tensor_scalar(out=tb_xx[:, 0, :], in0=iot[:], scalar1=sc_, scalar2=None, op0=ALU.mult),
                         reads=[iot, pr["th"]], writes=[tb_xx])
                    P.op("dve", lambda e: e.tensor_scalar(out=tb_xx[:, 1, :], in0=tb_xx[:, 0, :], scalar1=math.pi / 2, scalar2=None, op0=ALU.add),
                         reads=[tb_xx], writes=[tb_xx])
                    _reduce_sin(self, tb_xx, tb_u, tb_ki, tb_kf, tabs[j][d])
                    rt = rhoT[j][d]
                    P.op("dve", lambda e, rt=rt, sc_=pr["rho"][:, J:J + 1]: e.tensor_scalar(out=rt[:], in0=iot[:], scalar1=0.0, scalar2=sc_, op0=ALU.mult, op1=ALU.add),
                         reads=[iot, pr["rho"]], writes=[rt])
                    P.op("pool", lambda e, j=j, d=d: e.memset(state[j][d][:], 0.0), writes=[state[j][d]])
            for d in range(2):
                order = range(NCH) if d == 0 else range(NCH - 1, -1, -1)
                for ch in order:
                    t0 = ch * Tc
                    u = up.next()
                    seg = t0 // min(L, 2048)
                    P.dma(u[:], self.scr["su"][o * 128:(o + 1) * 128, t0:t0 + Tc], reads=[P.region("su", o, seg)], writes=[u])
                    uap = u[:, :] if d == 0 else u[:, ::-1]
                    yp = psy.next()
                    for j in range(4):
                        tab = tabs[j][d]; rt = rhoT[j][d]; st = state[j][d]
                        sn_, cs_ = tab[:, 0, :], tab[:, 1, :]
                        pr_, pi_ = psr.next(), psi.next()
                        P.op("pe", lambda e, pr_=pr_, j=j, d=d, uap=uap: e.matmul(pr_[:, 0:Tc], lhsT=BT[j][d][0][:], rhs=uap, start=True, stop=True), reads=[BT[j][d][0], u], writes=[pr_])
                        P.op("pe", lambda e, pi_=pi_, j=j, d=d, uap=uap: e.matmul(pi_[:, 0:Tc], lhsT=BT[j][d][1][:], rhs=uap, start=True, stop=True), reads=[BT[j][d][1], u], writes=[pi_])
                        w = {n: wk[n].next() for n in wk}

                        def tt(out, a, b, op, ra, rb, eng="dve"):
                            P.op(eng, lambda e: e.tensor_tensor(out=out[:], in0=a, in1=b, op=op), reads=ra + rb, writes=[out])
                        tt(w["t1"], pr_[:, 0:Tc], cs_, ALU.mult, [pr_], [tab])
                        tt(w["t2"], pi_[:, 0:Tc], sn_, ALU.mult, [pi_], [tab])
                        tt(w["re"], w["t1"][:], w["t2"][:], ALU.add, [w["t1"]], [w["t2"]])
                        tt(w["t3"], pi_[:, 0:Tc], cs_, ALU.mult, [pi_], [tab])
                        tt(w["t4"], pr_[:, 0:Tc], sn_, ALU.mult, [pr_], [tab])
                        tt(w["im"], w["t3"][:], w["t4"][:], ALU.subtract, [w["t3"]], [w["t4"]])
                        gr_, gi_ = pgr.next(), pgi.next()
                        P.op("dve", lambda e, w=w, rt=rt, st=st, gr_=gr_: e.tensor_tensor_scan(out=gr_[:, 0:Tc], data0=rt[:], data1=w["re"][:], initial=st[:, 0:1], op0=ALU.mult, op1=ALU.add),
                             reads=[rt, w["re"], st], writes=[gr_])
                        P.op("dve", lambda e, w=w, rt=rt, st=st, gi_=gi_: e.tensor_tensor_scan(out=gi_[:, 0:Tc], data0=rt[:], data1=w["im"][:], initial=st[:, 1:2], op0=ALU.mult, op1=ALU.add),
                             reads=[rt, w["im"], st], writes=[gi_])
                        tt(w["h1"], gr_[:, 0:Tc], cs_, ALU.mult, [gr_], [tab])
                        tt(w["h2"], gi_[:, 0:Tc], sn_, ALU.mult, [gi_], [tab])
                        tt(w["h3"], gr_[:, 0:Tc], sn_, ALU.mult, [gr_], [tab])
                        tt(w["h4"], gi_[:, 0:Tc], cs_, ALU.mult, [gi_], [tab])
                        self.tt(st[:, 0:1], w["h1"][:, Tc - 1:Tc], w["h2"][:, Tc - 1:Tc], ALU.subtract, [w["h1"], w["h2"]], [st])
                        self.tt(st[:, 1:2], w["h3"][:, Tc - 1:Tc], w["h4"][:, Tc - 1:Tc], ALU.add, [w["h3"], w["h4"], st], [st])
                        rv = (lambda b: b[:, :]) if d == 0 else (lambda b: b[:, ::-1])
                        self.mm(yp[:, 0:Tc], CT[j][d][0][:], rv(w["h1"]), [CT[j][d][0], w["h1"]], [yp], start=(j == 0), stop=False, inc=False)
                        self.mm(yp[:, 0:Tc], CT[j][d][2][:], rv(w["h2"]), [CT[j][d][2], w["h2"]], [yp], start=False, stop=False, inc=False)
                        self.mm(yp[:, 0:Tc], CT[j][d][1][:], rv(w["h3"]), [CT[j][d][1], w["h3"]], [yp], start=False, stop=False, inc=False)
                        self.mm(yp[:, 0:Tc], CT[j][d][1][:], rv(w["h4"]), [CT[j][d][1], w["h4"]], [yp], start=False, stop=(j == 3), inc=True)
                    if d == 0:
                        self.copy("act", yacc[:, t0:t0 + Tc], yp[:, 0:Tc], [yp], [yacc])
                        pass
                    else:
                        P.op("dve", lambda e, yp=yp, t0=t0: e.tensor_tensor(out=yacc[:, t0:t0 + Tc], in0=yp[:, 0:Tc], in1=yacc[:, t0:t0 + Tc], op=ALU.add), reads=[yp, yacc], writes=[yacc])
            if self.debug:
                P.dma(self.scr["dbg"][o * 128:(o + 1) * 128, 0:L], yacc[:, :], reads=[yacc])
            for ch in range(NCH):
                t0 = ch * Tc
                u = up.next()
                seg = t0 // min(L, 2048)
                P.dma(u[:], self.scr["su"][o * 128:(o + 1) * 128, t0:t0 + Tc], reads=[P.region("su", o, seg)], writes=[u])
                y = gp["y"].next(); y2 = gp["y2"].next(); pp = gp["p"].next(); inn = gp["in"].next(); sg = gp["s"].next(); ob = syb.next()
                P.op("dve", lambda e, u=u, y=y, t0=t0, o=o: e.scalar_tensor_tensor(out=y[:], in0=u[:], scalar=dcol[:, o:o + 1], in1=yacc[:, t0:t0 + Tc], op0=ALU.mult, op1=ALU.add),
                     reads=[u, dcol, yacc], writes=[y])
                P.op("dve", lambda e, y=y, y2=y2: e.tensor_tensor(out=y2[:], in0=y[:], in1=y[:], op=ALU.mult), reads=[y], writes=[y2])
                P.op("dve", lambda e, y2=y2, pp=pp: e.tensor_scalar(out=pp[:], in0=y2[:], scalar1=0.044715, scalar2=1.0, op0=ALU.mult, op1=ALU.add), reads=[y2], writes=[pp])
                P.op("dve", lambda e, pp=pp, y=y, inn=inn: e.tensor_tensor(out=inn[:], in0=pp[:], in1=y[:], op=ALU.mult), reads=[pp, y], writes=[inn])
                P.op("act", lambda e, inn=inn, sg=sg: e.activation(out=sg[:], in_=inn[:], func=AF.Sigmoid, scale=2.0 * math.sqrt(2.0 / math.pi)), reads=[inn], writes=[sg])
                P.op("dve", lambda e, sg=sg, y=y, ob=ob: e.tensor_tensor(out=ob[:], in0=sg[:], in1=y[:], op=ALU.mult), reads=[sg, y], writes=[ob])
                P.dma(self.scr["syT"][o * 128:(o + 1) * 128, t0:t0 + Tc], ob[:], reads=[ob], writes=[P.region("syT", o, ch)])
    with Scope(P) as S:
        gw = S.sb("gw", [128, 8, BR], BF16)
        gb = S.sb("gb", [128, 8], F32)
        P.dma(gw[:], inp["ssm_glu_w"][l].rearrange("(k p) c -> p k c", p=128), writes=[gw], q="pool")
        P.dma(gb[:], inp["ssm_glu_b"][l].rearrange("(f p) -> p f", p=128), writes=[gb], slow=True)
        syp = S.pool("sy", [128, 8, Tc], BF16, 2)
        szp = S.pool("sz", [128, Tc], F32, 2)
        szs = S.pool("szs", [128, Tc], F32, 2)
        sgp = S.pool("sg", [128, Tc], F32, 2)
        t1p = S.pool("t1", [128, Tc], F32, 2)
        obp = S.pool("ob", [128, Tc], BF16, 2)
        psm = S.pspool("psm", 3, F32)
        for ch in range(NCH):
            t0 = ch * Tc
            sy = syp.next()
            P.dma(sy[:], self.scr["syT"][:, t0:t0 + Tc].rearrange("(k p) t -> p k t", p=128), reads=[P.region("syT", o, ch) for o in range(8)], writes=[sy])
            seg = t0 // min(L, 2048)
            for f in range(8):
                ps = psm.next()
                for k_ in range(8):
                    P.op("pe", lambda e, ps=ps, k_=k_, f=f, sy=sy: e.matmul(ps[:, 0:Tc], lhsT=gw[:, k_, f * 128:(f + 1) * 128], rhs=sy[:, k_, :], start=(k_ == 0), stop=(k_ == 7)),
                         reads=[gw, sy], writes=[ps], inc=(k_ == 7))
                sz = szp.next(); zs = szs.next(); sg = sgp.next(); t1 = t1p.next(); ob = obp.next()
                P.dma(sz[:], self.scr["sz"][f * 128:(f + 1) * 128, t0:t0 + Tc], reads=[P.region("sz", f, seg)], writes=[sz])
                P.op("act", lambda e, sg=sg, ps=ps, f=f: e.activation(out=sg[:], in_=ps[:, 0:Tc], func=AF.Sigmoid, bias=gb[:, f:f + 1]), reads=[ps, gb], writes=[sg])
                P.op("act", lambda e, sz=sz, zs=zs: e.activation(out=zs[:], in_=sz[:], func=AF.Silu), reads=[sz], writes=[zs])
                P.op("dve", lambda e, sg=sg, sy=sy, t1=t1, f=f: e.tensor_tensor(out=t1[:], in0=sg[:], in1=sy[:, f, :], op=ALU.mult), reads=[sg, sy], writes=[t1])
                P.op("dve", lambda e, t1=t1, zs=zs, ob=ob: e.tensor_tensor(out=ob[:], in0=t1[:], in1=zs[:], op=ALU.mult), reads=[t1, zs], writes=[ob])
                for q4 in range(Tc // 128):
                    pass
                P.dma(self.scr["brT"][BR + f * 128:BR + (f + 1) * 128, t0:t0 + Tc], ob[:], reads=[ob],
                      writes=[P.region("brT", 1, t0 // 128 + i, f) for i in range(Tc // 128)])


K.s5 = s5


BRANCHES = (0, 1, 2, 3, 4)
_CACHE = {}


def kernel(**inputs):
    inputs = {k_: np.ascontiguousarray(np.asarray(v)) for k_, v in inputs.items()}
    nc = bass.Bass("TRN2", target_bir_lowering=False)
    kk = K(nc, [("p", 2048), ("s", 8192)], depth=2, debug=False, branches=BRANCHES)
    kk.build()
    wnames = [n for n in inputs if n not in ("x_prompt", "x_sample", "mem_prompt", "mem_sample")]
    in_maps = []
    for c in range(8):
        m = {n: inputs[n] for n in wnames}
        m["x_p"] = inputs["x_prompt"][c]
        m["mem_p"] = inputs["mem_prompt"][c]
        m["x_s"] = inputs["x_sample"][c // 4]
        m["mem_s"] = inputs["mem_sample"][c // 4]
        in_maps.append(m)
    res = run_bass_kernel_spmd(nc, in_maps, core_ids=list(range(8)))
    y_p = np.stack([np.asarray(res.results[c]["y_p"], dtype=np.float32) for c in range(8)])
    y_s = np.stack([np.asarray(res.results[c]["y_s"], dtype=np.float32) for c in (0, 4)])
    return (y_p, y_s)


def _tt(self, out, a, b, op, reads, writes, eng="dve"):
    self.P.op(eng, lambda e: e.tensor_tensor(out=out, in0=a, in1=b, op=op), reads=reads, writes=writes)


def _ts(self, out, a, s1, s2, op0, op1, reads, writes, eng="dve"):
    if s2 is None:
        self.P.op(eng, lambda e: e.tensor_scalar(out=out, in0=a, scalar1=s1, scalar2=None, op0=op0), reads=reads, writes=writes)
    else:
        self.P.op(eng, lambda e: e.tensor_scalar(out=out, in0=a, scalar1=s1, scalar2=s2, op0=op0, op1=op1), reads=reads, writes=writes)


def _stt(self, out, a, sc, b, op0, op1, reads, writes):
    self.P.op("dve", lambda e: e.scalar_tensor_tensor(out=out, in0=a, scalar=sc, in1=b, op0=op0, op1=op1), reads=reads, writes=writes)


def _mm(self, out, lhsT, rhs, reads, writes, start=True, stop=True, inc=True):
    self.P.op("pe", lambda e: e.matmul(out, lhsT=lhsT, rhs=rhs, start=start, stop=stop), reads=reads, writes=writes, inc=inc)


def _tr(self, out, in_, ident, reads, writes, inc=True):
    self.P.op("pe", lambda e: e.transpose(out=out, in_=in_, identity=ident), reads=reads, writes=writes, inc=inc)


def _act(self, out, in_, func, reads, writes, scale=1.0, bias=0.0):
    self.P.op("act", lambda e: e.activation(out=out, in_=in_, func=func, scale=scale, bias=bias), reads=reads, writes=writes)


def _cp(self, out, in_, reads, writes, eng=None):
    eng = eng or self.evac_eng()
    if eng == "act":
        self.P.op("act", lambda e: e.activation(out=out, in_=in_, func=AF.Copy), reads=reads, writes=writes)
    else:
        self.P.op(eng, lambda e: e.tensor_copy(out=out, in_=in_), reads=reads, writes=writes)


K.tt, K.ts, K.stt, K.mm, K.tr, K.act, K.cp = _tt, _ts, _stt, _mm, _tr, _act, _cp


class SlotBuf:
    psum = True

    def __init__(self, name, parent, ap):
        self.name = name
        self.parent = parent
        self.ap = ap

    def __getitem__(self, k):
        return self.ap[k]

    @property
    def last_w(self):
        return self.parent.last_w

    @last_w.setter
    def last_w(self, v):
        self.parent.last_w = v

    @property
    def readers(self):
        return self.parent.readers

    @readers.setter
    def readers(self, v):
        self.parent.readers = v


def _rr(gens):
    gens = list(gens)
    if RR_SEQ:
        for g in gens:
            for _ in g:
                pass
        return
    while gens:
        nxt = []
        for g in gens:
            try:
                next(g)
                nxt.append(g)
            except StopIteration:
                pass
        gens = nxt


def deltanet(self, l, L):
    P = self.P
    inp = self.inp
    idf = self.c["identf"]
    onesf = self.c["onesf"]
    NCHK = L // 128
    SEG = min(L, 2048)
    NH = 8
    with Scope(P) as S:
        LOW = S.sb("LOW", [128, 128], F32); UPP = S.sb("UPP", [128, 128], F32)
        SLOW = S.sb("SLOW", [128, 128], F32); SUPP = S.sb("SUPP", [128, 128], F32)
        for t, (cm, st, op) in ((LOW, (1, -1, ALU.is_ge)), (UPP, (-1, 1, ALU.is_ge)), (SLOW, (1, -1, ALU.is_gt)), (SUPP, (-1, 1, ALU.is_gt))):
            P.op("pool", lambda e, t=t: e.memset(t[:], 1.0), writes=[t])
            P.op("pool", lambda e, t=t, cm=cm, st=st, op=op: e.affine_select(out=t[:], in_=t[:], pattern=[[st, 128]], compare_op=op, fill=0.0, base=0, channel_multiplier=cm),
                 reads=[t], writes=[t])
        cw = S.sb("cw", [128, 3, 24], F32)
        for j_ in range(3):
            P.dma(cw[:, j_, :], inp["dn_conv_w"][l, j_].rearrange("(k p) -> p k", p=128), writes=[cw], slow=True)
        alb = S.sb("alb", [128, 16], F32); dtb = S.sb("dtb", [128, 16], F32); nea = S.sb("nea", [128, 16], F32)
        P.dma(alb[:], inp["dn_a_log"][l].rearrange("a h -> (a h)").partition_broadcast(128), writes=[alb])
        P.dma(dtb[:], inp["dn_dt_bias"][l].rearrange("a h -> (a h)").partition_broadcast(128), writes=[dtb])
        self.act(nea[:], alb[:], AF.Exp, [alb], [nea])
        self.ts(nea[:], nea[:], -1.0, None, ALU.mult, None, [nea], [nea])
        ng = S.sb("ng", [128, 128], F32)
        P.dma(ng[:], inp["dn_norm_g"][l].partition_broadcast(128), writes=[ng])
        slots = []
        for b in range(8):
            pb = S.ps("ppb%d" % b, F32)
            for q4 in range(4):
                slots.append(SlotBuf("pps%d_%d" % (b, q4), pb, pb.h[:, q4 * 128:(q4 + 1) * 128]))
        pp = Rot(slots)
        ppA = [Rot([slots[4 * (i // 3) + (i % 3)]]) for i in range(24)]
        ppH = [Rot(slots[4 * h:4 * h + 4]) for h in range(8)]
        xin = S.pool("xin", [128, 130], F32, 24)
        cvp = S.pool("cv", [128, 128], F32, 24)
        sqp = S.pool("sq", [128, 128], F32, 16)
        rsp = S.pool("rsq", [128, 128], F32, 16)
        qTp = S.pool("qT", [128, 128], F32, NH); kTp = S.pool("kT", [128, 128], F32, NH)
        ktp = S.pool("ktm", [128, 128], F32, NH); vtp = S.pool("vtm", [128, 128], F32, NH)
        dbp = S.pool("dba", [128, 32], F32, 2)
        sm = {n: S.pool(n, [128, 16], F32, 2) for n in ("beta", "nbeta", "x", "ax", "e", "ln", "mx", "g", "gcc", "gtot", "egc", "ekd", "glast", "cb", "dif")}
        wkn = ("gU", "egcb", "dm", "dec", "dst", "X", "Y", "X2", "Y2", "TT", "u", "wT", "att", "attT", "qgT", "kd", "vnew", "osb")
        alias = {"vb": "gU", "kbg": "dm", "of": "dst", "osum": "X", "junk": "Y", "on": "X2", "zt": "Y2", "zs": "dec"}
        wk = {n: S.pool(n, [128, 128], F32, NH) for n in wkn}
        obp = S.pool("obf", [128, 128], BF16, NH)
        ssp = S.pool("ss1", [128, 1], F32, NH); rs1 = S.pool("rs1", [128, 1], F32, NH); rstd1 = S.pool("rstd1", [128, 1], F32, NH)
        Sst = [[S.sb("S%d%d" % (h, d), [128, 128], F32) for d in range(2)] for h in range(8)]
        ofs = self.scr["dn_of"]
        for d in range(2):
            CUM, INCL, STRICT = (UPP, LOW, SLOW) if d == 0 else (LOW, UPP, SUPP)
            for h in range(8):
                P.op("pool", lambda e, t=Sst[h][d]: e.memset(t[:], 0.0), writes=[Sst[h][d]])
            order = range(NCHK) if d == 0 else range(NCHK - 1, -1, -1)
            for n in order:
                tok = n * 128
                seg = tok // SEG
                db = dbp.next()
                P.dma(db[:], self.scr["dba"][tok:tok + 128, :], reads=[P.region("dba", "t", n, 0)], writes=[db])
                s = {k_: sm[k_].next() for k_ in sm}
                self.act(s["beta"][:], db[:, 0:16], AF.Sigmoid, [db], [s["beta"]])
                self.ts(s["nbeta"][:], s["beta"][:], -1.0, None, ALU.mult, None, [s["beta"]], [s["nbeta"]])
                self.tt(s["x"][:], db[:, 16:32], dtb[:], ALU.add, [db, dtb], [s["x"]])
                self.ts(s["mx"][:], s["x"][:], 0.0, None, ALU.max, None, [s["x"]], [s["mx"]])
                self.stt(s["ax"][:], s["mx"][:], -2.0, s["x"][:], ALU.mult, ALU.add, [s["mx"], s["x"]], [s["ax"]])
                self.act(s["e"][:], s["ax"][:], AF.Exp, [s["ax"]], [s["e"]])
                self.act(s["ln"][:], s["e"][:], AF.Ln, [s["e"]], [s["ln"]], bias=1.0)
                self.tt(s["mx"][:], s["mx"][:], s["ln"][:], ALU.add, [s["mx"], s["ln"]], [s["mx"]])
                self.tt(s["g"][:], s["mx"][:], nea[:], ALU.mult, [s["mx"], nea], [s["g"]])
                pg = pp.next()
                self.mm(pg[:, 0:8], CUM[:], s["g"][:, d * 8:(d + 1) * 8], [CUM, s["g"]], [pg])
                self.mm(pg[:, 16:24], onesf[:], s["g"][:, d * 8:(d + 1) * 8], [onesf, s["g"]], [pg])
                self.cp(s["gcc"][:, 0:8], pg[:, 0:8], [pg], [s["gcc"]], eng="dve")
                self.cp(s["gtot"][:, 0:8], pg[:, 16:24], [pg], [s["gtot"]], eng="dve")
                self.act(s["egc"][:, 0:8], s["gcc"][:, 0:8], AF.Exp, [s["gcc"]], [s["egc"]])
                self.tt(s["dif"][:, 0:8], s["gtot"][:, 0:8], s["gcc"][:, 0:8], ALU.subtract, [s["gtot"], s["gcc"]], [s["dif"]])
                self.act(s["ekd"][:, 0:8], s["dif"][:, 0:8], AF.Exp, [s["dif"]], [s["ekd"]])
                self.act(s["glast"][:, 0:8], s["gtot"][:, 0:8], AF.Exp, [s["gtot"]], [s["glast"]])
                self.tt(s["cb"][:, 0:8], s["beta"][:, d * 8:(d + 1) * 8], s["egc"][:, 0:8], ALU.mult, [s["beta"], s["egc"]], [s["cb"]])
                heads = [dict() for _ in range(8)]

                def stageA(which, tile_i, h, pp):
                    xi = xin.next()
                    lo = max(tok - 1, 0); hi = min(tok + 129, L)
                    c_lo = lo - (tok - 1); c_hi = 130 - ((tok + 129) - hi)
                    if c_lo > 0 or c_hi < 130:
                        P.op("pool", lambda e: e.memset(xi[:], 0.0), writes=[xi])
                    rds = [P.region("dqkv", tile_i, sg_) for sg_ in sorted(set([lo // SEG, (hi - 1) // SEG]))]
                    P.dma(xi[:, c_lo:c_hi], self.scr["dqkv"][tile_i * 128:(tile_i + 1) * 128, lo:hi], reads=rds, writes=[xi])
                    cv = cvp.next()
                    self.ts(cv[:], xi[:, 0:128], cw[:, 0, tile_i:tile_i + 1], None, ALU.mult, None, [xi, cw], [cv])
                    self.stt(cv[:], xi[:, 1:129], cw[:, 1, tile_i:tile_i + 1], cv[:], ALU.mult, ALU.add, [xi, cw, cv], [cv])
                    self.stt(cv[:], xi[:, 2:130], cw[:, 2, tile_i:tile_i + 1], cv[:], ALU.mult, ALU.add, [xi, cw, cv], [cv])
                    yield
                    self.act(cv[:], cv[:], AF.Silu, [cv], [cv])
                    yield
                    if which == "v":
                        pt = pp.next()
                        self.tr(pt[:, :], cv[:], idf[:], [cv, idf], [pt])
                        yield
                        vtm = vtp.next()
                        self.cp(vtm[:], pt[:, :], [pt], [vtm])
                        heads[h]["vtm"] = vtm
                        return
                    sq = sqp.next()
                    self.tt(sq[:], cv[:], cv[:], ALU.mult, [cv], [sq])
                    pn = pp.next()
                    self.mm(pn[:, :], onesf[:], sq[:], [onesf, sq], [pn])
                    yield
                    rq = rsp.next()
                    self.act(rq[:], pn[:, :], AF.Sqrt, [pn], [rq], bias=EPS)
                    yield
                    P.op("dve", lambda e: e.reciprocal(out=rq[:], in_=rq[:]), reads=[rq], writes=[rq])
                    dst = (qTp if which == "q" else kTp).next()
                    if which == "q":
                        self.stt(dst[:], cv[:], 128.0 ** -0.5, rq[:], ALU.mult, ALU.mult, [cv, rq], [dst])
                        heads[h]["qT"] = dst
                        return
                    self.tt(dst[:], cv[:], rq[:], ALU.mult, [cv, rq], [dst])
                    heads[h]["kT"] = dst
                    pt = pp.next()
                    self.tr(pt[:, :], dst[:], idf[:], [dst, idf], [pt])
                    yield
                    ktm = ktp.next()
                    self.cp(ktm[:], pt[:, :], [pt], [ktm])
                    heads[h]["ktm"] = ktm

                _rr([stageA(wh, off + h, h, ppA[3 * h + i_]) for h in range(8) for i_, (wh, off) in enumerate((("q", 0), ("k", 8), ("v", 16)))])

                def unit(h):
                    pp = ppH[h]
                    w = {k_: wk[k_].next() for k_ in wk}
                    for a_, b_ in alias.items():
                        w[a_] = w[b_]
                    qT, kT, ktm, vtm = heads[h]["qT"], heads[h]["kT"], heads[h]["ktm"], heads[h]["vtm"]
                    gcol = s["g"][:, d * 8 + h:d * 8 + h + 1]
                    self.act(w["gU"][:], CUM[:], AF.Copy, [CUM, s["g"]], [w["gU"]], scale=gcol)
                    pgb = pp.next()
                    self.mm(pgb[:, :], onesf[:], w["gU"][:], [onesf, w["gU"]], [pgb])
                    pG = pp.next()
                    self.mm(pG[:, :], kT[:], kT[:], [kT], [pG])
                    pq = pp.next()
                    self.mm(pq[:, :], qT[:], kT[:], [qT, kT], [pq])
                    yield
                    self.act(w["egcb"][:], pgb[:, :], AF.Exp, [pgb], [w["egcb"]])
                    self.ts(w["dm"][:], pgb[:, :], -1.0, s["gcc"][:, h:h + 1], ALU.mult, ALU.add, [pgb, s["gcc"]], [w["dm"]])
                    self.ts(w["dm"][:], w["dm"][:], 0.0, None, ALU.min, None, [w["dm"]], [w["dm"]])
                    yield
                    self.act(w["dm"][:], w["dm"][:], AF.Exp, [w["dm"]], [w["dm"]])
                    yield
                    self.tt(w["dst"][:], w["dm"][:], STRICT[:], ALU.mult, [w["dm"], STRICT], [w["dst"]])
                    self.stt(w["X"][:], pG[:, :], s["nbeta"][:, d * 8 + h:d * 8 + h + 1], w["dst"][:], ALU.mult, ALU.mult, [pG, s["nbeta"], w["dst"]], [w["X"]])
                    pt = pp.next()
                    self.tr(pt[:, :], w["X"][:], idf[:], [w["X"], idf], [pt])
                    self.tt(w["dec"][:], w["dm"][:], INCL[:], ALU.mult, [w["dm"], INCL], [w["dec"]])
                    self.tt(w["att"][:], pq[:, :], w["dec"][:], ALU.mult, [pq, w["dec"]], [w["att"]])
                    pat = pp.next()
                    self.tr(pat[:, :], w["att"][:], idf[:], [w["att"], idf], [pat])
                    yield
                    self.cp(w["Y"][:], pt[:, :], [pt], [w["Y"]], eng="act")
                    self.cp(w["attT"][:], pat[:, :], [pat], [w["attT"]], eng="act")
                    self.tt(w["TT"][:], pt[:, :], idf[:], ALU.add, [pt, idf], [w["TT"]])
                    self.tt(w["qgT"][:], qT[:], w["egcb"][:], ALU.mult, [qT, w["egcb"]], [w["qgT"]])
                    yield
                    X, Y, X2, Y2 = w["X"], w["Y"], w["X2"], w["Y2"]
                    for kk in range(1, 7):
                        pX = pp.next()
                        self.mm(pX[:, :], Y[:], X[:], [X, Y], [pX])
                        if kk < 6:
                            pY = pp.next()
                            self.mm(pY[:, :], X[:], Y[:], [X, Y], [pY])
                        yield
                        self.cp(X2[:], pX[:, :], [pX], [X2], eng="act")
                        if kk < 6:
                            self.cp(Y2[:], pY[:, :], [pY], [Y2], eng="dve")
                        yield
                        pT = pp.next()
                        self.mm(pT[:, :], X2[:], w["TT"][:], [X2, w["TT"]], [pT])
                        yield
                        self.tt(w["TT"][:], w["TT"][:], pT[:, :], ALU.add, [w["TT"], pT], [w["TT"]])
                        X, X2 = X2, X
                        Y, Y2 = Y2, Y
                    TT = w["TT"]
                    self.act(w["vb"][:], vtm[:], AF.Copy, [vtm, s["beta"]], [w["vb"]], scale=s["beta"][:, d * 8 + h:d * 8 + h + 1])
                    self.act(w["kbg"][:], ktm[:], AF.Copy, [ktm, s["cb"]], [w["kbg"]], scale=s["cb"][:, h:h + 1])
                    self.act(w["kd"][:], ktm[:], AF.Copy, [ktm, s["ekd"]], [w["kd"]], scale=s["ekd"][:, h:h + 1])
                    yield
                    pu = pp.next()
                    self.mm(pu[:, :], TT[:], w["vb"][:], [TT, w["vb"]], [pu])
                    pw = pp.next()
                    self.mm(pw[:, :], w["kbg"][:], TT[:], [TT, w["kbg"]], [pw])
                    yield
                    self.cp(w["u"][:], pu[:, :], [pu], [w["u"]], eng="act")
                    self.cp(w["wT"][:], pw[:, :], [pw], [w["wT"]], eng="dve")
                    yield
                    St = Sst[h][d]
                    p1 = pp.next()
                    self.mm(p1[:, :], w["wT"][:], St[:], [w["wT"], St], [p1])
                    yield
                    self.tt(w["vnew"][:], w["u"][:], p1[:, :], ALU.subtract, [w["u"], p1], [w["vnew"]])
                    yield
                    p2 = pp.next()
                    self.mm(p2[:, 0:128], w["qgT"][:], St[:], [w["qgT"], St], [p2], start=True, stop=False, inc=False)
                    self.mm(p2[:, 0:128], w["attT"][:], w["vnew"][:], [w["attT"], w["vnew"]], [p2], start=False, stop=True)
                    p3 = pp.next()
                    self.mm(p3[:, :], w["kd"][:], w["vnew"][:], [w["kd"], w["vnew"]], [p3])
                    yield
                    self.stt(St[:], St[:], s["glast"][:, h:h + 1], p3[:, :], ALU.mult, ALU.add, [St, s["glast"], p3], [St])
                    if d == 0:
                        self.cp(w["osb"][:], p2[:, 0:128], [p2], [w["osb"]], eng="act")
                        P.dma(ofs[tok:tok + 128, h * 128:(h + 1) * 128], w["osb"][:], reads=[w["osb"]], writes=[P.region("dn_of", n, h)])
                        return
                    P.dma(w["of"][:], ofs[tok:tok + 128, h * 128:(h + 1) * 128], reads=[P.region("dn_of", n, h)], writes=[w["of"]])
                    P.dma(w["zt"][:], self.scr["dz"][h * 128:(h + 1) * 128, tok:tok + 128], reads=[P.region("dz", h, seg)], writes=[w["zt"]])
                    self.tt(w["osum"][:], w["of"][:], p2[:, 0:128], ALU.add, [w["of"], p2], [w["osum"]])
                    yield
                    ss = ssp.next(); r1 = rs1.next(); rstd = rstd1.next()
                    P.op("act", lambda e: e.activation(out=w["junk"][:], in_=w["osum"][:], func=AF.Square, accum_out=ss[:]), reads=[w["osum"]], writes=[w["junk"], ss])
                    self.act(r1[:], ss[:], AF.Sqrt, [ss], [r1], scale=1.0 / 128, bias=EPS)
                    yield
                    P.op("dve", lambda e: e.reciprocal(out=rstd[:], in_=r1[:]), reads=[r1], writes=[rstd])
                    self.stt(w["on"][:], w["osum"][:], rstd[:], ng[:], ALU.mult, ALU.mult, [w["osum"], rstd, ng], [w["on"]])
                    pt2 = pp.next()
                    self.tr(pt2[:, :], w["on"][:], idf[:], [w["on"], idf], [pt2])
                    self.act(w["zs"][:], w["zt"][:], AF.Silu, [w["zt"]], [w["zs"]])
                    yield
                    ob = obp.next()
                    self.tt(ob[:], pt2[:, :], w["zs"][:], ALU.mult, [pt2, w["zs"]], [ob])
                    P.dma(self.scr["brT"][3 * BR + h * 128:3 * BR + (h + 1) * 128, tok:tok + 128], ob[:], reads=[ob], writes=[P.region("brT", 3, n, h)])

                _rr([unit(h) for h in range(8)])


K.deltanet = deltanet


def xattn(self, l, L, sn):
    P = self.P
    inp = self.inp
    idb = self.c["identb"]
    SEG = min(L, 2048)
    scale = 256.0 ** -0.5
    with Scope(P) as S:
        kT = S.sb("kT", [128, 8, 256], BF16)
        V = S.sb("V", [128, 2, BR], BF16)
        with Scope(P) as S2:
            memT = S2.sb("memT", [128, 16, 256], BF16)
            gt = S2.sb("gt", [128, D], F32)
            tp = self.norm_pools(S2)
            xpool = S2.pool("xt", [128, D], F32, 2)
            hbpool = S2.pool("hb", [128, D], BF16, 2)
            pstp = S2.pspool("pst", 2, BF16)
            psm = S2.pspool("psm", 3, F32)
            wpool = S2.pool("wb", [128, 16, 256], BF16, 3)
            P.dma(gt[:], inp["mem_ln_g"][l].partition_broadcast(128), writes=[gt])
            mem = inp["mem_" + sn]
            self.make_hT(S2, lambda tt: mem[tt * 128:(tt + 1) * 128, :], gt, memT, 2, tp, xpool, hbpool, pstp, lambda tt: [])
            wkv = inp["xa_w_kv"][l]
            for c0 in range(0, 1024, 256):
                wb = wpool.next()
                P.dma(wb[:], wkv[:, c0:c0 + 256].rearrange("(k p) c -> p k c", p=128), writes=[wb], q="pool")
                for ct in range(2):
                    ps = psm.next()
                    for k in range(16):
                        self.mm(ps[:, 0:256], wb[:, k, ct * 128:(ct + 1) * 128], memT[:, k, :], [wb, memT], [ps], start=(k == 0), stop=(k == 15), inc=(k == 15))
                    self.cp(kT[:, c0 // 128 + ct, :], ps[:, 0:256], [ps], [kT])
            for c0 in range(0, 1024, 256):
                wb = wpool.next()
                P.dma(wb[:], wkv[:, 1024 + c0:1024 + c0 + 256].rearrange("(k p) c -> p k c", p=128), writes=[wb], q="pool")
                for mt in range(2):
                    ps = psm.next()
                    for k in range(16):
                        self.mm(ps[:, 0:256], memT[:, k, mt * 128:(mt + 1) * 128], wb[:, k, :], [wb, memT], [ps], start=(k == 0), stop=(k == 15), inc=(k == 15))
                    self.cp(V[:, mt, c0:c0 + 256], ps[:, 0:256], [ps], [V])
        qp = S.pool("q", [128, 8, 128], BF16, 2)
        zp = S.pool("z", [128, 8, 128], F32, 2)
        zsp = S.pool("zs", [128, 8, 128], F32, 2)
        obp = S.pool("ob", [128, 8, 128], BF16, 2)
        ep = S.pool("e", [128, 256], F32, 2)
        pbp = S.pool("pb", [128, 256], BF16, 2)
        ptp = S.pool("pt", [128, 2, 128], BF16, 2)
        mxp = S.pool("mx", [128, 1], F32, 3); nbp = S.pool("nb", [128, 1], F32, 3); rsp = S.pool("rs", [128, 1], F32, 3); rip = S.pool("ri", [128, 1], F32, 3)
        pss = S.pspool("pss", 2, F32)
        pst = S.pspool("pst", 2, BF16)
        pso = S.pspool("pso", 3, F32)
        for n in range(L // 128):
            tok = n * 128
            seg = tok // SEG
            q = qp.next(); z = zp.next(); zs = zsp.next(); ob = obp.next()
            P.dma(q[:], self.scr["xq"][:, tok:tok + 128].rearrange("(k p) t -> p k t", p=128), reads=[P.region("xq", f, seg) for f in range(8)], writes=[q], q="pool")
            P.dma(z[:], self.scr["xz"][:, tok:tok + 128].rearrange("(k p) t -> p k t", p=128), reads=[P.region("xz", f, seg) for f in range(8)], writes=[z])
            self.act(zs[:], z[:], AF.Silu, [z], [zs])
            for hh in range(4):
                ps = pss.next()
                for c2 in range(2):
                    self.mm(ps[:, 0:256], q[:, hh * 2 + c2, :], kT[:, hh * 2 + c2, :], [q, kT], [ps], start=(c2 == 0), stop=(c2 == 1), inc=(c2 == 1))
                mx = mxp.next(); nb = nbp.next(); rs = rsp.next(); ri = rip.next(); e = ep.next(); pb = pbp.next(); pt_sb = ptp.next()
                P.op("dve", lambda e_, mx=mx, ps=ps: e_.reduce_max(out=mx[:], in_=ps[:, 0:256], axis=AX.X), reads=[ps], writes=[mx])
                self.ts(nb[:], mx[:], -scale, None, ALU.mult, None, [mx], [nb])
                P.op("act", lambda e_, e=e, ps=ps, nb=nb, rs=rs: e_.activation(out=e[:], in_=ps[:, 0:256], func=AF.Exp, scale=scale, bias=nb[:], accum_out=rs[:]),
                     reads=[ps, nb], writes=[e, rs])
                P.op("dve", lambda e_, rs=rs, ri=ri: e_.reciprocal(out=ri[:], in_=rs[:]), reads=[rs], writes=[ri])
                self.ts(pb[:], e[:], ri[:], None, ALU.mult, None, [e, ri], [pb])
                ptt = pst.next()
                for mt in range(2):
                    self.tr(ptt[:, mt * 128:(mt + 1) * 128], pb[:, mt * 128:(mt + 1) * 128], idb[:], [pb, idb], [ptt], inc=(mt == 1))
                self.cp(pt_sb[:], ptt[:, 0:256].rearrange("p (a b) -> p a b", a=2), [ptt], [pt_sb])
                for c2 in range(2):
                    po = pso.next()
                    for mt in range(2):
                        self.mm(po[:, 0:128], V[:, mt, hh * 256 + c2 * 128:hh * 256 + (c2 + 1) * 128], pt_sb[:, mt, :], [V, pt_sb], [po], start=(mt == 0), stop=(mt == 1), inc=(mt == 1))
                    self.tt(ob[:, hh * 2 + c2, :], po[:, 0:128], zs[:, hh * 2 + c2, :], ALU.mult, [po, zs], [ob])
            P.dma(self.scr["brT"][4 * BR:5 * BR, tok:tok + 128].rearrange("(k p) t -> p k t", p=128), ob[:], reads=[ob], writes=[P.region("brT", 4, n, f) for f in range(8)])


K.xattn = xattn


def attention(self, l, L):
    P = self.P
    inp = self.inp
    idf = self.c["identf"]
    onesf = self.c["onesf"]
    onesb = self.c["onesb"]
    SEG = min(L, 2048)
    TQ = min(512, L)
    NKC = L // 128
    with Scope(P) as S:
        cosT = S.sb("cosT", [128, L], F32)
        sinT = S.sb("sinT", [128, L], F32)
        RM = S.sb("RM", [128, 128], F32)
        qg = S.sb("qg", [128, 1], F32); kg = S.sb("kg", [128, 1], F32)
        P.dma(qg[:], inp["attn_qn_g"][l].rearrange("(p o) -> p o", o=1), writes=[qg])
        P.dma(kg[:], inp["attn_kn_g"][l].rearrange("(p o) -> p o", o=1), writes=[kg])
        for b in range(4):
            if b % 2 == 0:
                self.ts(RM[:, b * 32:(b + 1) * 32], idf[:, (b + 1) * 32:(b + 2) * 32], -1.0, None, ALU.mult, None, [idf, RM], [RM])
            else:
                self.cp(RM[:, b * 32:(b + 1) * 32], idf[:, (b - 1) * 32:b * 32], [idf, RM], [RM], eng="dve")
        with Scope(P) as S2:
            pi_ = S2.sb("pi", [128, 1], F32); f_ = S2.sb("f", [128, 1], F32); t_ = S2.sb("t", [128, 1], F32); fr = S2.sb("fr", [128, 1], F32)
            P.op("pool", lambda e: e.iota(pi_[:], pattern=[[0, 1]], base=0, channel_multiplier=1, allow_small_or_imprecise_dtypes=True), writes=[pi_])
            self.cp(f_[:], pi_[:], [pi_], [f_], eng="dve")
            for thr in (32.0, 64.0, 96.0):
                self.ts(t_[:], pi_[:], thr, -32.0, ALU.is_ge, ALU.mult, [pi_, t_], [t_])
                self.tt(f_[:], f_[:], t_[:], ALU.add, [f_, t_], [f_])
            self.act(fr[:], f_[:], AF.Exp, [f_], [fr], scale=-math.log(10000.0) / 32.0)
            CB = min(1024, L)
            pos = S2.sb("pos", [128, CB], F32)
            xx = S2.sb("xx", [128, 2, CB], F32); u = S2.sb("u", [128, 2, CB], F32); ki = S2.sb("ki", [128, 2, CB], I32); kf = S2.sb("kf", [128, 2, CB], F32)
            sc = S2.sb("sc", [128, 2, CB], F32)
            for cb in range(L // CB):
                r0 = (cb * CB) // 64
                P.op("pool", lambda e, r0=r0: e.iota(pos[0:64, :], pattern=[[1, CB // 64], [0, 64]], base=r0, channel_multiplier=0, allow_small_or_imprecise_dtypes=True), reads=[pos], writes=[pos])
                P.op("pool", lambda e: e.iota(pos[64:128, :], pattern=[[0, CB // 64], [1, 64]], base=0, channel_multiplier=0, allow_small_or_imprecise_dtypes=True), reads=[pos], writes=[pos])
                self.ts(xx[:, 0, :], pos[:], fr[:], None, ALU.mult, None, [pos, fr, xx], [xx])
                self.ts(xx[:, 1, :], xx[:, 0, :], math.pi / 2, None, ALU.add, None, [xx], [xx])
                _reduce_sin(self, xx, u, ki, kf, sc)
                self.cp(sinT[:, cb * CB:(cb + 1) * CB], sc[:, 0, :], [sc], [sinT], eng="act")
                self.cp(cosT[:, cb * CB:(cb + 1) * CB], sc[:, 1, :], [sc], [cosT], eng="dve")
        kT = S.sb("kTa", [128, 2, L], BF16)
        V = S.sb("Va", [128, NKC, 256], BF16)
        P.dma(V[:], self.scr["av"][0:L, :].rearrange("(n p) c -> p n c", p=128), reads=[P.region("av", "t", n, 0) for n in range(NKC)], writes=[V], q="pool")
        xq = S.pool("xq", [128, TQ], F32, 2)
        sqp = S.pool("sq", [128, TQ], F32, 2)
        rqp = S.pool("rq", [128, TQ], F32, 2)
        qnp = S.pool("qn", [128, TQ], F32, 2)
        t1p = S.pool("t1", [128, TQ], F32, 2)
        t2p = S.pool("t2", [128, TQ], F32, 2)
        qrp = S.pool("qr", [128, TQ], BF16, 3)
        ptp = S.pool("pt", [128, TQ], BF16, 4)
        zp = S.pool("z", [128, TQ], F32, 2); zsp = S.pool("zs", [128, TQ], F32, 2)
        rdp = S.pool("rd", [128, TQ], F32, 2); o1p = S.pool("o1", [128, TQ], F32, 2)
        obp = S.pool("ob", [128, TQ], BF16, 2)
        psx = S.pspool("psx", 1, F32)
        pst = S.pspool("pst", 3, F32)
        pso = S.pspool("pso", 2, F32)
        psd = S.pspool("psd", 2, F32)

        def norm_rope(src_name, row0, t0, gcol, region, out_ap, out_bufs, scl):
            x = xq.next()
            P.dma(x[:], self.scr[src_name][row0:row0 + 128, t0:t0 + TQ], reads=[region], writes=[x])
            sq = sqp.next(); rq = rqp.next(); qn = qnp.next(); t1 = t1p.next(); t2 = t2p.next()
            self.tt(sq[:], x[:], x[:], ALU.mult, [x], [sq])
            ps = psx.next()
            self.mm(ps[:, 0:TQ], onesf[:], sq[:], [onesf, sq], [ps])
            self.act(rq[:], ps[:, 0:TQ], AF.Sqrt, [ps], [rq], scale=1.0 / 128, bias=EPS)
            P.op("dve", lambda e: e.reciprocal(out=rq[:], in_=rq[:]), reads=[rq], writes=[rq])
            self.stt(qn[:], x[:], gcol[:], rq[:], ALU.mult, ALU.mult, [x, gcol, rq], [qn])
            ps2 = psx.next()
            self.mm(ps2[:, 0:TQ], RM[:], qn[:], [RM, qn], [ps2])
            self.tt(t1[:], qn[:], cosT[:, t0:t0 + TQ], ALU.mult, [qn, cosT], [t1])
            self.tt(t2[:], ps2[:, 0:TQ], sinT[:, t0:t0 + TQ], ALU.mult, [ps2, sinT], [t2])
            self.stt(out_ap, t1[:], scl, t2[:], ALU.mult, ALU.add, [t1, t2], out_bufs)

        def norm_rope2(src_name, row0, t0, gcol, region, out_ap, out_bufs, scl):
            x = xq.next()
            P.dma(x[:], self.scr[src_name][row0:row0 + 128, t0:t0 + TQ], reads=[region], writes=[x])
            sq = sqp.next(); rq = rqp.next(); qn = qnp.next(); t1 = t1p.next(); t2 = t2p.next()
            self.tt(sq[:], x[:], x[:], ALU.mult, [x], [sq])
            ps = psx.next()
            self.mm(ps[:, 0:TQ], onesf[:], sq[:], [onesf, sq], [ps])
            self.act(rq[:], ps[:, 0:TQ], AF.Sqrt, [ps], [rq], scale=1.0 / 128, bias=EPS)
            P.op("dve", lambda e: e.reciprocal(out=rq[:], in_=rq[:]), reads=[rq], writes=[rq])
            self.stt(qn[:], x[:], gcol[:], rq[:], ALU.mult, ALU.mult, [x, gcol, rq], [qn])
            ps2 = psx.next()
            self.mm(ps2[:, 0:TQ], RM[:], qn[:], [RM, qn], [ps2])
            self.stt(t1[:], qn[:], scl, cosT[:, t0:t0 + TQ], ALU.mult, ALU.mult, [qn, cosT], [t1])
            self.stt(t2[:], ps2[:, 0:TQ], scl, sinT[:, t0:t0 + TQ], ALU.mult, ALU.mult, [ps2, sinT], [t2])
            self.tt(out_ap, t1[:], t2[:], ALU.add, [t1, t2], out_bufs)

        for kvh in range(2):
            for tq in range(L // TQ):
                t0 = tq * TQ
                norm_rope2("ak", kvh * 128, t0, kg, P.region("ak", kvh, t0 // SEG), kT[:, kvh, t0:t0 + TQ], [kT], 1.0)
        work = [(tq, hq) for tq in range(L // TQ) for hq in range(8)]

        def prep_q(tq_, hq_):
            qr_ = qrp.next()
            norm_rope2("aq", hq_ * 128, tq_ * TQ, qg, P.region("aq", hq_, (tq_ * TQ) // SEG), qr_[:], [qr_], 128.0 ** -0.5)
            return qr_
        qr_next = prep_q(*work[0])
        for wi, (tq, hq) in enumerate(work):
            if True:
                t0 = tq * TQ
                seg = t0 // SEG
                kvh = hq // 4
                qr = qr_next
                if wi + 1 < len(work):
                    qr_next = prep_q(*work[wi + 1])
                po = pso.next(); pd = psd.next()
                PRE = 2
                sts = {}

                def issue_st(kc_):
                    pst_ = pst.next()
                    self.mm(pst_[:, 0:TQ], kT[:, kvh, kc_ * 128:(kc_ + 1) * 128], qr[:], [kT, qr], [pst_])
                    sts[kc_] = pst_
                for kc_ in range(min(PRE, NKC)):
                    issue_st(kc_)
                for kc in range(NKC):
                    if kc + PRE < NKC:
                        issue_st(kc + PRE)
                    pst_ = sts.pop(kc)
                    pt = ptp.next()
                    self.act(pt[:], pst_[:, 0:TQ], AF.Exp, [pst_], [pt])
                    self.mm(po[:, 0:TQ], V[:, kc, kvh * 128:(kvh + 1) * 128], pt[:], [V, pt], [po], start=(kc == 0), stop=(kc == NKC - 1), inc=False)
                    self.mm(pd[:, 0:TQ], onesb[:], pt[:], [onesb, pt], [pd], start=(kc == 0), stop=(kc == NKC - 1), inc=True)
                z = zp.next(); zs = zsp.next(); rd = rdp.next(); o1 = o1p.next(); ob = obp.next()
                P.dma(z[:], self.scr["az"][hq * 128:(hq + 1) * 128, t0:t0 + TQ], reads=[P.region("az", hq, seg)], writes=[z])
                self.act(zs[:], z[:], AF.Silu, [z], [zs])
                P.op("dve", lambda e, rd=rd, pd=pd: e.reciprocal(out=rd[:], in_=pd[:, 0:TQ]), reads=[pd], writes=[rd])
                self.tt(o1[:], po[:, 0:TQ], rd[:], ALU.mult, [po, rd], [o1])
                self.tt(ob[:], o1[:], zs[:], ALU.mult, [o1, zs], [ob])
                P.dma(self.scr["brT"][hq * 128:(hq + 1) * 128, t0:t0 + TQ], ob[:], reads=[ob], writes=[P.region("brT", 0, t0 // 128 + i, hq) for i in range(TQ // 128)])


K.attention = attention
```
